# Optimizing a Trainium2 kernel written in Bass

```python
import math
import jax, jax.numpy as jnp
from jax import lax
import numpy as np

D_MODEL = 1024
BATCH = 8
SEQ = 2048
DEPTH = 2
DEC_BATCH = 128
DEC_SEQ = 1
PAST_LEN = 16384
PAGE_SIZE = 128

N_AB = (DEPTH + 1) // 2
N_CD = DEPTH // 2
HALF = D_MODEL // 2
D_FF = 4 * D_MODEL
CHUNK = 64
CONV_W = 4
EPS = 1e-6
DT_MIN = 1e-3
DT_MAX = 1e-1
GLA_HEADS = 4
GLA_DV = HALF // GLA_HEADS
GLA_DK = GLA_DV // 2
GLA_GATE_RANK = 16
GLA_GATE_NORM = 16.0
S5_H = 16
S5_GROUPS = HALF // S5_H
S5_P = 64
SSD_HEADDIM = 64
SSD_HEADS = HALF // SSD_HEADDIM
SSD_GROUPS = 2
SSD_DSTATE = 128
SSD_CONV_DIM = HALF + 2 * SSD_GROUPS * SSD_DSTATE
GDN_HEADS = 4
GDN_DK = HALF // GDN_HEADS
GDN_DV = HALF // GDN_HEADS
GDN_CONV_DIM = 2 * GDN_HEADS * GDN_DK + GDN_HEADS * GDN_DV

IN_AB = 2 * GLA_HEADS * GLA_DK + HALF + GLA_GATE_RANK + HALF + HALF
IN_CD = HALF + SSD_CONV_DIM + SSD_HEADS + GDN_CONV_DIM + GDN_HEADS * GDN_DV + 2 * GDN_HEADS
STATE_KEYS = ('gla', 's5_re', 's5_im', 'ssd', 'ssd_conv', 'gdn', 'gdn_conv')

kernel_name = 'hybrid_gla_s5_ssd_gdn_step'


def _rmsnorm(x, g):
    x32 = x.astype(jnp.float32)
    y = x32 * lax.rsqrt(jnp.mean(x32 * x32, axis=-1, keepdims=True) + EPS)
    return (y * g.astype(jnp.float32)).astype(x.dtype)


def _l2norm(x):
    return x * lax.rsqrt(jnp.sum(x * x, axis=-1, keepdims=True) + EPS)


def _split(a, sizes):
    out, o = [], 0
    for s in sizes:
        out.append(a[..., o:o + s])
        o += s
    return out


def _to_chunks(a, c):
    L = a.shape[1]
    n = -(-L // c)
    a = jnp.pad(a, [(0, 0), (0, n * c - L)] + [(0, 0)] * (a.ndim - 2))
    a = a.reshape((a.shape[0], n, c) + a.shape[2:])
    return jnp.moveaxis(a, 1, 0)


def _from_chunks(a, L):
    a = jnp.moveaxis(a, 0, 1)
    a = a.reshape((a.shape[0], a.shape[1] * a.shape[2]) + a.shape[3:])
    return a[:, :L]


def _seg_decay(cum, strict):
    c = cum.shape[1]
    idx = jnp.arange(c)
    mask = (idx[:, None] > idx[None, :]) if strict else (idx[:, None] >= idx[None, :])
    mask = mask.reshape((1, c, c) + (1,) * (cum.ndim - 2))
    diff = cum[:, :, None] - cum[:, None, :]
    return jnp.exp(jnp.where(mask, diff, -jnp.inf))


def _causal_conv(x, buf, w):
    L = x.shape[1]
    xe = jnp.concatenate([buf.astype(x.dtype), x], axis=1)
    out = xe[:, 0:L] * w[0]
    for t in range(1, CONV_W):
        out = out + xe[:, t:t + L] * w[t]
    return out, xe[:, L:]


def _gla_scan(q, k, v, log_a, s0):
    f32 = jnp.float32
    L = q.shape[1]
    c = min(CHUNK, L)
    xs = tuple(_to_chunks(t.astype(f32), c) for t in (q, k, v, log_a))

    def step(S, inp):
        qc, kc, vc, lac = inp
        cum = jnp.cumsum(lac, axis=1)
        dec = _seg_decay(cum, False)
        att = jnp.einsum('bihk,bjhk,bijhk->bhij', qc, kc, dec)
        o = jnp.einsum('bhij,bjhv->bihv', att, vc) + jnp.einsum('bihk,bhkv->bihv', qc * jnp.exp(cum), S)
        last = cum[:, -1]
        kd = kc * jnp.exp(last[:, None] - cum)
        S = jnp.exp(last)[..., None] * S + jnp.einsum('bjhk,bjhv->bhkv', kd, vc)
        return S, o

    S, o = lax.scan(step, s0.astype(f32), xs)
    return _from_chunks(o, L), S


def _s5_scan(u, h0_re, h0_im, lam_re, lam_im, b_re, b_im, c_re, c_im, d, log_dt):
    f32 = jnp.float32
    u = u.astype(f32)
    lr, li = lam_re.astype(f32), lam_im.astype(f32)
    b_re, b_im, c_re, c_im = (t.astype(f32) for t in (b_re, b_im, c_re, c_im))
    dt = jnp.exp(log_dt.astype(f32))[:, None]
    mag = jnp.exp(lr * dt)
    ang = li * dt
    a_re, a_im = mag * jnp.cos(ang), mag * jnp.sin(ang)
    den = lr * lr + li * li
    n_re, n_im = a_re - 1.0, a_im
    k_re = (n_re * lr + n_im * li) / den
    k_im = (n_im * lr - n_re * li) / den
    bb_re = k_re[..., None] * b_re - k_im[..., None] * b_im
    bb_im = k_re[..., None] * b_im + k_im[..., None] * b_re
    bu_re = jnp.einsum('blgh,gph->blgp', u, bb_re)
    bu_im = jnp.einsum('blgh,gph->blgp', u, bb_im)
    h0_re, h0_im = h0_re.astype(f32), h0_im.astype(f32)
    bu_re = bu_re.at[:, 0].add(a_re * h0_re - a_im * h0_im)
    bu_im = bu_im.at[:, 0].add(a_re * h0_im + a_im * h0_re)
    L = u.shape[1]
    ar = jnp.broadcast_to(a_re, (1, L) + a_re.shape)
    ai = jnp.broadcast_to(a_im, (1, L) + a_im.shape)

    def combine(e1, e2):
        a1r, a1i, b1r, b1i = e1
        a2r, a2i, b2r, b2i = e2
        return (a2r * a1r - a2i * a1i, a2r * a1i + a2i * a1r,
                a2r * b1r - a2i * b1i + b2r, a2r * b1i + a2i * b1r + b2i)

    _, _, h_re, h_im = lax.associative_scan(combine, (ar, ai, bu_re, bu_im), axis=1)
    y = (jnp.einsum('blgp,ghp->blgh', h_re, c_re) - jnp.einsum('blgp,ghp->blgh', h_im, c_im)
         + d.astype(f32) * u)
    return y, h_re[:, -1], h_im[:, -1]


def _ssd_scan(xdt, la, bm, cm, s0):
    f32 = jnp.float32
    L = xdt.shape[1]
    c = min(CHUNK, L)
    xs = tuple(_to_chunks(t.astype(f32), c) for t in (xdt, la, bm, cm))

    def step(S, inp):
        xc, lac, bc, cc = inp
        cum = jnp.cumsum(lac, axis=1)
        dec = _seg_decay(cum, False)
        cb = jnp.einsum('bign,bjgn->bijg', cc, bc)
        y = jnp.einsum('bijg,bijgr,bjgrp->bigrp', cb, dec, xc)
        y = y + jnp.einsum('bign,bgrpn->bigrp', cc, S) * jnp.exp(cum)[..., None]
        last = cum[:, -1]
        w = jnp.exp(last[:, None] - cum)
        S = jnp.exp(last)[..., None, None] * S + jnp.einsum('bjgr,bjgrp,bjgn->bgrpn', w, xc, bc)
        return S, y

    S, y = lax.scan(step, s0.astype(f32), xs)
    return _from_chunks(y, L), S


def _gdn_scan(q, k, v, beta, g, s0):
    f32 = jnp.float32
    L = q.shape[1]
    c = min(CHUNK, L)
    xs = tuple(_to_chunks(t.astype(f32), c) for t in (q, k, v, beta, g))

    def step(S, inp):
        qc, kc, vc, bc, gc = inp
        cum = jnp.cumsum(gc, axis=1)
        dec_incl = _seg_decay(cum, False)
        dec_strict = _seg_decay(cum, True)
        gam = jnp.exp(cum)
        m = jnp.einsum('bihk,bjhk,bijh->bhij', kc, kc, dec_strict) * jnp.swapaxes(bc, 1, 2)[..., None]
        rhs = (vc - gam[..., None] * jnp.einsum('bihk,bhkv->bihv', kc, S)) * bc[..., None]
        u = lax.linalg.triangular_solve(m, jnp.swapaxes(rhs, 1, 2), left_side=True, lower=True,
                                        unit_diagonal=True)
        att = jnp.einsum('bihk,bjhk,bijh->bhij', qc, kc, dec_incl)
        o = jnp.einsum('bhij,bhjv->bihv', att, u) + gam[..., None] * jnp.einsum('bihk,bhkv->bihv', qc, S)
        last = cum[:, -1]
        w = jnp.exp(last[:, None] - cum)
        S = jnp.exp(last)[..., None, None] * S + jnp.einsum('bjh,bjhk,bhjv->bhkv', w, kc, u)
        return S, o

    S, o = lax.scan(step, s0.astype(f32), xs)
    return _from_chunks(o, L), S


def _mixer_ab(h, s_gla, s_re, s_im, p, j):
    f32 = jnp.float32
    bsz, L, _ = h.shape
    q, k, v, glr, r, u = _split(h @ p['w_in_ab'][j], (GLA_HEADS * GLA_DK, GLA_HEADS * GLA_DK, HALF,
                                                       GLA_GATE_RANK, HALF, HALF))
    q = q.reshape(bsz, L, GLA_HEADS, GLA_DK) * (GLA_DK ** -0.5)
    k = k.reshape(bsz, L, GLA_HEADS, GLA_DK)
    v = v.reshape(bsz, L, GLA_HEADS, GLA_DV)
    log_a = jax.nn.log_sigmoid((glr @ p['w_gla_gate'][j] + p['b_gla_gate'][j]).astype(f32)) / GLA_GATE_NORM
    log_a = log_a.reshape(bsz, L, GLA_HEADS, GLA_DK)
    o, s_gla_new = _gla_scan(q, k, v, log_a, s_gla)
    o_gla = (_rmsnorm(o, p['g_gla_norm'][j]).reshape(bsz, L, HALF) * jax.nn.silu(r.astype(f32))).astype(h.dtype)
    y, s_re_new, s_im_new = _s5_scan(u.reshape(bsz, L, S5_GROUPS, S5_H), s_re, s_im,
                                     p['s5_lam_re'][j], p['s5_lam_im'][j], p['s5_b_re'][j], p['s5_b_im'][j],
                                     p['s5_c_re'][j], p['s5_c_im'][j], p['s5_d'][j], p['s5_log_dt'][j])
    y = jax.nn.gelu(y.reshape(bsz, L, HALF))
    o_s5 = (y * jax.nn.sigmoid(y @ p['w_s5_glu'][j].astype(f32) + p['b_s5_glu'][j].astype(f32))).astype(h.dtype)
    out = jnp.concatenate([o_gla, o_s5], axis=-1) @ p['w_out_ab'][j]
    return out, s_gla_new, s_re_new, s_im_new


def _mixer_cd(h, s_ssd, c_ssd, s_gdn, c_gdn, p, j):
    f32 = jnp.float32
    bsz, L, _ = h.shape
    R = SSD_HEADS // SSD_GROUPS
    z, xbc, dt_raw, qkv, gate, b_raw, a_raw = _split(
        h @ p['w_in_cd'][j], (HALF, SSD_CONV_DIM, SSD_HEADS, GDN_CONV_DIM, GDN_HEADS * GDN_DV, GDN_HEADS, GDN_HEADS))
    xbc, c_ssd_new = _causal_conv(xbc, c_ssd, p['ssd_conv_w'][j])
    xbc = jax.nn.silu((xbc + p['ssd_conv_b'][j]).astype(f32))
    xs, bm, cm = _split(xbc, (HALF, SSD_GROUPS * SSD_DSTATE, SSD_GROUPS * SSD_DSTATE))
    xs = xs.reshape(bsz, L, SSD_GROUPS, R, SSD_HEADDIM)
    bm = bm.reshape(bsz, L, SSD_GROUPS, SSD_DSTATE)
    cm = cm.reshape(bsz, L, SSD_GROUPS, SSD_DSTATE)
    dt = jax.nn.softplus(dt_raw.astype(f32) + p['ssd_dt_bias'][j].astype(f32)).reshape(bsz, L, SSD_GROUPS, R)
    a = -jnp.exp(p['ssd_a_log'][j].astype(f32)).reshape(SSD_GROUPS, R)
    s0 = s_ssd.reshape(bsz, SSD_GROUPS, R, SSD_HEADDIM, SSD_DSTATE)
    y, s_ssd_new = _ssd_scan(xs * dt[..., None], dt * a, bm, cm, s0)
    y = y + p['ssd_d'][j].astype(f32).reshape(SSD_GROUPS, R)[..., None] * xs
    zg = jax.nn.silu(z.astype(f32)).reshape(bsz, L, SSD_GROUPS, R * SSD_HEADDIM)
    o_ssd = _rmsnorm(y.reshape(bsz, L, SSD_GROUPS, R * SSD_HEADDIM) * zg,
                     p['ssd_norm'][j].reshape(SSD_GROUPS, R * SSD_HEADDIM))
    o_ssd = o_ssd.reshape(bsz, L, HALF).astype(h.dtype)
    s_ssd_new = s_ssd_new.reshape(bsz, SSD_HEADS, SSD_HEADDIM, SSD_DSTATE)
    qkv, c_gdn_new = _causal_conv(qkv, c_gdn, p['gdn_conv_w'][j])
    qkv = jax.nn.silu(qkv.astype(f32))
    q, k, v = _split(qkv, (GDN_HEADS * GDN_DK, GDN_HEADS * GDN_DK, GDN_HEADS * GDN_DV))
    q = _l2norm(q.reshape(bsz, L, GDN_HEADS, GDN_DK)) * (GDN_DK ** -0.5)
    k = _l2norm(k.reshape(bsz, L, GDN_HEADS, GDN_DK))
    v = v.reshape(bsz, L, GDN_HEADS, GDN_DV)
    beta = jax.nn.sigmoid(b_raw.astype(f32))
    g = -jnp.exp(p['gdn_a_log'][j].astype(f32)) * jax.nn.softplus(a_raw.astype(f32) + p['gdn_dt_bias'][j].astype(f32))
    o, s_gdn_new = _gdn_scan(q, k, v, beta, g, s_gdn)
    o_gdn = _rmsnorm(o, p['gdn_norm'][j]) * jax.nn.silu(gate.astype(f32)).reshape(bsz, L, GDN_HEADS, GDN_DV)
    o_gdn = o_gdn.reshape(bsz, L, HALF).astype(h.dtype)
    out = jnp.concatenate([o_ssd, o_gdn], axis=-1) @ p['w_out_cd'][j]
    return out, s_ssd_new, c_ssd_new, s_gdn_new, c_gdn_new


def _mlp(h, w_up, w_down):
    a = jax.nn.relu(h @ w_up)
    return (a * a) @ w_down


def _trunk(x, st, p):
    new = {name: [] for name in STATE_KEYS}
    for i in range(DEPTH):
        j = i // 2
        h = _rmsnorm(x, p['norm_mix'][i])
        if i % 2 == 0:
            mix, s0, s1, s2 = _mixer_ab(h, st['gla'][j], st['s5_re'][j], st['s5_im'][j], p, j)
            for name, val in zip(('gla', 's5_re', 's5_im'), (s0, s1, s2)):
                new[name].append(val)
        else:
            mix, s0, s1, s2, s3 = _mixer_cd(h, st['ssd'][j], st['ssd_conv'][j], st['gdn'][j], st['gdn_conv'][j], p, j)
            for name, val in zip(('ssd', 'ssd_conv', 'gdn', 'gdn_conv'), (s0, s1, s2, s3)):
                new[name].append(val)
        x = x + mix
        x = x + _mlp(_rmsnorm(x, p['norm_mlp'][i]), p['w_up'][i], p['w_down'][i])
    y = _rmsnorm(x, p['norm_final'])
    return y, tuple(jnp.stack(new[name]) for name in STATE_KEYS)


def setup_inputs(seed: int = 0) -> dict:
    key = jax.random.key(seed)
    ks = iter(jax.random.split(key, 64))
    f32 = jnp.float32

    def nrm(shape, scale):
        return jax.random.normal(next(ks), shape, f32) * scale

    def unif(shape, lo, hi):
        return jax.random.uniform(next(ks), shape, f32, lo, hi)

    def gain(shape):
        return 1.0 + nrm(shape, 0.01)

    def dt_bias(shape):
        dt = jnp.exp(unif(shape, math.log(DT_MIN), math.log(DT_MAX)))
        return dt + jnp.log(-jnp.expm1(-dt))

    n_idx = jnp.arange(S5_P, dtype=f32)
    return {
        'x_prompt': nrm((BATCH, SEQ, D_MODEL), 1.0),
        'x_sample': nrm((DEC_BATCH, DEC_SEQ, D_MODEL), 1.0),
        'state_gla': nrm((N_AB, DEC_BATCH, GLA_HEADS, GLA_DK, GLA_DV), 0.5),
        'state_s5_re': nrm((N_AB, DEC_BATCH, S5_GROUPS, S5_P), 0.1),
        'state_s5_im': nrm((N_AB, DEC_BATCH, S5_GROUPS, S5_P), 0.1),
        'state_ssd': nrm((N_CD, DEC_BATCH, SSD_HEADS, SSD_HEADDIM, SSD_DSTATE), 0.1),
        'state_ssd_conv': nrm((N_CD, DEC_BATCH, CONV_W - 1, SSD_CONV_DIM), 1.0),
        'state_gdn': nrm((N_CD, DEC_BATCH, GDN_HEADS, GDN_DK, GDN_DV), 0.1),
        'state_gdn_conv': nrm((N_CD, DEC_BATCH, CONV_W - 1, GDN_CONV_DIM), 1.0),
        'norm_mix': gain((DEPTH, D_MODEL)),
        'norm_mlp': gain((DEPTH, D_MODEL)),
        'norm_final': gain((D_MODEL,)),
        'w_up': nrm((DEPTH, D_MODEL, D_FF), D_MODEL ** -0.5),
        'w_down': nrm((DEPTH, D_FF, D_MODEL), D_FF ** -0.5),
        'w_in_ab': nrm((N_AB, D_MODEL, IN_AB), D_MODEL ** -0.5),
        'w_out_ab': nrm((N_AB, 2 * HALF, D_MODEL), (2 * HALF) ** -0.5),
        'w_gla_gate': nrm((N_AB, GLA_GATE_RANK, GLA_HEADS * GLA_DK), GLA_GATE_RANK ** -0.5),
        'b_gla_gate': nrm((N_AB, GLA_HEADS * GLA_DK), 0.02),
        'g_gla_norm': gain((N_AB, GLA_DV)),
        's5_lam_re': -0.5 + nrm((N_AB, S5_GROUPS, S5_P), 0.01),
        's5_lam_im': math.pi * n_idx + nrm((N_AB, S5_GROUPS, S5_P), 0.01),
        's5_b_re': nrm((N_AB, S5_GROUPS, S5_P, S5_H), (2 * S5_H) ** -0.5),
        's5_b_im': nrm((N_AB, S5_GROUPS, S5_P, S5_H), (2 * S5_H) ** -0.5),
        's5_c_re': nrm((N_AB, S5_GROUPS, S5_H, S5_P), (2 * S5_P) ** -0.5),
        's5_c_im': nrm((N_AB, S5_GROUPS, S5_H, S5_P), (2 * S5_P) ** -0.5),
        's5_d': nrm((N_AB, S5_GROUPS, S5_H), 1.0),
        's5_log_dt': unif((N_AB, S5_GROUPS), math.log(DT_MIN), math.log(DT_MAX)),
        'w_s5_glu': nrm((N_AB, HALF, HALF), HALF ** -0.5),
        'b_s5_glu': nrm((N_AB, HALF), 0.02),
        'w_in_cd': nrm((N_CD, D_MODEL, IN_CD), D_MODEL ** -0.5),
        'w_out_cd': nrm((N_CD, 2 * HALF, D_MODEL), (2 * HALF) ** -0.5),
        'ssd_conv_w': nrm((N_CD, CONV_W, SSD_CONV_DIM), CONV_W ** -0.5),
        'ssd_conv_b': nrm((N_CD, SSD_CONV_DIM), 0.02),
        'ssd_dt_bias': dt_bias((N_CD, SSD_HEADS)),
        'ssd_a_log': jnp.log(unif((N_CD, SSD_HEADS), 1.0, 16.0)),
        'ssd_d': 1.0 + nrm((N_CD, SSD_HEADS), 0.1),
        'ssd_norm': gain((N_CD, HALF)),
        'gdn_conv_w': nrm((N_CD, CONV_W, GDN_CONV_DIM), CONV_W ** -0.5),
        'gdn_a_log': jnp.log(unif((N_CD, GDN_HEADS), 1.0, 16.0)),
        'gdn_dt_bias': dt_bias((N_CD, GDN_HEADS)),
        'gdn_norm': gain((N_CD, GDN_DV)),
    }


def reference(x_prompt, x_sample, state_gla, state_s5_re, state_s5_im, state_ssd, state_ssd_conv,
              state_gdn, state_gdn_conv, norm_mix, norm_mlp, norm_final, w_up, w_down,
              w_in_ab, w_out_ab, w_gla_gate, b_gla_gate, g_gla_norm,
              s5_lam_re, s5_lam_im, s5_b_re, s5_b_im, s5_c_re, s5_c_im, s5_d, s5_log_dt,
              w_s5_glu, b_s5_glu, w_in_cd, w_out_cd, ssd_conv_w, ssd_conv_b, ssd_dt_bias,
              ssd_a_log, ssd_d, ssd_norm, gdn_conv_w, gdn_a_log, gdn_dt_bias, gdn_norm):
    f32 = jnp.float32
    p = dict(norm_mix=norm_mix, norm_mlp=norm_mlp, norm_final=norm_final, w_up=w_up, w_down=w_down,
             w_in_ab=w_in_ab, w_out_ab=w_out_ab, w_gla_gate=w_gla_gate, b_gla_gate=b_gla_gate,
             g_gla_norm=g_gla_norm, s5_lam_re=s5_lam_re, s5_lam_im=s5_lam_im, s5_b_re=s5_b_re,
             s5_b_im=s5_b_im, s5_c_re=s5_c_re, s5_c_im=s5_c_im, s5_d=s5_d, s5_log_dt=s5_log_dt,
             w_s5_glu=w_s5_glu, b_s5_glu=b_s5_glu, w_in_cd=w_in_cd, w_out_cd=w_out_cd,
             ssd_conv_w=ssd_conv_w, ssd_conv_b=ssd_conv_b, ssd_dt_bias=ssd_dt_bias, ssd_a_log=ssd_a_log,
             ssd_d=ssd_d, ssd_norm=ssd_norm, gdn_conv_w=gdn_conv_w, gdn_a_log=gdn_a_log,
             gdn_dt_bias=gdn_dt_bias, gdn_norm=gdn_norm)
    bsz = x_prompt.shape[0]
    st_prompt = dict(
        gla=jnp.zeros((N_AB, bsz, GLA_HEADS, GLA_DK, GLA_DV), f32),
        s5_re=jnp.zeros((N_AB, bsz, S5_GROUPS, S5_P), f32),
        s5_im=jnp.zeros((N_AB, bsz, S5_GROUPS, S5_P), f32),
        ssd=jnp.zeros((N_CD, bsz, SSD_HEADS, SSD_HEADDIM, SSD_DSTATE), f32),
        ssd_conv=jnp.zeros((N_CD, bsz, CONV_W - 1, SSD_CONV_DIM), x_prompt.dtype),
        gdn=jnp.zeros((N_CD, bsz, GDN_HEADS, GDN_DK, GDN_DV), f32),
        gdn_conv=jnp.zeros((N_CD, bsz, CONV_W - 1, GDN_CONV_DIM), x_prompt.dtype))
    st_sample = dict(gla=state_gla, s5_re=state_s5_re, s5_im=state_s5_im, ssd=state_ssd,
                     ssd_conv=state_ssd_conv, gdn=state_gdn, gdn_conv=state_gdn_conv)
    y_prompt, (p_gla, p_s5_re, p_s5_im, p_ssd, p_ssd_conv, p_gdn, p_gdn_conv) = _trunk(x_prompt, st_prompt, p)
    y_sample, (s_gla, s_s5_re, s_s5_im, s_ssd, s_ssd_conv, s_gdn, s_gdn_conv) = _trunk(x_sample, st_sample, p)
    return (y_prompt, y_sample, p_gla, p_s5_re, p_s5_im, p_ssd, p_ssd_conv, p_gdn, p_gdn_conv,
            s_gla, s_s5_re, s_s5_im, s_ssd, s_ssd_conv, s_gdn, s_gdn_conv)
```

```python
import math
from contextlib import ExitStack

import numpy as np
import concourse.bass as bass
import concourse.mybir as mybir
from concourse.bass_utils import run_bass_kernel_spmd

F32 = mybir.dt.float32
BF16 = mybir.dt.bfloat16
I32 = mybir.dt.int32
ALU = mybir.AluOpType
AF = mybir.ActivationFunctionType

EPS = 1e-6
T = 2048
NS = 16
NT = T + NS
D = 1024
IN_AB = 2064
IN_CD = 3600
NEG = -30000.0


class Unit:
    __slots__ = ('w', 'rs')

    def __init__(self):
        self.w = None
        self.rs = {}


class DSem:
    def __init__(self, h, key):
        self.h = h
        self.key = key
        self.count = 0
        self.batch = None


class Sched:
    def __init__(self, nc, es):
        self.nc = nc
        self.es = es
        self.eng = {'pe': nc.tensor, 'dve': nc.vector, 'act': nc.scalar, 'pool': nc.gpsimd, 'sp': nc.sync}
        self.semh = {}
        for k in self.eng:
            self.semh[k] = es.enter_context(nc.semaphore('s_' + k))
        self.cnt = {k: 0 for k in self.eng}
        self.known = {k: {} for k in self.eng}
        self.dsems = []
        self.nwaits = 0
        self.nops = 0

    def dsem(self, name=None):
        key = 'd%d' % len(self.dsems)
        h = self.es.enter_context(self.nc.semaphore(name or key))
        self.semh[key] = h
        ds = DSem(h, key)
        self.dsems.append(ds)
        return ds

    def _val(self, ev):
        vr = ev[1]
        if vr[0] is None:
            ds = vr[1]
            vr[0] = ds.count
            ds.batch = None
        return vr[0]

    def _need(self, e, ev, needs, skip_same=False):
        if ev is None:
            return
        k = ev[0]
        if skip_same and k == e:
            return
        v = self._val(ev)
        if self.known[e].get(k, 0) >= v:
            return
        if k in needs and needs[k][0] >= v:
            return
        needs[k] = (v, ev[2])

    def _emit_waits(self, e, needs):
        kn = self.known[e]
        for k, (v, snap) in needs.items():
            if kn.get(k, 0) >= v:
                continue
            self.eng[e].wait_ge(self.semh[k], v)
            self.nwaits += 1
            kn[k] = v
            if snap:
                for k2, v2 in snap.items():
                    if kn.get(k2, 0) < v2:
                        kn[k2] = v2

    def _deps(self, e, reads, writes, isdma=False):
        needs = {}
        for u in reads:
            self._need(e, u.w, needs)
        skip = (e == 'pe') and not isdma
        for u in writes:
            self._need(e, u.w, needs, skip_same=skip)
            for ev in u.rs.values():
                self._need(e, ev, needs, skip_same=skip)
        self._emit_waits(e, needs)

    def op(self, e, fn, reads=(), writes=(), inc=True):
        self._deps(e, reads, writes)
        ins = fn()
        self.nops += 1
        if inc:
            self.cnt[e] += 1
            ins.then_inc(self.semh[e], 1)
            val = self.cnt[e]
        else:
            val = self.cnt[e] + 1
        ev = (e, [val], dict(self.known[e]) if inc else None)
        for u in writes:
            u.w = ev
            u.rs = {}
        for u in reads:
            old = u.rs.get(e)
            if old is None or old[1][0] <= val:
                u.rs[e] = ev
        return ins

    def dma(self, q, out, in_, ds, reads=(), writes=()):
        self._deps(q, reads, writes, isdma=True)
        kn = self.known[q]
        if ds.batch is None:
            if ds.count > 0 and kn.get(ds.key, 0) < ds.count:
                self.eng[q].wait_ge(ds.h, ds.count)
                self.nwaits += 1
                kn[ds.key] = ds.count
            ds.batch = [None, ds]
        ins = self.eng[q].dma_start(out=out, in_=in_)
        ins.then_inc(ds.h, 16)
        ds.count += 16
        ev = (ds.key, ds.batch, dict(kn))
        for u in writes:
            u.w = ev
            u.rs = {}
        for u in reads:
            u.rs[ds.key] = ev
        return ins

    def barrier(self):
        for ds in self.dsems:
            if ds.batch is not None:
                ds.batch[0] = ds.count
                ds.batch = None
        for e in self.eng:
            needs = {}
            for k in self.eng:
                if k != e and self.cnt[k] > 0:
                    needs[k] = (self.cnt[k], None)
            for ds in self.dsems:
                if ds.count > 0:
                    needs[ds.key] = (ds.count, None)
            self._emit_waits(e, needs)


class V:
    __slots__ = ('ap', 'units')

    def __init__(self, ap, units):
        self.ap = ap
        self.units = units

    def __getitem__(self, idx):
        return V(self.ap[idx], self.units)

    def re(self, pat, **kw):
        return V(self.ap.rearrange(pat, **kw), self.units)

    def bc(self, shape):
        return V(self.ap.to_broadcast(list(shape)), self.units)

    def un(self, d):
        return V(self.ap.unsqueeze(d), self.units)

    def bitcast(self, dt):
        return V(self.ap.bitcast(dt), self.units)


class Tl:
    def __init__(self, t):
        self.t = t
        self.u = Unit()

    def __getitem__(self, idx):
        return V(self.t[idx], (self.u,))

    def sub(self, unit, idx):
        return V(self.t[idx], (unit,))


class Tl2(Tl):
    def __init__(self, t):
        Tl.__init__(self, t)
        self.us = (Unit(), Unit())

    def __getitem__(self, idx):
        return V(self.t[idx], self.us)

    def half(self, hp, k):
        return V(self.t[:, k * hp:k * (hp + 1)], (self.us[hp],))


class WB:
    def __init__(self, t, blocks):
        self.t = t
        self.blocks = blocks
        self.units = [Unit() for _ in blocks]

    def __getitem__(self, idx):
        c0 = idx[2].start
        for (lo, hi), u in zip(self.blocks, self.units):
            if lo <= c0 < hi:
                return V(self.t[idx], (u,))
        raise KeyError(c0)


def _u(*vs):
    out = []
    for v in vs:
        if isinstance(v, V):
            for u in v.units:
                if u not in out:
                    out.append(u)
    return out


def _ap(v):
    return v.ap if isinstance(v, V) else v


def build_program(debug=False, nlayers=2, do_mlp=True, gdn='full', mlp1=True, gstop=99):
    nc = bass.Bass("TRN2", target_bir_lowering=False)

    def din(name, shape):
        return nc.dram_tensor(name, list(shape), F32, kind="ExternalInput").ap()

    def dout(name, shape):
        return nc.dram_tensor(name, list(shape), F32, kind="ExternalOutput").ap()

    I = {}
    for name, shape in [
        ('xp', (T, D)), ('xs', (NS, D)), ('sgla', (NS, 4, 64, 128)), ('ss5re', (NS, 2048)), ('ss5im', (NS, 2048)),
        ('sssd', (NS, 8, 64, 128)), ('sssdc', (NS, 3, 1024)), ('sgdn', (NS, 4, 128, 128)), ('sgdnc', (NS, 3, 1536)),
        ('norm_mix', (2, D)), ('norm_mlp', (2, D)), ('norm_final', (D,)), ('w_up', (2, D, 4096)), ('w_down', (2, 4096, D)),
        ('w_in_ab', (D, IN_AB)), ('w_out_ab', (D, D)), ('w_gla_gate', (16, 256)), ('b_gla_gate', (256,)), ('g_gla_norm', (128,)),
        ('s5_lam_re', (2048,)), ('s5_lam_im', (2048,)), ('s5_b_re', (2048, 16)), ('s5_b_im', (2048, 16)),
        ('s5_c_re', (512, 64)), ('s5_c_im', (512, 64)), ('s5_d', (512,)), ('s5_log_dt', (32,)),
        ('w_s5_glu', (512, 512)), ('b_s5_glu', (512,)), ('w_in_cd', (D, IN_CD)), ('w_out_cd', (D, D)),
        ('ssd_conv_w', (4, 1024)), ('ssd_conv_b', (1024,)), ('ssd_dt_bias', (8,)), ('ssd_a_log', (8,)), ('ssd_d', (8,)),
        ('ssd_norm', (512,)), ('gdn_conv_w', (4, 1536)), ('gdn_a_log', (4,)), ('gdn_dt_bias', (4,)), ('gdn_norm', (128,)),
    ]:
        I[name] = din(name, shape)
    O = {}
    for name, shape in [
        ('yp', (T, D)), ('ys', (NS, D)), ('p_gla', (4, 64, 128)), ('p_s5re', (16, 128)), ('p_s5im', (16, 128)),
        ('p_ssd', (8, 64, 128)), ('p_ssdc', (3, 1024)), ('p_gdn', (4, 128, 128)), ('p_gdnc', (3, 1536)),
        ('s_gla', (NS, 4, 64, 128)), ('s_s5re', (NS, 2048)), ('s_s5im', (NS, 2048)), ('s_ssd', (NS, 8, 64, 128)),
        ('s_ssdc', (NS, 3, 1024)), ('s_gdn', (NS, 4, 128, 128)), ('s_gdnc', (NS, 3, 1536)),
    ]:
        O[name] = dout(name, shape)
    if debug:
        O['dbg'] = dout('dbg', (6, 128, 8, NT))

    es = ExitStack()
    with es:
        es.enter_context(nc.allow_non_contiguous_dma(reason="small parameter / state layouts"))
        S = Sched(nc, es)
        cnt = [0]

        def sb(st, shape, dt, name=None):
            cnt[0] += 1
            return Tl(st.enter_context(nc.sbuf_tensor("%s_%d" % (name or 't', cnt[0]), list(shape), dt)))

        def mm(out, lhsT, rhs, start=True, stop=True):
            S.op('pe', lambda: nc.tensor.matmul(out.ap, lhsT=lhsT.ap, rhs=rhs.ap, start=start, stop=stop),
                 reads=_u(lhsT, rhs), writes=_u(out), inc=stop)

        def tr(out, in_, ident):
            S.op('pe', lambda: nc.tensor.transpose(out=out.ap, in_=in_.ap, identity=ident.ap), reads=_u(in_, ident), writes=_u(out))

        def act(out, in_, func, bias=None, scale=None, accum=None):
            kw = {}
            if bias is not None:
                kw['bias'] = _ap(bias)
            if scale is not None:
                kw['scale'] = _ap(scale)
            if accum is not None:
                kw['accum_out'] = accum.ap
            S.op('act', lambda: nc.scalar.activation(out=out.ap, in_=in_.ap, func=func, **kw),
                 reads=_u(in_, bias, scale), writes=_u(out, accum))

        def E(e):
            return {'dve': nc.vector, 'pool': nc.gpsimd}[e]

        def tt(e, out, in0, in1, op):
            S.op(e, lambda: E(e).tensor_tensor(out=out.ap, in0=in0.ap, in1=in1.ap, op=op), reads=_u(in0, in1), writes=_u(out))

        def ts(e, out, in0, s1, op0, s2=None, op1=None):
            kw = dict(out=out.ap, in0=in0.ap, scalar1=_ap(s1), scalar2=_ap(s2), op0=op0)
            if op1 is not None:
                kw['op1'] = op1
            S.op(e, lambda: E(e).tensor_scalar(**kw), reads=_u(in0, s1, s2), writes=_u(out))

        def stt(out, in0, scalar, in1, op0, op1, accum=None):
            kw = {}
            if accum is not None:
                kw['accum_out'] = accum.ap
            S.op('dve', lambda: nc.vector.scalar_tensor_tensor(out=out.ap, in0=in0.ap, scalar=_ap(scalar), in1=in1.ap, op0=op0, op1=op1, **kw),
                 reads=_u(in0, scalar, in1), writes=_u(out, accum))

        def cp(e, out, in_):
            if e == 'act':
                S.op('act', lambda: nc.scalar.copy(out=out.ap, in_=in_.ap), reads=_u(in_), writes=_u(out))
            else:
                S.op(e, lambda: E(e).tensor_copy(out=out.ap, in_=in_.ap), reads=_u(in_), writes=_u(out))

        def scan(out, d0, d1, init):
            S.op('dve', lambda: nc.vector.tensor_tensor_scan(out=out.ap, data0=d0.ap, data1=d1.ap, initial=_ap(init), op0=ALU.mult, op1=ALU.add),
                 reads=_u(d0, d1, init), writes=_u(out))

        def memset(e, out, val):
            S.op(e, lambda: E(e).memset(out.ap, val), writes=_u(out))

        def red(out, in_, op=ALU.add):
            S.op('dve', lambda: nc.vector.tensor_reduce(out=out.ap, in_=in_.ap, axis=mybir.AxisListType.X, op=op), reads=_u(in_), writes=_u(out))

        def dma(q, out, in_, ds):
            S.dma(q, _ap(out), _ap(in_), ds, reads=_u(in_), writes=_u(out))

        P = es
        xT = sb(P, [128, 8, NT], F32, 'xT')
        xun = [Unit() for _ in range(17)]
        CH = [(c * 128, 128) for c in range(16)] + [(T, NS)]

        def xch(c):
            c0, n = CH[c]
            return V(xT.t[:, :, c0:c0 + n], (xun[c],))

        def xcols(c0, n):
            us = tuple(xun[c] for c in range(17) if CH[c][0] < c0 + n and CH[c][0] + CH[c][1] > c0)
            return V(xT.t[:, :, c0:c0 + n], us)

        psb = [Tl(es.enter_context(nc.psum_tensor("ps%d" % i, [128, 512], F32))) for i in range(8)]
        psrot = {'set': list(range(8)), 'i': 0}

        def psn():
            s = psrot['set']
            b = psb[s[psrot['i'] % len(s)]]
            psrot['i'] += 1
            return b

        d_init = S.dsem('init')
        d_out = S.dsem('out')
        d_w = [S.dsem('w%d' % i) for i in range(6)]
        d_st = [S.dsem('st%d' % i) for i in range(8)]

        ident = sb(P, [128, 128], F32, 'ident')
        identb = sb(P, [128, 128], BF16, 'identb')
        ones_bf = sb(P, [128, 128], BF16, 'ones')
        ones_f = sb(P, [128, 128], F32, 'onesf')
        triu = sb(P, [128, 128], F32, 'triu')
        mask01 = sb(P, [128, 128], BF16, 'mask01')
        negm_f = sb(P, [128, 128], F32, 'negmf')
        posm_f = sb(P, [128, 128], F32, 'posmf')
        neghalf_ = sb(P, [128, 1], F32, 'neghalf')
        sq_s = sb(P, [128, 8, 128], BF16, 'sq_s')
        v_s = sb(P, [128, 512], F32, 'v_s')
        rstd_s = sb(P, [128, 512], F32, 'rstd_s')
        ntmp = sb(P, [128, 8, 128], BF16, 'ntmp')
        gmix = sb(P, [128, 2, 8], F32, 'gmix')
        gmlp = sb(P, [128, 2, 8], F32, 'gmlp')
        gfin = sb(P, [128, 8], F32, 'gfin')

        memset('pool', ident[:], 1.0)
        S.op('pool', lambda: nc.gpsimd.affine_select(out=ident.t[:], in_=ident.t[:], pattern=[[-1, 128]], compare_op=ALU.is_equal,
                                                      fill=0.0, base=0, channel_multiplier=1), reads=[ident.u], writes=[ident.u])
        cp('dve', identb[:], ident[:])
        memset('pool', ones_bf[:], 1.0)
        memset('pool', ones_f[:], 1.0)
        memset('dve', neghalf_[:], -0.5)
        memset('pool', triu[:], 1.0)
        S.op('pool', lambda: nc.gpsimd.affine_select(out=triu.t[:], in_=triu.t[:], pattern=[[1, 128]], compare_op=ALU.is_ge,
                                                      fill=0.0, base=0, channel_multiplier=-1), reads=[triu.u], writes=[triu.u])
        cp('dve', mask01[:], triu[:])
        ts('dve', negm_f[:], triu[:], -1.0, ALU.add, -NEG, ALU.mult)
        memset('pool', posm_f[:], 1.0)
        S.op('pool', lambda: nc.gpsimd.affine_select(out=posm_f.t[:], in_=posm_f.t[:], pattern=[[-1, 128]], compare_op=ALU.is_gt,
                                                      fill=0.0, base=0, channel_multiplier=1), reads=[posm_f.u], writes=[posm_f.u])
        ts('dve', posm_f[:], posm_f[:], -1.0, ALU.add, NEG, ALU.mult)

        dma('sp', gmix[:], I['norm_mix'].rearrange("l (k p) -> p l k", p=128), d_init)
        dma('sp', gmlp[:], I['norm_mlp'].rearrange("l (k p) -> p l k", p=128), d_init)
        dma('sp', gfin[:], I['norm_final'].rearrange("(k p) -> p k", p=128), d_init)

        def ddump(v, off, width):
            if debug:
                np_ = v.ap.shape[0]
                dst = O['dbg'][5].rearrange("p k n -> p (k n)")[0:np_, off:off + width]
                dma('pool', dst, v, d_out)

        def dump(i):
            if debug:
                dma('sp', O['dbg'][i], xcols(0, NT), d_out)

        def rmsnorm_fm(xv, gv, hT, n, K=8, eng='pool'):
            sq = sq_s[:, 0:K, 0:n]
            tt(eng, sq, xv, xv, ALU.mult)
            pb = psn()
            for k in range(K):
                mm(pb[:, 0:n], ones_bf[:], sq_s[:, k, 0:n], start=(k == 0), stop=(k == K - 1))
            act(v_s[:, 0:n], pb[:, 0:n], AF.Ln, scale=1.0 / (128 * K), bias=EPS)
            act(rstd_s[:, 0:n], v_s[:, 0:n], AF.Exp, scale=-0.5)
            tmp = ntmp[:, 0:K, 0:n]
            tt('dve', tmp, xv, rstd_s[:, 0:n].un(1).bc([128, K, n]), ALU.mult)
            tt(eng, hT, tmp, gv.un(2).bc([128, K, n]), ALU.mult)

        def proj(Wt, c0, M, hT, out):
            for k in range(8):
                mm(out, Wt[:, k, c0:c0 + M], hT[:, k, :], start=(k == 0), stop=(k == 7))

        def projA(hT, Wt, c0, N, out):
            for k in range(8):
                mm(out, hT[:, k, :], Wt[:, k, c0:c0 + N], start=(k == 0), stop=(k == 7))

        def pnorm_part(po, n, gcol, gate, out, osb, sqo, t1, H=4):
            n4 = H * n
            cp('act', osb[:, 0:n4], po)
            act(sqo[:, 0:n4], po, AF.Square)
            pn = psn()
            mm(pn[:, 0:n4], ones_bf[:], sqo[:, 0:n4])
            act(v_s[:, 0:n4], pn[:, 0:n4], AF.Ln, scale=1.0 / 128, bias=EPS)
            act(rstd_s[:, 0:n4], v_s[:, 0:n4], AF.Exp, scale=-0.5)
            tt('dve', t1[:, 0:n4], osb[:, 0:n4], rstd_s[:, 0:n4], ALU.mult)
            stt(out, t1[:, 0:n4].re("p (h n) -> p h n", h=H), gcol, gate, ALU.mult, ALU.mult)

        def load_w(st, name, src, ncols, dsem, nsplit=8):
            w = sb(st, [128, 8, ncols], BF16, name)
            sv = src.rearrange("(k p) n -> p k n", p=128)
            for k in range(8):
                dma('pool', w[:, k, :], sv[:, k, :], dsem)
            return w

        hT_all = sb(P, [128, 8, NT], BF16, 'hTall')
        hun = [Unit() for _ in range(17)]

        def hcols(c0, n):
            us = tuple(hun[c] for c in range(17) if CH[c][0] < c0 + n and CH[c][0] + CH[c][1] > c0)
            return V(hT_all.t[:, :, c0:c0 + n], us)

        def hch(c):
            return hcols(*CH[c])

        def norm_chunk(c, gv, eng='pool'):
            rmsnorm_fm(xch(c), gv, hch(c), CH[c][1], eng=eng)

        def norm_all(gv):
            for c in range(17):
                norm_chunk(c, gv)

        def phase0():
            with ExitStack() as ph:
                xin = [sb(ph, [128, D], F32, 'xin') for _ in range(2)]
                xin_s = sb(ph, [NS, D], F32, 'xins')
                dx = [S.dsem('x0'), S.dsem('x1')]
                dma('sp', xin_s[:], I['xs'], d_init)
                for blk in range(16):
                    b = blk % 2
                    dma('sp', xin[b][:], I['xp'][blk * 128:(blk + 1) * 128, :], dx[b])
                    for half in range(2):
                        pb = psn()
                        for j in range(4):
                            k = half * 4 + j
                            tr(pb[:, j * 128:(j + 1) * 128], xin[b][:, k * 128:(k + 1) * 128], ident[:])
                        cp('act' if half == 0 else 'dve', xch(blk)[:, half * 4:(half + 1) * 4, :], pb[:].re("p (a b) -> p a b", a=4))
                    norm_chunk(blk, gmix[:, 0, :], 'dve')
                pb = psn()
                for k in range(8):
                    tr(pb[:, k * NS:(k + 1) * NS], xin_s[:, k * 128:(k + 1) * 128], ident[0:NS, 0:NS])
                cp('dve', xch(16), pb[:, 0:8 * NS].re("p (k s) -> p k s", k=8))
                norm_chunk(16, gmix[:, 0, :], 'dve')
                S.barrier()

        def mlp_phase(layer, do_norm, cg_hook=None, st_hook=None):
            with ExitStack() as ph:
                abuf = [sb(ph, [128, 4, NT], BF16, 'abuf') for _ in range(2)]
                wup = [sb(ph, [128, 8, 512], BF16, 'wup') for _ in range(2)]
                wdn = [sb(ph, [128, 4, 1024], BF16, 'wdn') for _ in range(2)]
                rbuf = [sb(ph, [128, 512], BF16, 'rbuf') for _ in range(3)]
                rs_s = sb(ph, [128, 4, NS], BF16, 'rss')
                pss = psb[7]
                psrot['set'] = list(range(7))
                us_up = pss.u
                us_dn = pss.u
                wupv = I['w_up'][layer].rearrange("(k p) n -> p k n", p=128)
                wdnv = I['w_down'][layer].rearrange("(k p) n -> p k n", p=128)

                def loadw(f):
                    b = f % 2
                    dma('pool', wup[b][:], wupv[:, :, f * 512:(f + 1) * 512], d_w[b])
                    dma('pool', wdn[b][:], wdnv[:, f * 4:(f + 1) * 4, :], d_w[2 + b])
                loadw(0)
                if do_norm:
                    norm_all(gmlp[:, layer, :])
                if st_hook is not None:
                    st_hook(ph)
                ri = 0
                for f in range(8):
                    b = f % 2
                    if f + 1 < 8:
                        loadw(f + 1)
                    for m in range(4):
                        for half in range(2):
                            pbs = [psn(), psn()]
                            for k in range(8):
                                lw = wup[b][:, k, m * 128:(m + 1) * 128]
                                for j in range(2):
                                    c0 = half * 1024 + j * 512
                                    mm(pbs[j][:, :], lw, hcols(c0, 512)[:, k, :], start=(k == 0), stop=(k == 7))
                                if half == 1:
                                    mm(V(pss.t[:, m * NS:(m + 1) * NS], (us_up,)), lw, hcols(T, NS)[:, k, :], start=(k == 0), stop=(k == 7))
                            for j in range(2):
                                c0 = half * 1024 + j * 512
                                r = rbuf[ri % 3]
                                ri += 1
                                act(r[:], pbs[j][:], AF.Relu)
                                tt('pool', abuf[b][:, m, c0:c0 + 512], r[:], r[:], ALU.mult)
                    act(rs_s[:], V(pss.t[:, 0:4 * NS], (us_up,)).re("p (m s) -> p m s", m=4), AF.Relu)
                    tt('pool', abuf[b][:, :, T:NT], rs_s[:], rs_s[:], ALU.mult)
                    for cg in range(4):
                        for mo in range(8):
                            pb = psn()
                            for k in range(4):
                                mm(pb[:, :], wdn[b][:, k, mo * 128:(mo + 1) * 128], abuf[b][:, k, cg * 512:(cg + 1) * 512], start=(k == 0), stop=(k == 3))
                            xv = xcols(cg * 512, 512)[:, mo, :]
                            tt('dve', xv, pb[:, :], xv, ALU.add)
                        if f == 7 and cg_hook is not None:
                            for c in range(cg * 4, cg * 4 + 4):
                                cg_hook(c)
                    for mo in range(8):
                        for k in range(4):
                            mm(V(pss.t[:, 64 + mo * NS:64 + (mo + 1) * NS], (us_dn,)), wdn[b][:, k, mo * 128:(mo + 1) * 128], abuf[b][:, k, T:NT],
                               start=(k == 0), stop=(k == 3))
                    xs_ = xch(16)
                    tt('dve', xs_, V(pss.t[:, 64:64 + 8 * NS], (us_dn,)).re("p (m s) -> p m s", m=8), xs_, ALU.add)
                    if f == 7 and cg_hook is not None:
                        cg_hook(16)
                psrot['set'] = list(range(8))
                S.barrier()

        def load_wc(st, name, src, c0, ncols, dsem):
            w = sb(st, [128, 8, ncols], BF16, name)
            sv = src.rearrange("(k p) n -> p k n", p=128)
            for k in range(8):
                dma('pool', w[:, k, :], sv[:, k, c0:c0 + ncols], dsem)
            return w

        def load_wr(st, name, src, r0, dsem):
            w = sb(st, [128, 4, D], BF16, name)
            sv = src[r0:r0 + 512, :].rearrange("(k p) n -> p k n", p=128)
            for k in range(4):
                dma('pool', w[:, k, :], sv[:, k, :], dsem)
            return w

        def out_proj(Wo, om, n, xc):
            for half in range(2):
                pxo = psn()
                for j in range(4):
                    m = half * 4 + j
                    for k in range(4):
                        mm(pxo[:, j * n:(j + 1) * n], Wo[:, k, m * 128:(m + 1) * 128], om[:, k, 0:n], start=(k == 0), stop=(k == 3))
                xv = xc[:, half * 4:(half + 1) * 4, :]
                tt('dve', xv, pxo[:, 0:4 * n].re("p (j n) -> p j n", j=4), xv, ALU.add)

        def pass_gla(pre=None):
            with ExitStack() as ph:
                Win, Wout, Wgate = pre
                bgate = sb(ph, [64, 4], F32, 'bgate')
                dma('sp', bgate[:], I['b_gla_gate'].rearrange("(h k) -> k h", k=64), d_init)
                negb = sb(ph, [64, 4], F32, 'negb')
                ts('dve', negb[:], bgate[:], -1.0, ALU.mult)
                ggla = sb(ph, [128, 1], F32, 'ggla')
                dma('sp', ggla[:], I['g_gla_norm'].rearrange("(p o) -> p o", o=1), d_init)
                glrT = sb(ph, [16, 128], BF16, 'glrT')
                sp_t = sb(ph, [64, 4, 128], F32, 'sp_t')
                cs_t = sb(ph, [64, 4, 128], F32, 'cs_t')
                Ep = sb(ph, [64, 4, 128], F32, 'Ep')
                Em = sb(ph, [64, 4, 128], F32, 'Em')
                qs = sb(ph, [64, 4, 128], BF16, 'qs')
                ks = sb(ph, [64, 4, 128], BF16, 'ks')
                qsf = sb(ph, [64, 4, NS], F32, 'qsf')
                vT = sb(ph, [128, 4, 128], BF16, 'vT')
                sr = sb(ph, [128, 4, 128], BF16, 'sr')
                ones64 = sb(ph, [64, 128], F32, 'ones64')
                memset('pool', ones64[:], 1.0)
                kvtok = sb(ph, [128, 768], BF16, 'kvtok')
                attT = sb(ph, [128, 4, 128], BF16, 'attT')
                Sg = sb(ph, [64, 4, 128], F32, 'Sg')
                Sbf = sb(ph, [64, 4, 128], BF16, 'Sbf')
                tmpS = sb(ph, [64, 4, 128], F32, 'tmpS')
                memset('pool', Sg[:], 0.0)
                memset('pool', Sbf[:], 0.0)
                osb = sb(ph, [128, 512], F32, 'osb')
                sqo = sb(ph, [128, 512], BF16, 'sqo')
                t1 = osb
                omix = sb(ph, [128, 4, 128], BF16, 'omix')
                ktok_s = sb(ph, [NS, 4, 64], BF16, 'ktoks')
                vtok_s = sb(ph, [NS, 4, 128], BF16, 'vtoks')
                Kd = sb(ph, [NS, 2, 256], BF16, 'Kd')
                a_s = sb(ph, [64, 4, NS], F32, 'a_s')
                identb16 = identb[0:NS, 0:NS]
                sgl = [sb(ph, [64, 2, 4, 128], F32, 'sgl') for _ in range(2)]
                sgl2 = sgl

                qs2 = [qs, sb(ph, [64, 4, 128], BF16, 'qs2')]
                kv2 = [kvtok, sb(ph, [128, 768], BF16, 'kvtok2')]
                att2 = [attT, sb(ph, [128, 4, 128], BF16, 'attT2')]
                sr2 = [sr, sb(ph, [128, 4, 128], BF16, 'sr2')]
                Ep2 = [Ep, sb(ph, [64, 4, 128], F32, 'Ep2')]

                def gla_front(c):
                    c0, n = CH[c]
                    p = c % 2
                    qs_, kv_, att_, sr_, Ep_ = qs2[p], kv2[p], att2[p], sr2[p], Ep2[p]
                    hTn = hcols(c0, n)
                    pq = psn()
                    for h in range(4):
                        proj(Win, h * 64, 64, hTn, pq[0:64, h * n:(h + 1) * n])
                    pk = psn()
                    for h in range(4):
                        proj(Win, 256 + h * 64, 64, hTn, pk[0:64, h * n:(h + 1) * n])
                    pgl = psn()
                    proj(Win, 1024, 16, hTn, pgl[0:16, 0:n])
                    cp('act', glrT[:, 0:n], pgl[0:16, 0:n])
                    pg = psn()
                    for h in range(4):
                        mm(pg[0:64, h * n:(h + 1) * n], Wgate[:, h * 64:(h + 1) * 64], glrT[:, 0:n])
                    for h in range(4):
                        act(sp_t[:, h, 0:n], pg[0:64, h * n:(h + 1) * n], AF.Exp, scale=-1.0, bias=negb[:, h:h + 1])
                    act(sp_t[:, :, 0:n], sp_t[:, :, 0:n], AF.Ln, bias=1.0)
                    pqv = pq[0:64, 0:4 * n].re("p (h n) -> p h n", h=4)
                    pkv = pk[0:64, 0:4 * n].re("p (h n) -> p h n", h=4)
                    for h in range(4):
                        scan(cs_t[:, h, 0:n], ones64[:, 0:n], sp_t[:, h, 0:n], 0.0)
                    act(Ep_[:, :, 0:n], cs_t[:, :, 0:n], AF.Exp, scale=-1.0 / 16)
                    act(Em[:, :, 0:n], cs_t[:, :, 0:n], AF.Exp, scale=1.0 / 16)
                    stt(qs_[:, :, 0:n], pqv, 0.125, Ep_[:, :, 0:n], ALU.mult, ALU.mult)
                    tt('dve', ks[:, :, 0:n], pkv, Em[:, :, 0:n], ALU.mult)
                    pv = psn()
                    for h in range(4):
                        proj(Win, 512 + h * 128, 128, hTn, pv[:, h * n:(h + 1) * n])
                    cp('act', vT[:, :, 0:n], pv[:, 0:4 * n].re("p (h n) -> p h n", h=4))
                    pr = psn()
                    for h in range(4):
                        proj(Win, 1040 + h * 128, 128, hTn, pr[:, h * n:(h + 1) * n])
                    act(sr_[:, :, 0:n], pr[:, 0:4 * n].re("p (h n) -> p h n", h=4), AF.Silu)
                    pt = psn()
                    ptb = pt[:].bitcast(BF16)
                    for h in range(4):
                        tr(ptb[0:n, h * 64:(h + 1) * 64], ks[:, h, 0:n], identb[0:64, 0:64])
                    for h in range(4):
                        tr(ptb[0:n, 256 + h * 128:256 + (h + 1) * 128], vT[:, h, 0:n], identb[:])
                    cp('dve', kv_[0:n, :], ptb[0:n, 0:768])
                    pa = psn()
                    for h in range(4):
                        mm(pa[0:n, h * n:(h + 1) * n], ks[:, h, 0:n], qs_[:, h, 0:n])
                    tt('dve', att_[0:n, :, 0:n], pa[0:n, 0:4 * n].re("p (h n) -> p h n", h=4), mask01[0:n, 0:n].un(1).bc([n, 4, n]), ALU.mult)

                def gla_back(c):
                    c0, n = CH[c]
                    p = c % 2
                    qs_, kv_, att_, sr_, Ep_ = qs2[p], kv2[p], att2[p], sr2[p], Ep2[p]
                    po = psn()
                    for h in range(4):
                        mm(po[:, h * n:(h + 1) * n], kv_[0:n, 256 + h * 128:256 + (h + 1) * 128], att_[0:n, h, 0:n], start=True, stop=False)
                        mm(po[:, h * n:(h + 1) * n], Sbf[:, h, :], qs_[:, h, 0:n], start=False, stop=True)
                    pS = psn()
                    for h in range(4):
                        mm(pS[0:64, h * 128:(h + 1) * 128], kv_[0:n, h * 64:(h + 1) * 64], kv_[0:n, 256 + h * 128:256 + (h + 1) * 128])
                    tt('dve', tmpS[:], pS[0:64, :].re("p (h v) -> p h v", h=4), Sg[:], ALU.add)
                    tt('pool', Sg[:], tmpS[:], Ep_[:, :, n - 1:n].bc([64, 4, 128]), ALU.mult)
                    cp('act', Sbf[:], Sg[:])
                    if c == 15:
                        dma('sp', O['p_gla'].rearrange("h k v -> k h v"), Sg[:], d_out)
                    pnorm_part(po[:, 0:4 * n], n, ggla[:, 0:1], sr_[:, :, 0:n], omix[:, :, 0:n], osb, sqo, t1)
                    out_proj(Wout, omix, n, xch(c))

                gla_front(0)
                for c in range(16):
                    if c + 1 < 16:
                        gla_front(c + 1)
                    gla_back(c)

                for c in range(16, 17):
                    c0, n = CH[c]
                    sample = (c == 16)
                    xc = xch(c)
                    hTn = hcols(c0, n)
                    pq = psn()
                    for h in range(4):
                        proj(Win, h * 64, 64, hTn, pq[0:64, h * n:(h + 1) * n])
                    pk = psn()
                    for h in range(4):
                        proj(Win, 256 + h * 64, 64, hTn, pk[0:64, h * n:(h + 1) * n])
                    pgl = psn()
                    proj(Win, 1024, 16, hTn, pgl[0:16, 0:n])
                    cp('act', glrT[:, 0:n], pgl[0:16, 0:n])
                    pg = psn()
                    for h in range(4):
                        mm(pg[0:64, h * n:(h + 1) * n], Wgate[:, h * 64:(h + 1) * 64], glrT[:, 0:n])
                    for h in range(4):
                        act(sp_t[:, h, 0:n], pg[0:64, h * n:(h + 1) * n], AF.Exp, scale=-1.0, bias=negb[:, h:h + 1])
                    act(sp_t[:, :, 0:n], sp_t[:, :, 0:n], AF.Ln, bias=1.0)
                    pqv = pq[0:64, 0:4 * n].re("p (h n) -> p h n", h=4)
                    pkv = pk[0:64, 0:4 * n].re("p (h n) -> p h n", h=4)
                    if not sample:
                        for h in range(4):
                            scan(cs_t[:, h, 0:n], ones64[:, 0:n], sp_t[:, h, 0:n], 0.0)
                        act(Ep[:, :, 0:n], cs_t[:, :, 0:n], AF.Exp, scale=-1.0 / 16)
                        act(Em[:, :, 0:n], cs_t[:, :, 0:n], AF.Exp, scale=1.0 / 16)
                        stt(qs[:, :, 0:n], pqv, 0.125, Ep[:, :, 0:n], ALU.mult, ALU.mult)
                        tt('dve', ks[:, :, 0:n], pkv, Em[:, :, 0:n], ALU.mult)
                    else:
                        act(a_s[:], sp_t[:, :, 0:n], AF.Exp, scale=-1.0 / 16)
                        act(qsf[:], pqv, AF.Copy, scale=0.125)
                        cp('dve', ks[:, :, 0:n], pkv)
                    pv = psn()
                    for h in range(4):
                        proj(Win, 512 + h * 128, 128, hTn, pv[:, h * n:(h + 1) * n])
                    cp('act', vT[:, :, 0:n], pv[:, 0:4 * n].re("p (h n) -> p h n", h=4))
                    pr = psn()
                    for h in range(4):
                        proj(Win, 1040 + h * 128, 128, hTn, pr[:, h * n:(h + 1) * n])
                    act(sr[:, :, 0:n], pr[:, 0:4 * n].re("p (h n) -> p h n", h=4), AF.Silu)
                    if sample:
                        po = psb[7]
                        psrot['set'] = list(range(7))
                    else:
                        po = psn()
                    pt = psn()
                    ptb = pt[:].bitcast(BF16)
                    for h in range(4):
                        tr(ptb[0:n, h * 64:(h + 1) * 64], ks[:, h, 0:n], identb[0:64, 0:64])
                    for h in range(4):
                        tr(ptb[0:n, 256 + h * 128:256 + (h + 1) * 128], vT[:, h, 0:n], identb[:])
                    if not sample:
                        cp('dve', kvtok[0:n, :], ptb[0:n, 0:768])
                        pa = psn()
                        for h in range(4):
                            mm(pa[0:n, h * n:(h + 1) * n], ks[:, h, 0:n], qs[:, h, 0:n])
                        tt('dve', attT[0:n, :, 0:n], pa[0:n, 0:4 * n].re("p (h n) -> p h n", h=4), mask01[0:n, 0:n].un(1).bc([n, 4, n]), ALU.mult)
                        for h in range(4):
                            mm(po[:, h * n:(h + 1) * n], kvtok[0:n, 256 + h * 128:256 + (h + 1) * 128], attT[0:n, h, 0:n], start=True, stop=False)
                            mm(po[:, h * n:(h + 1) * n], Sbf[:, h, :], qs[:, h, 0:n], start=False, stop=True)
                        pS = psn()
                        for h in range(4):
                            mm(pS[0:64, h * 128:(h + 1) * 128], kvtok[0:n, h * 64:(h + 1) * 64], kvtok[0:n, 256 + h * 128:256 + (h + 1) * 128])
                        tt('dve', tmpS[:], pS[0:64, :].re("p (h v) -> p h v", h=4), Sg[:], ALU.add)
                        tt('pool', Sg[:], tmpS[:], Ep[:, :, n - 1:n].bc([64, 4, 128]), ALU.mult)
                        cp('act', Sbf[:], Sg[:])
                        if c == 15:
                            dma('sp', O['p_gla'].rearrange("h k v -> k h v"), Sg[:], d_out)
                    else:
                        cp('dve', ktok_s[:], ptb[0:NS, 0:256].re("p (h k) -> p h k", h=4))
                        cp('dve', vtok_s[:], ptb[0:NS, 256:768].re("p (h v) -> p h v", h=4))
                        for gi in range(8):
                            b = gi % 2
                            s0 = gi * 2
                            dma('sp', sgl[b][:], I['sgla'][s0:s0 + 2].rearrange("s h k v -> k s h v"), d_st[b])
                            tt('dve', Kd[:], ktok_s[:].re("p h k -> p (h k)").un(1).bc([NS, 2, 256]), identb[0:NS, s0:s0 + 2].un(2).bc([NS, 2, 256]), ALU.mult)
                            pso = [psn(), psn()]
                            for si in range(2):
                                for h in range(4):
                                    mm(pso[si][0:64, h * 128:(h + 1) * 128], Kd[:, si, h * 64:(h + 1) * 64], vtok_s[:, h, :])
                            av = a_s[:].re("p h s -> p s h")[:, s0:s0 + 2, :].un(3).bc([64, 2, 4, 128])
                            tt('pool', sgl2[b][:], sgl[b][:], av, ALU.mult)
                            for si in range(2):
                                tt('dve', sgl2[b][:, si], pso[si][0:64, :].re("p (h v) -> p h v", h=4), sgl2[b][:, si], ALU.add)
                            for si in range(2):
                                for h in range(4):
                                    col = h * NS + s0 + si
                                    mm(po[:, col:col + 1], sgl2[b][:, si, h, :], qsf[:, h, s0 + si:s0 + si + 1])
                            dma('sp', O['s_gla'][s0:s0 + 2].rearrange("s h k v -> k s h v"), sgl2[b][:], d_st[2 + b])
                    pnorm_part(po[:, 0:4 * n], n, ggla[:, 0:1], sr[:, :, 0:n], omix[:, :, 0:n], osb, sqo, t1)
                    out_proj(Wout, omix, n, xc)
                    psrot['set'] = list(range(8))
                S.barrier()

        def pass_s5(pre=None):
            with ExitStack() as ph:
                Win, Wout, Wglu = pre
                bglu = sb(ph, [128, 4], F32, 'bglu')
                dma('sp', bglu[:], I['b_s5_glu'].rearrange("(m p) -> p m", p=128), d_init)
                DU = sb(ph, [128, 4], F32, 'DU')
                dma('sp', DU[:], I['s5_d'].rearrange("(u q) -> q u", q=128), d_init)

                def s16(name):
                    return sb(ph, [128, 16], F32, name)
                LR = s16('LR'); LI = s16('LI'); LDT = s16('LDT')
                dma('sp', LR[:], I['s5_lam_re'].rearrange("(t q) -> q t", q=128), d_init)
                dma('sp', LI[:], I['s5_lam_im'].rearrange("(t q) -> q t", q=128), d_init)
                ldv = I['s5_log_dt'].rearrange("(t g) -> g t", g=2)
                for g2 in range(2):
                    dma('sp', LDT[g2 * 64:(g2 + 1) * 64, :], ldv[g2].partition_broadcast(64), d_init)
                DT = s16('DT'); Rm = s16('Rm'); TH = s16('TH'); t16a = s16('t16a'); t16b = s16('t16b'); t16c = s16('t16c')
                AR = s16('AR'); AI_ = s16('AI'); KR = s16('KR'); KI = s16('KI')
                nre = s16('nre'); den = s16('den'); rden = s16('rden')
                c16a = s16('c16a'); c16b = s16('c16b'); Hre = s16('Hre'); Him = s16('Him')
                ginit_re = s16('gire'); ginit_im = s16('giim'); glast_re = s16('glre'); glast_im = s16('glim')
                t16i = sb(ph, [128, 16], I32, 't16i')
                Bpad_re = sb(ph, [128, 16, 128], BF16, 'Bpr')
                Bpad_im = sb(ph, [128, 16, 128], BF16, 'Bpi')
                Cpad_re = sb(ph, [128, 16, 128], BF16, 'Cpr')
                Cpad_imn = sb(ph, [128, 16, 128], BF16, 'Cpi')
                uTs = [sb(ph, [128, 4, 128], BF16, 'uT') for _ in range(2)]
                hre_gs = [[sb(ph, [128, 4, 128], BF16, 'hre') for _ in range(4)] for _ in range(2)]
                him_gs = [[sb(ph, [128, 4, 128], BF16, 'him') for _ in range(4)] for _ in range(2)]
                yv = sb(ph, [128, 4, 128], F32, 'yv')
                ygb = sb(ph, [128, 4, 128], BF16, 'ygb')
                sg_t = sb(ph, [128, 4, 128], BF16, 'sg')
                omix = sb(ph, [128, 4, 128], BF16, 'omix')
                hout_tok = sb(ph, [NS, 128], F32, 'hout')

                act(DT[:], LDT[:], AF.Exp)
                tt('dve', t16a[:], LR[:], DT[:], ALU.mult)
                act(Rm[:], t16a[:], AF.Exp)
                tt('dve', TH[:], LI[:], DT[:], ALU.mult)
                ts('dve', t16a[:], TH[:], 1.0 / (2 * math.pi), ALU.mult)
                cp('dve', t16i[:], t16a[:])
                cp('dve', t16b[:], t16i[:])
                tt('dve', t16c[:], t16a[:], t16b[:], ALU.subtract)
                ts('dve', TH[:], t16c[:], 2 * math.pi, ALU.mult)

                pst = ExitStack()
                NTAU = 129
                COS = sb(pst, [128, 16, NTAU], F32, 'COS')
                SIN = sb(pst, [128, 16, NTAU], F32, 'SIN')
                with ExitStack() as tmpst:
                    taui = sb(tmpst, [128, NTAU], I32, 'taui')
                    tauf = sb(tmpst, [128, NTAU], F32, 'tauf')
                    S.op('pool', lambda: nc.gpsimd.iota(taui.t[:], pattern=[[1, NTAU]], base=0, channel_multiplier=0), writes=[taui.u])
                    cp('dve', tauf[:], taui[:])
                    U0 = sb(tmpst, [128, 8, NTAU], F32, 'U0')
                    U1 = sb(tmpst, [128, 8, NTAU], F32, 'U1')
                    U2 = sb(tmpst, [128, 8, NTAU], F32, 'U2')
                    UI = sb(tmpst, [128, 8, NTAU], I32, 'UI')
                    for th in range(2):
                        tsl = slice(th * 8, (th + 1) * 8)
                        tt('dve', U0[:], TH[:, tsl].un(2).bc([128, 8, NTAU]), tauf[:].un(1).bc([128, 8, NTAU]), ALU.mult)
                        ts('dve', U0[:], U0[:], 1.0 / (2 * math.pi), ALU.mult)
                        for (dst, off) in ((SIN, 0.0), (COS, 0.25)):
                            ts('dve', U1[:], U0[:], off, ALU.add)
                            cp('dve', UI[:], U1[:])
                            cp('dve', U2[:], UI[:])
                            tt('dve', U1[:], U1[:], U2[:], ALU.subtract)
                            act(dst[:, tsl, :], U1[:], AF.Sin, scale=2 * math.pi)
                    S.barrier()
                tt('dve', AR[:], Rm[:], COS[:, :, 1], ALU.mult)
                tt('dve', AI_[:], Rm[:], SIN[:, :, 1], ALU.mult)
                ts('dve', nre[:], AR[:], -1.0, ALU.add)
                tt('dve', t16a[:], LR[:], LR[:], ALU.mult)
                tt('dve', t16b[:], LI[:], LI[:], ALU.mult)
                tt('dve', den[:], t16a[:], t16b[:], ALU.add)
                S.op('dve', lambda: nc.vector.reciprocal(out=rden.t[:], in_=den.t[:]), reads=[den.u], writes=[rden.u])
                tt('dve', t16a[:], nre[:], LR[:], ALU.mult)
                tt('dve', t16b[:], AI_[:], LI[:], ALU.mult)
                tt('dve', t16c[:], t16a[:], t16b[:], ALU.add)
                tt('dve', KR[:], t16c[:], rden[:], ALU.mult)
                tt('dve', t16a[:], AI_[:], LR[:], ALU.mult)
                tt('dve', t16b[:], nre[:], LI[:], ALU.mult)
                tt('dve', t16c[:], t16a[:], t16b[:], ALU.subtract)
                tt('dve', KI[:], t16c[:], rden[:], ALU.mult)
                with ExitStack() as tmpst:
                    BR = sb(tmpst, [128, 16, 16], F32, 'BR')
                    BI = sb(tmpst, [128, 16, 16], F32, 'BI')
                    dma('sp', BR[:], I['s5_b_re'].rearrange("(t q) h -> q t h", q=128), d_st[4])
                    dma('sp', BI[:], I['s5_b_im'].rearrange("(t q) h -> q t h", q=128), d_st[4])
                    CUr = sb(tmpst, [128, 4, 64], F32, 'CUr')
                    CUi = sb(tmpst, [128, 4, 64], F32, 'CUi')
                    dma('sp', CUr[:], I['s5_c_re'].rearrange("(u q) p -> q u p", q=128), d_st[4])
                    dma('sp', CUi[:], I['s5_c_im'].rearrange("(u q) p -> q u p", q=128), d_st[4])
                    BBR = sb(tmpst, [128, 16, 16], F32, 'BBR')
                    BBI = sb(tmpst, [128, 16, 16], F32, 'BBI')
                    b1 = sb(tmpst, [128, 16, 16], F32, 'b1')
                    b2 = sb(tmpst, [128, 16, 16], F32, 'b2')
                    krb = KR[:].un(2).bc([128, 16, 16])
                    kib = KI[:].un(2).bc([128, 16, 16])
                    tt('dve', b1[:], BR[:], krb, ALU.mult)
                    tt('dve', b2[:], BI[:], kib, ALU.mult)
                    tt('dve', BBR[:], b1[:], b2[:], ALU.subtract)
                    tt('dve', b1[:], BI[:], krb, ALU.mult)
                    tt('dve', b2[:], BR[:], kib, ALU.mult)
                    tt('dve', BBI[:], b1[:], b2[:], ALU.add)
                    mki = sb(tmpst, [128, 4, 4, 8], I32, 'mki')
                    MK = sb(tmpst, [128, 16, 8], F32, 'MK')
                    for g2 in range(2):
                        S.op('pool', lambda g2=g2: nc.gpsimd.iota(mki.t[g2 * 64:(g2 + 1) * 64], pattern=[[0, 4], [-2, 4], [1, 8]], base=-g2, channel_multiplier=0),
                             writes=[mki.u])
                    cp('dve', MK[:], mki[:].re("p a b c -> p (a b) c"))
                    ts('dve', MK[:], MK[:], 0.0, ALU.is_equal)
                    EXr = sb(tmpst, [128, 4, 128], F32, 'EX')
                    EX4 = EXr[:].re("p t (a b) -> p t a b", a=8)
                    EXC4 = EXr[:].re("p t (g q) -> p t g q", g=2)
                    MKC = sb(tmpst, [128, 4, 128], F32, 'MKC')

                    def tr4(dst, scale=None):
                        pb = psn()
                        for q in range(4):
                            tr(pb[:, q * 128:(q + 1) * 128], EXr[:, q, :], ident[:])
                        if scale is None:
                            cp('act', dst, pb[:].re("p (a b) -> p a b", a=4))
                        else:
                            act(dst, pb[:].re("p (a b) -> p a b", a=4), AF.Copy, scale=scale)
                    for tg in range(4):
                        tsl = slice(tg * 4, (tg + 1) * 4)
                        mkb = MK[:, tsl].un(3).bc([128, 4, 8, 16])
                        tt('dve', EX4, mkb, mkb, ALU.mult)
                        tr4(MKC[:])
                        for (src, dst) in ((BBR, Bpad_re), (BBI, Bpad_im)):
                            tt('dve', EX4, src[:, tsl].un(2).bc([128, 4, 8, 16]), mkb, ALU.mult)
                            tr4(dst[:, tsl, :])
                        for (src, dst, sgn) in ((CUr, Cpad_re, 1.0), (CUi, Cpad_imn, -1.0)):
                            tt('dve', EXC4, src[:, tg, :].un(1).un(1).bc([128, 4, 2, 64]), MKC[:].re("p t (g q) -> p t g q", g=2), ALU.mult)
                            tr4(dst[:, tsl, :], scale=sgn)
                    S.barrier()

                class TS:
                    pass
                TD, TP = TS(), TS()
                for T_, nm in ((TD, 'd'), (TP, 'p')):
                    T_.s5b = sb(pst, [128, 4, 128], F32, nm + 's5b')
                    T_.gin_re = sb(pst, [128, 4, 128], F32, nm + 'ginre')
                    T_.gin_im = sb(pst, [128, 4, 128], F32, nm + 'ginim')
                    T_.g_re = sb(pst, [128, 4, 128], F32, nm + 'g_re')
                    T_.g_im = sb(pst, [128, 4, 128], F32, nm + 'g_im')
                memset('pool', ginit_re[:], 0.0)
                memset('pool', ginit_im[:], 0.0)

                def rot(tau, ore, oim):
                    tt('dve', c16a[:], glast_re[:], COS[:, :, tau], ALU.mult)
                    tt('dve', c16b[:], glast_im[:], SIN[:, :, tau], ALU.mult)
                    tt('dve', ore[:], c16a[:], c16b[:], ALU.subtract)
                    tt('dve', c16a[:], glast_im[:], COS[:, :, tau], ALU.mult)
                    tt('dve', c16b[:], glast_re[:], SIN[:, :, tau], ALU.mult)
                    tt('dve', oim[:], c16a[:], c16b[:], ALU.add)

                def pre(c):
                    c0, n = CH[c]
                    uT = uTs[c % 2]
                    pu = psn()
                    for ut in range(4):
                        proj(Win, ut * 128, 128, hcols(c0, n), pu[:, ut * n:(ut + 1) * n])
                    cp('act', uT[:, :, 0:n], pu[:, 0:4 * n].re("p (h n) -> p h n", h=4))

                def post(c):
                    c0, n = CH[c]
                    uT = uTs[c % 2]
                    hre_g, him_g = hre_gs[c % 2], him_gs[c % 2]
                    py = psn()
                    for ut in range(4):
                        for q in range(4):
                            t = ut * 4 + q
                            mm(py[:, ut * n:(ut + 1) * n], Cpad_re[:, t, :], hre_g[ut][:, q, 0:n], start=(q == 0), stop=False)
                            mm(py[:, ut * n:(ut + 1) * n], Cpad_imn[:, t, :], him_g[ut][:, q, 0:n], start=False, stop=(q == 3))
                    tt('pool', yv[:, :, 0:n], uT[:, :, 0:n], DU[:].un(2).bc([128, 4, n]), ALU.mult)
                    tt('dve', yv[:, :, 0:n], py[:, 0:4 * n].re("p (u n) -> p u n", u=4), yv[:, :, 0:n], ALU.add)
                    act(ygb[:, :, 0:n], yv[:, :, 0:n], AF.Gelu_apprx_tanh)
                    pg2 = psn()
                    for m in range(4):
                        for k in range(4):
                            mm(pg2[:, m * n:(m + 1) * n], Wglu[:, k, m * 128:(m + 1) * 128], ygb[:, k, 0:n], start=(k == 0), stop=(k == 3))
                    for m in range(4):
                        act(sg_t[:, m, 0:n], pg2[:, m * n:(m + 1) * n], AF.Sigmoid, bias=bglu[:, m:m + 1])
                    tt('dve', omix[:, :, 0:n], ygb[:, :, 0:n], sg_t[:, :, 0:n], ALU.mult)
                    out_proj(Wout, omix, n, xch(c))
                    norm_chunk(c, gmlp[:, 0, :])

                for c in range(16):
                    c0, n = CH[c]
                    pre(c)
                    uT = uTs[c % 2]
                    hre_g, him_g = hre_gs[c % 2], him_gs[c % 2]

                    def mmgroup(tg):
                        pbr = psn()
                        pbi = psn()
                        for q in range(4):
                            t = tg * 4 + q
                            mm(pbr[:, q * n:(q + 1) * n], Bpad_re[:, t, :], uT[:, tg, 0:n])
                            mm(pbi[:, q * n:(q + 1) * n], Bpad_im[:, t, :], uT[:, tg, 0:n])
                        return (pbr[:, 0:4 * n].re("p (q n) -> p q n", q=4), pbi[:, 0:4 * n].re("p (q n) -> p q n", q=4))

                    def rot_in(e, tg, srcr, srci, T_):
                        Cg = COS[:, tg * 4:(tg + 1) * 4, 0:n]
                        Sn = SIN[:, tg * 4:(tg + 1) * 4, 0:n]
                        tt(e, T_.gin_re[:], srcr, Cg, ALU.mult)
                        tt(e, T_.s5b[:], srci, Sn, ALU.mult)
                        tt(e, T_.gin_re[:], T_.gin_re[:], T_.s5b[:], ALU.add)
                        tt(e, T_.gin_im[:], srci, Cg, ALU.mult)
                        tt(e, T_.s5b[:], srcr, Sn, ALU.mult)
                        tt(e, T_.gin_im[:], T_.gin_im[:], T_.s5b[:], ALU.subtract)

                    def scans(tg, T_):
                        for q in range(4):
                            t = tg * 4 + q
                            scan(T_.g_re[:, q, :], Rm[:, t:t + 1].bc([128, n]), T_.gin_re[:, q, :], ginit_re[:, t:t + 1])
                            scan(T_.g_im[:, q, :], Rm[:, t:t + 1].bc([128, n]), T_.gin_im[:, q, :], ginit_im[:, t:t + 1])

                    def rot_out(e, tg, T_):
                        Cg = COS[:, tg * 4:(tg + 1) * 4, 0:n]
                        Sn = SIN[:, tg * 4:(tg + 1) * 4, 0:n]
                        tt(e, T_.gin_re[:], T_.g_re[:], Cg, ALU.mult)
                        tt(e, T_.s5b[:], T_.g_im[:], Sn, ALU.mult)
                        tt(e, hre_g[tg][:], T_.gin_re[:], T_.s5b[:], ALU.subtract)
                        tt(e, T_.gin_im[:], T_.g_im[:], Cg, ALU.mult)
                        tt(e, T_.s5b[:], T_.g_re[:], Sn, ALU.mult)
                        tt(e, him_g[tg][:], T_.gin_im[:], T_.s5b[:], ALU.add)
                        cp('dve', glast_re[:, tg * 4:(tg + 1) * 4], T_.g_re[:, :, n - 1])
                        cp('dve', glast_im[:, tg * 4:(tg + 1) * 4], T_.g_im[:, :, n - 1])

                    r3, i3 = mmgroup(3)
                    cp('act', TP.g_re[:], r3)
                    cp('act', TP.g_im[:], i3)
                    rot_in('pool', 3, TP.g_re[:], TP.g_im[:], TP)
                    r0, i0 = mmgroup(0)
                    rot_in('dve', 0, r0, i0, TD)
                    scans(0, TD)
                    rot_out('dve', 0, TD)
                    scans(3, TP)
                    rot_out('pool', 3, TP)
                    for tg in (1, 2):
                        r_, i_ = mmgroup(tg)
                        rot_in('dve', tg, r_, i_, TD)
                        scans(tg, TD)
                        rot_out('dve', tg, TD)
                    rot(n, ginit_re, ginit_im)
                    if c == 15:
                        rot(n - 1, Hre, Him)
                        for (src, oname) in ((Hre, 'p_s5re'), (Him, 'p_s5im')):
                            pz = psn()
                            tr(pz[0:16, 0:128], src[:], ident[:])
                            cp('dve', hout_tok[0:16, 0:128], pz[0:16, 0:128])
                            dma('sp', O[oname], hout_tok[0:16, 0:128], d_out)
                    if c > 0:
                        post(c - 1)
                post(15)
                S.barrier()
                pst.close()

                with ExitStack() as sst:
                    h0re_tok = sb(sst, [NS, 2048], F32, 'h0re')
                    h0im_tok = sb(sst, [NS, 2048], F32, 'h0im')
                    hout2 = [sb(sst, [NS, 512], F32, 'hout2') for _ in range(2)]
                    hs_re = sb(sst, [128, 16, NS], F32, 'hsre')
                    hs_im = sb(sst, [128, 16, NS], F32, 'hsim')
                    s5m1 = sb(sst, [128, 16, NS], F32, 's5m1')
                    s5m2 = sb(sst, [128, 16, NS], F32, 's5m2')
                    dma('sp', h0re_tok[:], I['ss5re'], d_st[5])
                    dma('sp', h0im_tok[:], I['ss5im'], d_st[5])
                    pre(16)
                    uT = uTs[0]
                    hre_g, him_g = hre_gs[0], him_gs[0]
                    pzr = psn()
                    pzi = psn()
                    for t in range(16):
                        tr(pzr[:, t * NS:(t + 1) * NS], h0re_tok[:, t * 128:(t + 1) * 128], ident[0:NS, 0:NS])
                        tr(pzi[:, t * NS:(t + 1) * NS], h0im_tok[:, t * 128:(t + 1) * 128], ident[0:NS, 0:NS])
                    pbr = psn()
                    pbi = psn()
                    for t in range(16):
                        mm(pbr[:, t * NS:(t + 1) * NS], Bpad_re[:, t, :], uT[:, t // 4, 0:NS])
                        mm(pbi[:, t * NS:(t + 1) * NS], Bpad_im[:, t, :], uT[:, t // 4, 0:NS])
                    v3 = lambda p_: p_[:, 0:16 * NS].re("p (t s) -> p t s", t=16)
                    arb = AR[:].un(2).bc([128, 16, NS])
                    aib = AI_[:].un(2).bc([128, 16, NS])
                    tt('dve', s5m1[:], v3(pzr), arb, ALU.mult)
                    tt('dve', s5m2[:], v3(pzi), aib, ALU.mult)
                    tt('pool', s5m1[:], s5m1[:], s5m2[:], ALU.subtract)
                    tt('dve', hs_re[:], v3(pbr), s5m1[:], ALU.add)
                    tt('dve', s5m1[:], v3(pzi), arb, ALU.mult)
                    tt('dve', s5m2[:], v3(pzr), aib, ALU.mult)
                    tt('pool', s5m1[:], s5m1[:], s5m2[:], ALU.add)
                    tt('dve', hs_im[:], v3(pbi), s5m1[:], ALU.add)
                    for tg in range(4):
                        cp('act', hre_g[tg][:, :, 0:NS], hs_re[:, tg * 4:(tg + 1) * 4, :])
                        cp('act', him_g[tg][:, :, 0:NS], hs_im[:, tg * 4:(tg + 1) * 4, :])
                    for (src, oname) in ((hs_re, 's_s5re'), (hs_im, 's_s5im')):
                        for tg in range(4):
                            pz = psn()
                            for q in range(4):
                                t = tg * 4 + q
                                tr(pz[0:NS, q * 128:(q + 1) * 128], src[:, t, :], ident[:])
                            hb = hout2[tg % 2]
                            cp('dve', hb[:], pz[0:NS, :])
                            dma('sp', O[oname][:, tg * 512:(tg + 1) * 512], hb[:], d_out)
                    post(16)
                    S.barrier()
                S.barrier()


        def conv_diag(st, name, wsrc, ntile):
            cw = sb(st, [128, 4, ntile], F32, name + 'cw')
            dma('sp', cw[:], wsrc.rearrange("k (t p) -> p k t", p=128), d_init)
            DW = sb(st, [128, ntile, 4, 128], BF16, name)
            for t in range(ntile):
                for k in range(4):
                    ts('dve', DW[:, t, k, :], ident[:], cw[:, k, t:t + 1], ALU.mult)
            return DW

        def softplus_tok(out, pin, brow, n, w, tmp):
            tt('dve', tmp[0:n, 0:w], pin, brow[0:n, 0:w], ALU.add)
            act(tmp[0:n, 0:w], tmp[0:n, 0:w], AF.Exp)
            act(out, tmp[0:n, 0:w], AF.Ln, bias=1.0)

        def cum_stuff(la_tok, n, H, cum_tok, negcum, wl_tok, explast):
            pc = psn()
            mm(pc[0:n, 0:H], triu[0:n, 0:n], la_tok[0:n, 0:H])
            mm(pc[:, 16:16 + H], ones_f[0:n, :], la_tok[0:n, 0:H])
            cp('dve', cum_tok[0:n, 0:H], pc[0:n, 0:H])
            ts('dve', negcum[0:n, 0:H], pc[0:n, 0:H], -1.0, ALU.mult)
            tt('dve', wl_tok[0:n, 0:H], pc[0:n, 16:16 + H], negcum[0:n, 0:H], ALU.add)
            act(wl_tok[0:n, 0:H], wl_tok[0:n, 0:H], AF.Exp)
            act(explast[:, 0:H], pc[:, 16:16 + H], AF.Exp)

        def pass_ssd(pre=None):
            with ExitStack() as ph:
                Win = pre if pre is not None else load_wc(ph, 'win_ssd', I['w_in_cd'], 0, 1544, d_w[4])
                Wout = load_wr(ph, 'wout_ssd', I['w_out_cd'], 0, d_w[5])
                DW = conv_diag(ph, 'dwssd', I['ssd_conv_w'], 8)
                cb = sb(ph, [128, 8], F32, 'cb')
                dma('sp', cb[:], I['ssd_conv_b'].rearrange("(t p) -> p t", p=128), d_init)
                dtb = sb(ph, [128, 8], F32, 'dtb')
                dma('sp', dtb[:], I['ssd_dt_bias'].partition_broadcast(128), d_init)
                arow = sb(ph, [128, 8], F32, 'arow')
                dma('sp', arow[:], I['ssd_a_log'].partition_broadcast(128), d_init)
                act(arow[:], arow[:], AF.Exp)
                ts('dve', arow[:], arow[:], -1.0, ALU.mult)
                Dexp = sb(ph, [128, 4], F32, 'Dexp')
                dv = I['ssd_d'].rearrange("(t g) -> g t", g=2)
                for g2 in range(2):
                    dma('sp', Dexp[g2 * 64:(g2 + 1) * 64, :], dv[g2].partition_broadcast(64), d_init)
                gssd = sb(ph, [128, 4], F32, 'gssd')
                dma('sp', gssd[:], I['ssd_norm'].rearrange("(t p) -> p t", p=128), d_init)
                XB = sb(ph, [128, 8, 131], BF16, 'XB')
                memset('pool', XB[:], 0.0)
                XCs = [sb(ph, [128, 8, 128], BF16, 'XC') for _ in range(2)]
                zss = [sb(ph, [128, 4, 128], BF16, 'zs') for _ in range(2)]
                dt_tok = sb(ph, [128, 8], F32, 'dt_tok')
                la_tok = sb(ph, [128, 8], F32, 'la_tok')
                tmp8 = sb(ph, [128, 8], F32, 'tmp8')
                y2 = sb(ph, [128, 4, 128], F32, 'y2')
                yz = sb(ph, [128, 4, 128], F32, 'yz')
                sqz = sb(ph, [128, 4, 128], BF16, 'sqz')
                omix = sb(ph, [128, 4, 128], BF16, 'omix')
                ctok = sb(ph, [NS, 1024], F32, 'ctok')

                def pre(c):
                    c0, n = CH[c]
                    hTn = hcols(c0, n)
                    pz_ = psn()
                    for t in range(4):
                        proj(Win, t * 128, 128, hTn, pz_[:, t * n:(t + 1) * n])
                    act(zss[c % 2][:, :, 0:n], pz_[:, 0:4 * n].re("p (t n) -> p t n", t=4), AF.Silu)
                    pxs = [psn(), psn()]
                    for t in range(8):
                        proj(Win, 512 + t * 128, 128, hTn, pxs[t // 4][:, (t % 4) * n:(t % 4 + 1) * n])
                    pdt = psn()
                    projA(hTn, Win, 1536, 8, pdt[0:n, 0:8])
                    softplus_tok(dt_tok[0:n, :], pdt[0:n, 0:8], dtb, n, 8, tmp8)
                    tt('dve', la_tok[0:n, :], dt_tok[0:n, :], arow[0:n, :], ALU.mult)
                    return pxs

                def conv(n, rhs_of, p=0):
                    XC = XCs[p]
                    for half in range(2):
                        pc = psn()
                        for j in range(4):
                            t = half * 4 + j
                            for k in range(4):
                                mm(pc[:, j * n:(j + 1) * n], DW[:, t, k, :], rhs_of(t, k), start=(k == 0), stop=(k == 3))
                        for j in range(4):
                            t = half * 4 + j
                            act(XC[:, t, 0:n], pc[:, j * n:(j + 1) * n], AF.Silu, bias=cb[:, t:t + 1])

                def conv_state_out(c0, M, oap):
                    for j in range(2):
                        pcs = psn()
                        projA(hcols(c0, M), Win, 512 + j * 512, 512, pcs[0:M, :])
                        cp('act', ctok[0:M, j * 512:(j + 1) * 512], pcs[0:M, :])
                    dma('sp', oap, ctok[0:M, :], d_out)

                def post(c, yT):
                    c0, n = CH[c]
                    tt('dve', yz[:, :, 0:n], yT, zss[c % 2][:, :, 0:n], ALU.mult)
                    tt('dve', sqz[:, :, 0:n], yz[:, :, 0:n], yz[:, :, 0:n], ALU.mult)
                    pn = psn()
                    for g in range(2):
                        mm(pn[:, g * n:(g + 1) * n], ones_bf[:], sqz[:, 2 * g, 0:n], start=True, stop=False)
                        mm(pn[:, g * n:(g + 1) * n], ones_bf[:], sqz[:, 2 * g + 1, 0:n], start=False, stop=True)
                    act(v_s[:, 0:2 * n], pn[:, 0:2 * n], AF.Ln, scale=1.0 / 256, bias=EPS)
                    act(rstd_s[:, 0:2 * n], v_s[:, 0:2 * n], AF.Exp, scale=-0.5)
                    for g in range(2):
                        tt('dve', yz[:, 2 * g:2 * g + 2, 0:n], yz[:, 2 * g:2 * g + 2, 0:n], rstd_s[:, g * n:(g + 1) * n].un(1).bc([128, 2, n]), ALU.mult)
                    tt('dve', omix[:, :, 0:n], yz[:, :, 0:n], gssd[:].un(2).bc([128, 4, n]), ALU.mult)
                    out_proj(Wout, omix, n, xch(c))

                pst = ExitStack()
                lab = sb(pst, [128, 8, 64], F32, 'lab')
                labn = sb(pst, [128, 8, 128], F32, 'labn')
                csT = sb(pst, [128, 4, 128], F32, 'csT')
                ones128 = sb(pst, [128, 128], F32, 'ones128')
                memset('pool', ones128[:], 1.0)
                cum_tok = sb(pst, [128, 8], F32, 'cum_tok')
                negcum = sb(pst, [128, 8], F32, 'negcum')
                wl_tok = sb(pst, [128, 8], F32, 'wl_tok')
                dw_tok = sb(pst, [128, 8], F32, 'dw_tok')
                xbtoks = [sb(pst, [128, 768], BF16, 'xbtok') for _ in range(2)]
                xdtZs = [sb(pst, [128, 8, 128], BF16, 'xdtZ') for _ in range(2)]
                for z_ in xdtZs:
                    memset('pool', z_[:], 0.0)
                xws = [sb(pst, [128, 512], BF16, 'xw') for _ in range(2)]
                decT = sb(pst, [128, 8, 128], BF16, 'decT')
                MTs = [sb(pst, [128, 8, 128], BF16, 'MT') for _ in range(2)]
                Ecums = [sb(pst, [128, 4, 128], F32, 'Ecum2') for _ in range(2)]
                explasts = [sb(pst, [128, 8], F32, 'explast2') for _ in range(2)]
                y1 = sb(pst, [128, 4, 128], F32, 'y1')
                ST = sb(pst, [128, 512], F32, 'ST')
                STbf = sb(pst, [128, 512], BF16, 'STbf')
                tmpST = sb(pst, [128, 512], F32, 'tmpST')
                memset('pool', ST[:], 0.0)
                memset('pool', STbf[:], 0.0)

                def ssd_front(c):
                    c0, n = CH[c]
                    p = c % 2
                    XC, xbtok, xdtZ, xw, MT, Ecum_, explast_ = XCs[p], xbtoks[p], xdtZs[p], xws[p], MTs[p], Ecums[p], explasts[p]
                    pxs = pre(c)
                    for half in range(2):
                        cp('act' if half == 0 else 'dve', XB[:, half * 4:(half + 1) * 4, 3:3 + n], pxs[half][:, 0:4 * n].re("p (t n) -> p t n", t=4))
                    conv(n, lambda t, k: XB[:, t, k:k + n], p)
                    cp('dve', XB[:, :, 0:3], XB[:, :, n:n + 3])
                    if c == 15:
                        conv_state_out(T - 3, 3, O['p_ssdc'])
                    cp('dve', lab[0:n], la_tok[0:n, :].un(2).bc([n, 8, 64]))
                    pexp = psn()
                    for t in range(4):
                        mm(pexp[:, t * n:(t + 1) * n], lab[0:n, 2 * t:2 * t + 2, :].re("p a b -> p (a b)"), ident[0:n, 0:n])
                    for t in range(4):
                        scan(csT[:, t, 0:n], ones128[:, 0:n], pexp[:, t * n:(t + 1) * n], 0.0)
                    act(Ecum_[:, :, 0:n], csT[:, :, 0:n], AF.Exp)
                    cum_stuff(la_tok, n, 8, cum_tok, negcum, wl_tok, explast_)
                    tt('dve', dw_tok[0:n, :], dt_tok[0:n, :], wl_tok[0:n, :], ALU.mult)
                    pt = psn()
                    ptb = pt[:].bitcast(BF16)
                    for t in range(6):
                        tr(ptb[0:n, t * 128:(t + 1) * 128], XC[:, t, 0:n], identb[:])
                    cp('dve', xbtok[0:n, :], ptb[0:n, 0:768])
                    xsv = xbtok[0:n, 0:512].re("p (t a c) -> p t a c", t=4, a=2)
                    for h2 in range(2):
                        tt('dve', xdtZ[0:n].re("p (t a) c -> p t a c", a=2)[:, :, h2, h2 * 64:(h2 + 1) * 64], xsv[:, :, h2, :],
                           dt_tok[0:n, :].re("p (t a) -> p t a", a=2)[:, :, h2].un(2).bc([n, 4, 64]), ALU.mult)
                    tt('dve', xw[0:n, :].re("p (h c) -> p h c", h=8), xbtok[0:n, 0:512].re("p (h c) -> p h c", h=8),
                       dw_tok[0:n, :].un(2).bc([n, 8, 64]), ALU.mult)
                    pcb = psn()
                    for g in range(2):
                        mm(pcb[0:n, g * n:(g + 1) * n], XC[:, 4 + g, 0:n], XC[:, 6 + g, 0:n])
                    cp('dve', labn[0:n, :, 0:n], la_tok[0:n, :].un(2).bc([n, 8, n]))
                    pdec = [psn(), psn()]
                    for h in range(8):
                        o_ = pdec[h // 4][0:n, (h % 4) * n:(h % 4 + 1) * n]
                        mm(o_, labn[0:n, h, 0:n], triu[0:n, 0:n], start=True, stop=False)
                        mm(o_, ident[0:n, 0:n], negm_f[0:n, 0:n], start=False, stop=True)
                    for h in range(8):
                        act(decT[0:n, h, 0:n], pdec[h // 4][0:n, (h % 4) * n:(h % 4 + 1) * n], AF.Exp, bias=negcum[0:n, h:h + 1])
                    for g in range(2):
                        tt('dve', MT[0:n, 4 * g:4 * g + 4, 0:n], pcb[0:n, g * n:(g + 1) * n].un(1).bc([n, 4, n]), decT[0:n, 4 * g:4 * g + 4, 0:n], ALU.mult)

                def ssd_back(c):
                    c0, n = CH[c]
                    p = c % 2
                    XC, xbtok, xdtZ, xw, MT, Ecum_, explast_ = XCs[p], xbtoks[p], xdtZs[p], xws[p], MTs[p], Ecums[p], explasts[p]
                    py = psn()
                    for t in range(4):
                        for h2 in range(2):
                            h = 2 * t + h2
                            mm(py[:, t * n:(t + 1) * n], xdtZ[0:n, h, :], MT[0:n, h, 0:n], start=(h2 == 0), stop=(h2 == 1))
                    pi_ = psn()
                    for t in range(4):
                        mm(pi_[:, t * n:(t + 1) * n], STbf[:, t * 128:(t + 1) * 128], XC[:, 6 + t // 2, 0:n])
                    tt('dve', y1[:, :, 0:n], pi_[:, 0:4 * n].re("p (t n) -> p t n", t=4), Ecum_[:, :, 0:n], ALU.mult)
                    tt('dve', y2[:, :, 0:n], py[:, 0:4 * n].re("p (t n) -> p t n", t=4), y1[:, :, 0:n], ALU.add)
                    tt('dve', y1[:, :, 0:n], XC[:, 0:4, 0:n], Dexp[:].un(2).bc([128, 4, n]), ALU.mult)
                    tt('dve', y2[:, :, 0:n], y2[:, :, 0:n], y1[:, :, 0:n], ALU.add)
                    pS = psn()
                    for g in range(2):
                        mm(pS[:, g * 256:(g + 1) * 256], xbtok[0:n, 512 + g * 128:512 + (g + 1) * 128], xw[0:n, g * 256:(g + 1) * 256])
                    tt('dve', tmpST[:].re("p (h c) -> p h c", h=8), ST[:].re("p (h c) -> p h c", h=8), explast_[:].un(2).bc([128, 8, 64]), ALU.mult)
                    tt('dve', ST[:], pS[:, :], tmpST[:], ALU.add)
                    cp('act', STbf[:], ST[:])
                    post(c, y2[:, :, 0:n])

                ssd_front(0)
                for c in range(16):
                    if c + 1 < 16:
                        ssd_front(c + 1)
                    ssd_back(c)
                for half in range(1):
                    pz = psn()
                    for t in range(4):
                        tr(pz[:, t * 128:(t + 1) * 128], ST[:, t * 128:(t + 1) * 128], ident[:])
                    cp('dve', tmpST[:], pz[:, :])
                    dma('sp', O['p_ssd'].rearrange("h p n -> (h p) n").rearrange("(t q) n -> q t n", q=128), tmpST[:].re("p (t n) -> p t n", t=4), d_out)
                S.barrier()
                pst.close()

                with ExitStack() as sst:
                    c = 16
                    c0, n = CH[c]
                    HX = sb(sst, [128, 8, 4, NS], BF16, 'HX')
                    scv = sb(sst, [3 * NS, 1024], F32, 'scv')
                    dma('sp', scv[:], I['sssdc'].rearrange("s k f -> (s k) f"), d_st[6])
                    dma('sp', O['s_ssdc'][:, 0:2, :], I['sssdc'][:, 1:3, :], d_out)
                    pxs = pre(c)
                    for half in range(2):
                        cp('act' if half == 0 else 'dve', HX[:, half * 4:(half + 1) * 4, 3, :], pxs[half][:, 0:4 * n].re("p (t n) -> p t n", t=4))
                    for half in range(2):
                        ph_ = psn()
                        for j in range(4):
                            t = half * 4 + j
                            tr(ph_[:, j * 48:(j + 1) * 48], scv[:, t * 128:(t + 1) * 128], ident[0:48, 0:48])
                        cp('dve', HX[:, half * 4:(half + 1) * 4, 0:3, :], ph_[:, 0:4 * 48].re("p (t s k) -> p t k s", t=4, k=3))
                    conv(n, lambda t, k: HX[:, t, k, :], 0)
                    XC = XCs[0]
                    conv_state_out(T, NS, O['s_ssdc'][:, 2, :])
                    da_tok = sb(sst, [NS, 8], F32, 'da_tok')
                    act(da_tok[:], la_tok[0:NS, :], AF.Exp)
                    dab = sb(sst, [NS, 8, 64], F32, 'dab')
                    cp('dve', dab[:], da_tok[:].un(2).bc([NS, 8, 64]))
                    pe_ = psn()
                    for t in range(4):
                        mm(pe_[:, t * NS:(t + 1) * NS], dab[:, 2 * t:2 * t + 2, :].re("p a b -> p (a b)"), ident[0:NS, 0:NS])
                    daT = sb(sst, [128, 4, NS], F32, 'daT')
                    cp('dve', daT[:], pe_[:, 0:4 * NS].re("p (t s) -> p t s", t=4))
                    pt = psn()
                    ptb = pt[:].bitcast(BF16)
                    for t in range(8):
                        tr(ptb[0:NS, t * 128:(t + 1) * 128], XC[:, t, 0:NS], identb[:])
                    xbc_s = sb(sst, [NS, 1024], BF16, 'xbc_s')
                    cp('dve', xbc_s[:], ptb[0:NS, 0:1024])
                    xdt_s = sb(sst, [NS, 512], BF16, 'xdt_s')
                    tt('pool', xdt_s[:].re("p (h c) -> p h c", h=8), xbc_s[:, 0:512].re("p (h c) -> p h c", h=8), dt_tok[0:NS, :].un(2).bc([NS, 8, 64]), ALU.mult)
                    OH = sb(sst, [NS, NS, 128], BF16, 'OH')
                    cp('dve', OH[:], identb[0:NS, 0:NS].un(2).bc([NS, NS, 128]))
                    XdZ = sb(sst, [NS, 4, 512], BF16, 'XdZ')
                    Ssl = [sb(sst, [128, 4, 4, 128], F32, 'Ssl') for _ in range(2)]
                    prod = sb(sst, [128, 4, 128], F32, 'prod')
                    ysT = sb(sst, [128, 4, NS], F32, 'ysT')
                    for gi in range(4):
                        b = gi % 2
                        s0 = gi * 4
                        for t in range(4):
                            dma('sp', Ssl[b][:, t], I['sssd'][s0:s0 + 4, 2 * t:2 * t + 2].rearrange("s a p n -> (a p) s n"), d_st[b])
                        tt('pool', XdZ[:], xdt_s[:].un(1).bc([NS, 4, 512]), identb[0:NS, s0:s0 + 4].un(2).bc([NS, 4, 512]), ALU.mult)
                        pcs = [psn(), psn()]
                        for g in range(2):
                            for si in range(4):
                                mm(pcs[g][:, si * 128:(si + 1) * 128], OH[:, s0 + si, :], xbc_s[:, 768 + g * 128:768 + (g + 1) * 128])
                        for t in range(4):
                            pso = psn()
                            for si in range(4):
                                mm(pso[:, si * 128:(si + 1) * 128], XdZ[:, si, t * 128:(t + 1) * 128], xbc_s[:, 512 + (t // 2) * 128:512 + (t // 2 + 1) * 128])
                            tt('pool', Ssl[b][:, t], Ssl[b][:, t], daT[:, t, s0:s0 + 4].un(2).bc([128, 4, 128]), ALU.mult)
                            tt('dve', Ssl[b][:, t], pso[:, :].re("p (s n) -> p s n", s=4), Ssl[b][:, t], ALU.add)
                            tt('dve', prod[:], pcs[t // 2][:, :].re("p (s n) -> p s n", s=4), Ssl[b][:, t], ALU.mult)
                            red(ysT[:, t, s0:s0 + 4], prod[:])
                        for t in range(4):
                            dma('sp', O['s_ssd'][s0:s0 + 4, 2 * t:2 * t + 2].rearrange("s a p n -> (a p) s n"), Ssl[b][:, t], d_st[2 + b])
                    y1s = sb(sst, [128, 4, NS], F32, 'y1s')
                    tt('pool', y1s[:], XC[:, 0:4, 0:NS], Dexp[:].un(2).bc([128, 4, NS]), ALU.mult)
                    tt('dve', y2[:, :, 0:NS], ysT[:], y1s[:], ALU.add)
                    post(c, y2[:, :, 0:NS])
                    S.barrier()
                S.barrier()

        def pass_gdn():
            with ExitStack() as ph:
                Wt_ = sb(ph, [128, 8, 2056], BF16, 'win_gdn')
                wblocks = [(1536, 2056), (0, 512), (512, 1024), (1024, 1536)]
                Win = WB(Wt_.t, wblocks)
                svw = I['w_in_cd'].rearrange("(k p) n -> p k n", p=128)
                for bi, (lo, hi) in enumerate(wblocks):
                    S.dma('pool', Wt_.t[:, :, lo:hi], svw[:, :, 1544 + lo:1544 + hi], d_w[bi], writes=[Win.units[bi]])
                Wout = load_wr(ph, 'wout_gdn', I['w_out_cd'], 512, d_w[5])
                DW = conv_diag(ph, 'dwgdn', I['gdn_conv_w'], 12)
                arow = sb(ph, [128, 4], F32, 'arowg')
                dma('sp', arow[:], I['gdn_a_log'].partition_broadcast(128), d_init)
                act(arow[:], arow[:], AF.Exp)
                ts('dve', arow[:], arow[:], -1.0, ALU.mult)
                dtb = sb(ph, [128, 4], F32, 'dtbg')
                dma('sp', dtb[:], I['gdn_dt_bias'].partition_broadcast(128), d_init)
                ggdn = sb(ph, [128, 1], F32, 'ggdn')
                dma('sp', ggdn[:], I['gdn_norm'].rearrange("(p o) -> p o", o=1), d_init)
                XQ = sb(ph, [128, 12, 131], BF16, 'XQ')
                memset('pool', XQ[:], 0.0)
                QC = sb(ph, [128, 12, 128], BF16, 'QC')
                gss = [sb(ph, [128, 4, 128], BF16, 'gs') for _ in range(2)]
                qkv_tok = sb(ph, [128, 12, 128], BF16, 'qkv_tok')
                sqk = sb(ph, [128, 8, 128], BF16, 'sqk')
                ssq = sb(ph, [128, 8], F32, 'ssq')
                rs = sb(ph, [128, 8], F32, 'rs')
                qn_tok = sb(ph, [128, 4, 128], BF16, 'qn_tok')
                kn_tok = sb(ph, [128, 4, 128], BF16, 'kn_tok')
                beta_tok = sb(ph, [128, 4], F32, 'beta_tok')
                g_tok = sb(ph, [128, 4], F32, 'g_tok')
                tmp4 = sb(ph, [128, 4], F32, 'tmp4')
                knT = sb(ph, [128, 4, 128], BF16, 'knT')
                qnT = sb(ph, [128, 4, 128], BF16, 'qnT')
                osb = sb(ph, [128, 512], F32, 'osb')
                sqo = sb(ph, [128, 512], BF16, 'sqo')
                t1 = osb
                omix = sb(ph, [128, 4, 128], BF16, 'omix')
                ctok = [sb(ph, [NS, 512], F32, 'ctokg') for _ in range(2)]

                def pre(c):
                    c0, n = CH[c]
                    hTn = hcols(c0, n)
                    pg_ = psn()
                    for t in range(4):
                        proj(Win, 1536 + t * 128, 128, hTn, pg_[:, t * n:(t + 1) * n])
                    act(gss[c % 2][:, :, 0:n], pg_[:, 0:4 * n].re("p (t n) -> p t n", t=4), AF.Silu)
                    pxs = [psn(), psn(), psn()]
                    for t in range(12):
                        proj(Win, t * 128, 128, hTn, pxs[t // 4][:, (t % 4) * n:(t % 4 + 1) * n])
                    pba = psn()
                    projA(hTn, Win, 2048, 8, pba[0:n, 0:8])
                    act(beta_tok[0:n, :], pba[0:n, 0:4], AF.Sigmoid)
                    softplus_tok(g_tok[0:n, :], pba[0:n, 4:8], dtb, n, 4, tmp4)
                    tt('dve', g_tok[0:n, :], g_tok[0:n, :], arow[0:n, :], ALU.mult)
                    return pxs

                def conv(n, rhs_of):
                    for b3 in range(3):
                        pc = psn()
                        for j in range(4):
                            t = b3 * 4 + j
                            for k in range(4):
                                mm(pc[:, j * n:(j + 1) * n], DW[:, t, k, :], rhs_of(t, k), start=(k == 0), stop=(k == 3))
                        act(QC[:, b3 * 4:(b3 + 1) * 4, 0:n], pc[:, 0:4 * n].re("p (t n) -> p t n", t=4), AF.Silu)

                def conv_state_out(c0, M, oap):
                    for j in range(3):
                        pcs = psn()
                        projA(hcols(c0, M), Win, j * 512, 512, pcs[0:M, :])
                        cp('act', ctok[j % 2][0:M, :], pcs[0:M, :])
                        dma('sp', oap[:, j * 512:(j + 1) * 512], ctok[j % 2][0:M, :], d_out)

                def tokprep(n):
                    pts = [psn(), psn()]
                    ptb0 = pts[0][:].bitcast(BF16)
                    ptb1 = pts[1][:].bitcast(BF16)
                    for t in range(8):
                        tr(ptb0[0:n, t * 128:(t + 1) * 128], QC[:, t, 0:n], identb[:])
                    for t in range(4):
                        tr(ptb1[0:n, t * 128:(t + 1) * 128], QC[:, 8 + t, 0:n], identb[:])
                    cp('dve', qkv_tok[0:n, 0:8, :], ptb0[0:n, :].re("p (t f) -> p t f", t=8))
                    cp('act', qkv_tok[0:n, 8:12, :], ptb1[0:n, 0:512].re("p (t f) -> p t f", t=4))
                    if gstop <= 1.2:
                        return
                    for h in range(8):
                        act(sqk[0:n, h, :], qkv_tok[0:n, h, :], AF.Square, accum=ssq[0:n, h:h + 1])
                    if gstop <= 1.3:
                        return
                    act(rs[0:n, :], ssq[0:n, :], AF.Ln, bias=EPS)
                    act(rs[0:n, :], rs[0:n, :], AF.Exp, scale=-0.5)
                    if gstop <= 1.4:
                        return
                    ts('dve', rs[0:n, 0:4], rs[0:n, 0:4], 128.0 ** -0.5, ALU.mult)
                    tt('dve', qn_tok[0:n], qkv_tok[0:n, 0:4, :], rs[0:n, 0:4].un(2).bc([n, 4, 128]), ALU.mult)
                    tt('dve', kn_tok[0:n], qkv_tok[0:n, 4:8, :], rs[0:n, 4:8].un(2).bc([n, 4, 128]), ALU.mult)
                    if gstop <= 1.6:
                        return
                    pt2a = psn()
                    pt2b = psn()
                    for h in range(4):
                        mm(pt2a[:, h * n:(h + 1) * n], kn_tok[0:n, h, :], identb[0:n, 0:n])
                        mm(pt2b[:, h * n:(h + 1) * n], qn_tok[0:n, h, :], identb[0:n, 0:n])
                    cp('dve', knT[:, :, 0:n], pt2a[:, 0:4 * n].re("p (h n) -> p h n", h=4))
                    cp('act', qnT[:, :, 0:n], pt2b[:, 0:4 * n].re("p (h n) -> p h n", h=4))

                pst = ExitStack()
                cum_tok = sb(pst, [128, 4], F32, 'cum_tokg')
                negcum = sb(pst, [128, 4], F32, 'negcumg')
                wl_tok = sb(pst, [128, 4], F32, 'wl_tokg')
                gam_tok = sb(pst, [128, 4], F32, 'gam_tok')
                bg_tok = sb(pst, [128, 4], F32, 'bg_tok')
                explast = sb(pst, [128, 4], F32, 'explastg')
                gbn = sb(pst, [128, 4, 128], F32, 'gbn')
                Xf = Tl2(sb(pst, [128, 4, 256], F32, 'Xf').t)
                qg_tok = sb(pst, [128, 4, 128], BF16, 'qg_tok')
                kw_tok = sb(pst, [128, 4, 128], BF16, 'kw_tok')
                qgT = sb(pst, [128, 4, 128], BF16, 'qgT')
                decA = sb(pst, [128, 4, 128], BF16, 'decA')
                Pm = Tl2(sb(pst, [128, 4, 128], F32, 'Pm').t)
                PTm = Tl2(sb(pst, [128, 4, 128], F32, 'PTm').t)
                attqT = sb(pst, [128, 4, 128], BF16, 'attqT')
                WkT = sb(pst, [128, 4, 128], BF16, 'WkT')
                u_bf = sb(pst, [128, 4, 128], BF16, 'u_bf')
                Sg = sb(pst, [128, 4, 128], F32, 'Sgd')
                Sbf = sb(pst, [128, 4, 128], BF16, 'Sgdbf')
                memset('pool', Sg[:], 0.0)
                memset('pool', Sbf[:], 0.0)
                def front(c):
                    c0, n = CH[c]
                    pxs = pre(c)
                    for b3 in range(3):
                        cp(('act', 'dve', 'act')[b3], XQ[:, b3 * 4:(b3 + 1) * 4, 3:3 + n], pxs[b3][:, 0:4 * n].re("p (t n) -> p t n", t=4))
                    conv(n, lambda t, k: XQ[:, t, k:k + n])
                    cp('dve', XQ[:, :, 0:3], XQ[:, :, n:n + 3])
                    if c == 15:
                        conv_state_out(T - 3, 3, O['p_gdnc'])

                front(0)
                tokprep(128)
                psrot['set'] = list(range(7))
                po = psb[7]

                def post(c):
                    n = 128
                    pnorm_part(po[:, 0:4 * n], n, ggdn[:, 0:1], gss[c % 2][:, :, 0:n], omix[:, :, 0:n], osb, sqo, t1)
                    out_proj(Wout, omix, n, xch(c))
                    norm_chunk(c, gmlp[:, 1, :], 'dve')

                for c in range(16):
                    c0, n = CH[c]
                    if gstop <= 2:
                        break
                    cum_stuff(g_tok, n, 4, cum_tok, negcum, wl_tok, explast)
                    act(gam_tok[0:n, :], cum_tok[0:n, :], AF.Exp)
                    tt('dve', bg_tok[0:n, :], beta_tok[0:n, :], gam_tok[0:n, :], ALU.mult)
                    tt('dve', Xf[0:n, :, 0:128], qkv_tok[0:n, 8:12, :], beta_tok[0:n, :].un(2).bc([n, 4, 128]), ALU.mult)
                    tt('dve', Xf[0:n, :, 128:256], kn_tok[0:n], bg_tok[0:n, :].un(2).bc([n, 4, 128]), ALU.mult)
                    tt('dve', qg_tok[0:n], qn_tok[0:n], gam_tok[0:n, :].un(2).bc([n, 4, 128]), ALU.mult)
                    tt('dve', kw_tok[0:n], kn_tok[0:n], wl_tok[0:n, :].un(2).bc([n, 4, 128]), ALU.mult)
                    pt3 = psn()
                    for h in range(4):
                        mm(pt3[:, h * n:(h + 1) * n], qg_tok[0:n, h, :], identb[0:n, 0:n])
                    cp('dve', qgT[:, :, 0:n], pt3[:, 0:4 * n].re("p (h n) -> p h n", h=4))
                    if c == 0:
                        ddump(qkv_tok[0:n].re("p t f -> p (t f)"), 0, 1536)
                        ddump(qn_tok[0:n].re("p t f -> p (t f)"), 1536, 512)
                        ddump(kn_tok[0:n].re("p t f -> p (t f)"), 2048, 512)
                        ddump(beta_tok[0:n, :], 2560, 4)
                        ddump(g_tok[0:n, :], 2564, 4)
                        ddump(cum_tok[0:n, :], 2568, 4)
                        ddump(Xf[0:n].re("p t f -> p (t f)"), 6100, 1024)
                    if gstop <= 3:
                        break
                    pkk = psn()
                    pqk = psn()
                    for h in range(4):
                        mm(pkk[0:n, h * n:(h + 1) * n], knT[:, h, 0:n], knT[:, h, 0:n])
                        mm(pqk[0:n, h * n:(h + 1) * n], knT[:, h, 0:n], qnT[:, h, 0:n])
                    cp('dve', gbn[0:n, :, 0:n], g_tok[0:n, :].un(2).bc([n, 4, n]))
                    pr1 = psn()
                    pr2 = psn()
                    for h in range(4):
                        mm(pr1[0:n, h * n:(h + 1) * n], gbn[0:n, h, 0:n], triu[0:n, 0:n], start=True, stop=False)
                        mm(pr1[0:n, h * n:(h + 1) * n], ident[0:n, 0:n], posm_f[0:n, 0:n], start=False, stop=True)
                        mm(pr2[0:n, h * n:(h + 1) * n], gbn[0:n, h, 0:n], triu[0:n, 0:n], start=True, stop=False)
                        mm(pr2[0:n, h * n:(h + 1) * n], ident[0:n, 0:n], negm_f[0:n, 0:n], start=False, stop=True)
                    for h in range(4):
                        act(decA[0:n, h, 0:n], pr1[0:n, h * n:(h + 1) * n], AF.Exp, scale=-1.0, bias=cum_tok[0:n, h:h + 1])
                    tt('dve', decA[0:n, :, 0:n], pkk[0:n, 0:4 * n].re("p (h n) -> p h n", h=4), decA[0:n, :, 0:n], ALU.mult)
                    tt('dve', Pm[0:n, :, 0:n], decA[0:n, :, 0:n], beta_tok[0:n, :].un(2).bc([n, 4, n]), ALU.mult)
                    for h in range(4):
                        act(decA[0:n, h, 0:n], pr2[0:n, h * n:(h + 1) * n], AF.Exp, bias=negcum[0:n, h:h + 1])
                    tt('dve', attqT[0:n, :, 0:n], pqk[0:n, 0:4 * n].re("p (h n) -> p h n", h=4), decA[0:n, :, 0:n], ALU.mult)
                    pt4 = psn()
                    for h in range(4):
                        mm(pt4[0:n, h * n:(h + 1) * n], Pm[0:n, h, 0:n], ident[0:n, 0:n])
                    cp('act', PTm[0:n, :, 0:n], pt4[0:n, 0:4 * n].re("p (h n) -> p h n", h=4))
                    if c == 0:
                        ddump(Pm[0:n].re("p t f -> p (t f)"), 2600, 512)
                        ddump(attqT[0:n].re("p t f -> p (t f)"), 4300, 512)
                    if gstop <= 4:
                        break
                    if c > 0:
                        post(c - 1)
                    if c + 1 < 16:
                        front(c + 1)
                    nlev = 7
                    Xh = [Xf.half(hp, 2) for hp in range(2)]
                    Ph = [Pm.half(hp, 2) for hp in range(2)]
                    PTh = [PTm.half(hp, 2) for hp in range(2)]
                    for l in range(nlev):
                        pXs, pPs = [], []
                        for hp in range(2):
                            pX = psn()
                            for j in range(2):
                                mm(pX[0:n, j * 256:(j + 1) * 256], PTh[hp][0:n, j, 0:n], Xh[hp][0:n, j, :])
                            pXs.append(pX)
                            if l + 1 < nlev:
                                pP = psn()
                                for j in range(2):
                                    mm(pP[0:n, j * n:(j + 1) * n], PTh[hp][0:n, j, 0:n], Ph[hp][0:n, j, 0:n])
                                for j in range(2):
                                    mm(pP[0:n, 256 + j * n:256 + (j + 1) * n], Ph[hp][0:n, j, 0:n], PTh[hp][0:n, j, 0:n])
                                pPs.append(pP)
                        for hp in range(2):
                            tt('dve', Xh[hp][0:n], Xh[hp][0:n], pXs[hp][0:n, :].re("p (h f) -> p h f", h=2), ALU.subtract if l == 0 else ALU.add)
                            if l + 1 < nlev:
                                cp('act', Ph[hp][0:n, :, 0:n], pPs[hp][0:n, 0:2 * n].re("p (h n) -> p h n", h=2))
                                cp('act', PTh[hp][0:n, :, 0:n], pPs[hp][0:n, 256:256 + 2 * n].re("p (h n) -> p h n", h=2))
                    if c == 0:
                        ddump(Xf[0:n].re("p t f -> p (t f)"), 3200, 1024)
                    if gstop <= 5:
                        break
                    if c + 1 < 16:
                        tokprep(128)
                    pt5 = psn()
                    for h in range(4):
                        mm(pt5[:, h * n:(h + 1) * n], Xf[0:n, h, 128:256], ident[0:n, 0:n])
                    cp('dve', WkT[:, :, 0:n], pt5[:, 0:4 * n].re("p (h n) -> p h n", h=4))
                    pws = psn()
                    for h in range(4):
                        mm(pws[0:n, h * 128:(h + 1) * 128], WkT[:, h, 0:n], Sbf[:, h, :])
                    tt('dve', u_bf[0:n], Xf[0:n, :, 0:128], pws[0:n, :].re("p (h v) -> p h v", h=4), ALU.subtract)
                    for h in range(4):
                        mm(po[:, h * n:(h + 1) * n], u_bf[0:n, h, :], attqT[0:n, h, 0:n], start=True, stop=False)
                        mm(po[:, h * n:(h + 1) * n], Sbf[:, h, :], qgT[:, h, 0:n], start=False, stop=True)
                    pS = psn()
                    for h in range(4):
                        mm(pS[:, h * 128:(h + 1) * 128], kw_tok[0:n, h, :], u_bf[0:n, h, :])
                    tt('dve', Sg[:], Sg[:], explast[:].un(2).bc([128, 4, 128]), ALU.mult)
                    tt('dve', Sg[:], pS[:, :].re("p (h v) -> p h v", h=4), Sg[:], ALU.add)
                    cp('act', Sbf[:], Sg[:])
                    if c == 15:
                        dma('sp', O['p_gdn'].rearrange("h k v -> k h v"), Sg[:], d_out)
                    if c == 0:
                        ddump(u_bf[0:n].re("p t f -> p (t f)"), 4900, 512)
                        cp('act', osb[:, 0:4 * n], po[:, 0:4 * n])
                        ddump(osb[:, 0:4 * n], 5500, 512)
                post(15)
                psrot['set'] = list(range(8))
                S.barrier()
                pst.close()
                if gdn == 'prompt':
                    return

                with ExitStack() as sst:
                    c = 16
                    c0, n = CH[c]
                    HX = sb(sst, [128, 12, 4, NS], BF16, 'HXg')
                    scv = [sb(sst, [3 * NS, 512], F32, 'scvg') for _ in range(2)]
                    dma('sp', O['s_gdnc'][:, 0:2, :], I['sgdnc'][:, 1:3, :], d_out)
                    pxs = pre(c)
                    for b3 in range(3):
                        cp(('act', 'dve', 'act')[b3], HX[:, b3 * 4:(b3 + 1) * 4, 3, :], pxs[b3][:, 0:4 * n].re("p (t n) -> p t n", t=4))
                    for b3 in range(3):
                        ph_ = psn()
                        dma('sp', scv[b3 % 2][:], I['sgdnc'].rearrange("s k f -> (s k) f")[:, b3 * 512:(b3 + 1) * 512], d_st[6 + b3 % 2])
                        for j in range(4):
                            tr(ph_[:, j * 48:(j + 1) * 48], scv[b3 % 2][:, j * 128:(j + 1) * 128], ident[0:48, 0:48])
                        cp('dve', HX[:, b3 * 4:(b3 + 1) * 4, 0:3, :], ph_[:, 0:4 * 48].re("p (t s k) -> p t k s", t=4, k=3))
                    conv(n, lambda t, k: HX[:, t, k, :])
                    conv_state_out(T, NS, O['s_gdnc'][:, 2, :])
                    tokprep(n)
                    gam_s = sb(sst, [NS, 4], F32, 'gam_s')
                    act(gam_s[:], g_tok[0:NS, :], AF.Exp)
                    Dg = sb(sst, [NS, 2, 4, NS], F32, 'Dg')
                    idf16 = ident[0:NS, 0:NS].un(1).bc([NS, 4, NS])
                    tt('dve', Dg[:, 0], idf16, beta_tok[0:NS, :].un(2).bc([NS, 4, NS]), ALU.mult)
                    tt('dve', Dg[:, 1], idf16, gam_s[:].un(2).bc([NS, 4, NS]), ALU.mult)
                    pbc = psn()
                    mm(pbc[:, 0:128], ones_f[0:NS, :], Dg[:].re("p a h s -> p (a h s)"))
                    bgb = sb(sst, [128, 2, 4, NS], F32, 'bgb')
                    cp('dve', bgb[:], pbc[:, 0:128].re("p (a h s) -> p a h s", a=2, h=4))
                    knTf = sb(sst, [128, 4, NS], F32, 'knTf')
                    qnTf = sb(sst, [128, 4, NS], F32, 'qnTf')
                    cp('dve', knTf[:], knT[:, :, 0:NS])
                    cp('dve', qnTf[:], qnT[:, :, 0:NS])
                    GS = 2
                    KdG = sb(sst, [NS, GS, 512], BF16, 'KdG')
                    Ssl = [sb(sst, [128, GS, 4, 128], F32, 'Sslg') for _ in range(2)]
                    uT_s = sb(sst, [128, 4, NS], F32, 'uT_s')
                    t2_s = sb(sst, [128, 4, NS], F32, 't2_s')
                    u_tok = sb(sst, [NS, 4, 128], BF16, 'u_tok')
                    uTb = sb(sst, [128, 4, NS], BF16, 'uTb')
                    po = psb[7]
                    pks = psb[6]
                    psrot['set'] = list(range(6))
                    for gi in range(NS // GS):
                        b = gi % 2
                        s0 = gi * GS
                        for si in range(GS):
                            dma('sp', Ssl[b][:, si], I['sgdn'][s0 + si].rearrange("h k v -> k h v"), d_st[b])
                        for h in range(4):
                            for si in range(GS):
                                col = h * NS + s0 + si
                                mm(pks[:, col:col + 1], Ssl[b][:, si, h, :], knTf[:, h, s0 + si:s0 + si + 1])
                    tt('dve', t2_s[:], pks[:, 0:4 * NS].re("p (h s) -> p h s", h=4), bgb[:, 1], ALU.mult)
                    tt('dve', t2_s[:], QC[:, 8:12, 0:NS], t2_s[:], ALU.subtract)
                    tt('dve', uT_s[:], t2_s[:], bgb[:, 0], ALU.mult)
                    cp('dve', uTb[:], uT_s[:])
                    ptu = psn()
                    ptub = ptu[:].bitcast(BF16)
                    for h in range(4):
                        tr(ptub[0:NS, h * 128:(h + 1) * 128], uTb[:, h, :], identb[:])
                    cp('dve', u_tok[:], ptub[0:NS, 0:512].re("p (h v) -> p h v", h=4))
                    for gi in range(NS // GS):
                        b = gi % 2
                        s0 = gi * GS
                        for si in range(GS):
                            dma('sp', Ssl[b][:, si], I['sgdn'][s0 + si].rearrange("h k v -> k h v"), d_st[b])
                        tt('pool', KdG[:], kn_tok[0:NS].re("p h k -> p (h k)").un(1).bc([NS, GS, 512]), identb[0:NS, s0:s0 + GS].un(2).bc([NS, GS, 512]), ALU.mult)
                        for si in range(GS):
                            pso = psn()
                            for h in range(4):
                                mm(pso[:, h * 128:(h + 1) * 128], KdG[:, si, h * 128:(h + 1) * 128], u_tok[:, h, :])
                            tt('pool', Ssl[b][:, si], Ssl[b][:, si], bgb[:, 1, :, s0 + si:s0 + si + 1].bc([128, 4, 128]), ALU.mult)
                            tt('dve', Ssl[b][:, si], pso[:, :].re("p (h v) -> p h v", h=4), Ssl[b][:, si], ALU.add)
                            for h in range(4):
                                col = h * NS + s0 + si
                                mm(po[:, col:col + 1], Ssl[b][:, si, h, :], qnTf[:, h, s0 + si:s0 + si + 1])
                            dma('sp', O['s_gdn'][s0 + si].rearrange("h k v -> k h v"), Ssl[b][:, si], d_st[2 + b])
                    pnorm_part(po[:, 0:4 * n], n, ggdn[:, 0:1], gss[0][:, :, 0:n], omix[:, :, 0:n], osb, sqo, t1)
                    out_proj(Wout, omix, n, xch(c))
                    norm_chunk(c, gmlp[:, 1, :], 'dve')
                    psrot['set'] = list(range(8))
                    S.barrier()
                S.barrier()

        FT = {}

        def final_alloc(ph):
            FT['yT'] = sb(ph, [128, 8, 128], F32, 'yT')
            FT['ytok'] = [sb(ph, [128, D], F32, 'ytok') for _ in range(2)]
            FT['rs2'] = sb(ph, [128, 128], F32, 'rs2')

        def final_chunk(c):
            yT, ytok, rs2 = FT['yT'], FT['ytok'], FT['rs2']
            c0, n = CH[c]
            xc = xch(c)
            sq = sq_s[:, :, 0:n]
            tt('dve', sq, xc, xc, ALU.mult)
            pb = psn()
            for k in range(8):
                mm(pb[:, 0:n], ones_bf[:], sq_s[:, k, 0:n], start=(k == 0), stop=(k == 7))
            act(v_s[:, 0:n], pb[:, 0:n], AF.Ln, scale=1.0 / D, bias=EPS)
            act(rs2[:, 0:n], v_s[:, 0:n], AF.Exp, scale=-0.5)
            tt('dve', ntmp[:, :, 0:n], xc, rs2[:, 0:n].un(1).bc([128, 8, n]), ALU.mult)
            tt('dve', yT[:, :, 0:n], ntmp[:, :, 0:n], gfin[:].un(2).bc([128, 8, n]), ALU.mult)
            b = c % 2
            for half in range(2):
                pz = psn()
                for j in range(4):
                    k = half * 4 + j
                    tr(pz[0:n, j * 128:(j + 1) * 128], yT[:, k, 0:n], ident[:])
                cp('act' if half == 0 else 'dve', ytok[b][0:n, half * 512:(half + 1) * 512], pz[0:n, :])
            if c < 16:
                dma('sp', O['yp'][c0:c0 + n, :], ytok[b][0:n, :], d_out)
            else:
                dma('sp', O['ys'], ytok[b][0:n, :], d_out)

        dpre = [S.dsem('pre%d' % i) for i in range(3)]
        with ExitStack() as ws5:
            Win5 = load_wc(ws5, 'win_s5', I['w_in_ab'], 1552, 512, dpre[1])
            Wout5 = load_wr(ws5, 'wout_s5', I['w_out_ab'], 512, dpre[1])
            Wglu5 = sb(ws5, [128, 4, 512], BF16, 'wglu')
            with ExitStack() as wgla:
                Wing = load_wc(wgla, 'win_gla', I['w_in_ab'], 0, 1552, dpre[0])
                Woutg = load_wr(wgla, 'wout_gla', I['w_out_ab'], 0, dpre[0])
                Wgateg = sb(wgla, [16, 256], BF16, 'wgate')
                dma('pool', Wgateg[:], I['w_gla_gate'], dpre[0])
                dma('pool', Wglu5[:], I['w_s5_glu'].rearrange("(k p) n -> p k n", p=128), dpre[1])
                phase0()
                pass_gla((Wing, Woutg, Wgateg))
            pass_s5((Win5, Wout5, Wglu5))
        dump(0)
        if nlayers > 1:
            with ExitStack() as wssd:
                Winssd = load_wc(wssd, 'win_ssd', I['w_in_cd'], 0, 1544, dpre[2])
                mlp_phase(0, do_norm=False, cg_hook=lambda c: norm_chunk(c, gmix[:, 1, :], 'dve'))
                dump(1)
                pass_ssd(Winssd)
            dump(2)
            pass_gdn()
            dump(3)
            mlp_phase(1, do_norm=False, cg_hook=final_chunk, st_hook=final_alloc)
            dump(4)
        else:
            mlp_phase(0, do_norm=False, cg_hook=final_chunk, st_hook=final_alloc)
            dump(1)
        S.barrier()
        print("program: ops=%d waits=%d counts=%s" % (S.nops, S.nwaits, S.cnt))
    return nc


_NC_CACHE = {}


def _prep_inputs(inputs):
    f = lambda a: np.ascontiguousarray(np.asarray(a, dtype=np.float32))
    shared = {
        'norm_mix': f(inputs['norm_mix']), 'norm_mlp': f(inputs['norm_mlp']), 'norm_final': f(inputs['norm_final']),
        'w_up': f(inputs['w_up']), 'w_down': f(inputs['w_down']),
        'w_in_ab': f(inputs['w_in_ab'][0]), 'w_out_ab': f(inputs['w_out_ab'][0]), 'w_gla_gate': f(inputs['w_gla_gate'][0]),
        'b_gla_gate': f(inputs['b_gla_gate'][0]), 'g_gla_norm': f(inputs['g_gla_norm'][0]),
        's5_lam_re': f(inputs['s5_lam_re'][0]).reshape(2048), 's5_lam_im': f(inputs['s5_lam_im'][0]).reshape(2048),
        's5_b_re': f(inputs['s5_b_re'][0]).reshape(2048, 16), 's5_b_im': f(inputs['s5_b_im'][0]).reshape(2048, 16),
        's5_c_re': f(inputs['s5_c_re'][0]).reshape(512, 64), 's5_c_im': f(inputs['s5_c_im'][0]).reshape(512, 64),
        's5_d': f(inputs['s5_d'][0]).reshape(512), 's5_log_dt': f(inputs['s5_log_dt'][0]),
        'w_s5_glu': f(inputs['w_s5_glu'][0]), 'b_s5_glu': f(inputs['b_s5_glu'][0]),
        'w_in_cd': f(inputs['w_in_cd'][0]), 'w_out_cd': f(inputs['w_out_cd'][0]),
        'ssd_conv_w': f(inputs['ssd_conv_w'][0]), 'ssd_conv_b': f(inputs['ssd_conv_b'][0]), 'ssd_dt_bias': f(inputs['ssd_dt_bias'][0]),
        'ssd_a_log': f(inputs['ssd_a_log'][0]), 'ssd_d': f(inputs['ssd_d'][0]), 'ssd_norm': f(inputs['ssd_norm'][0]),
        'gdn_conv_w': f(inputs['gdn_conv_w'][0]), 'gdn_a_log': f(inputs['gdn_a_log'][0]), 'gdn_dt_bias': f(inputs['gdn_dt_bias'][0]),
        'gdn_norm': f(inputs['gdn_norm'][0]),
    }
    maps = []
    for c in range(8):
        s = slice(c * NS, (c + 1) * NS)
        m = dict(shared)
        m['xp'] = f(inputs['x_prompt'][c])
        m['xs'] = f(inputs['x_sample'][s, 0])
        m['sgla'] = f(inputs['state_gla'][0, s])
        m['ss5re'] = f(inputs['state_s5_re'][0, s]).reshape(NS, 2048)
        m['ss5im'] = f(inputs['state_s5_im'][0, s]).reshape(NS, 2048)
        m['sssd'] = f(inputs['state_ssd'][0, s])
        m['sssdc'] = f(inputs['state_ssd_conv'][0, s])
        m['sgdn'] = f(inputs['state_gdn'][0, s])
        m['sgdnc'] = f(inputs['state_gdn_conv'][0, s])
        maps.append(m)
    return maps


def _gather(res):
    R = res.results
    cat = lambda k: np.concatenate([np.asarray(r[k]) for r in R], axis=0)
    stk = lambda k: np.stack([np.asarray(r[k]) for r in R], axis=0)
    y_prompt = stk('yp')
    y_sample = cat('ys').reshape(128, 1, D)
    p_gla = stk('p_gla')[None]
    p_s5re = stk('p_s5re').reshape(8, 32, 64)[None]
    p_s5im = stk('p_s5im').reshape(8, 32, 64)[None]
    p_ssd = stk('p_ssd')[None]
    p_ssdc = stk('p_ssdc')[None]
    p_gdn = stk('p_gdn')[None]
    p_gdnc = stk('p_gdnc')[None]
    s_gla = cat('s_gla')[None]
    s_s5re = cat('s_s5re').reshape(128, 32, 64)[None]
    s_s5im = cat('s_s5im').reshape(128, 32, 64)[None]
    s_ssd = cat('s_ssd')[None]
    s_ssdc = cat('s_ssdc')[None]
    s_gdn = cat('s_gdn')[None]
    s_gdnc = cat('s_gdnc')[None]
    outs = (y_prompt, y_sample, p_gla, p_s5re, p_s5im, p_ssd, p_ssdc, p_gdn, p_gdnc,
            s_gla, s_s5re, s_s5im, s_ssd, s_ssdc, s_gdn, s_gdnc)
    return tuple(np.ascontiguousarray(o, dtype=np.float32) for o in outs)


def kernel(**inputs):
    if 'nc' not in _NC_CACHE:
        _NC_CACHE['nc'] = build_program()
    nc = _NC_CACHE['nc']
    maps = _prep_inputs(inputs)
    res = run_bass_kernel_spmd(nc, maps, core_ids=list(range(8)))
    return _gather(res)
```

```python
import math
from contextlib import ExitStack

import numpy as np
import concourse.bass as bass
import concourse.mybir as mybir
from concourse.bass_utils import run_bass_kernel_spmd

F32 = mybir.dt.float32
BF16 = mybir.dt.bfloat16
I32 = mybir.dt.int32
ALU = mybir.AluOpType
AF = mybir.ActivationFunctionType

EPS = 1e-6
T = 2048
NS = 16
NT = T + NS
D = 1024
IN_AB = 2064
IN_CD = 3600
NEG = -30000.0


class Unit:
    __slots__ = ('w', 'rs')

    def __init__(self):
        self.w = None
        self.rs = {}


class DSem:
    def __init__(self, h, key):
        self.h = h
        self.key = key
        self.count = 0
        self.batch = None


class Sched:
    def __init__(self, nc, es):
        self.nc = nc
        self.es = es
        self.eng = {'pe': nc.tensor, 'dve': nc.vector, 'act': nc.scalar, 'pool': nc.gpsimd, 'sp': nc.sync}
        self.semh = {}
        for k in self.eng:
            self.semh[k] = es.enter_context(nc.semaphore('s_' + k))
        self.cnt = {k: 0 for k in self.eng}
        self.known = {k: {} for k in self.eng}
        self.dsems = []
        self.nwaits = 0
        self.nops = 0

    def dsem(self, name=None):
        key = 'd%d' % len(self.dsems)
        h = self.es.enter_context(self.nc.semaphore(name or key))
        self.semh[key] = h
        ds = DSem(h, key)
        self.dsems.append(ds)
        return ds

    def _val(self, ev):
        vr = ev[1]
        if vr[0] is None:
            ds = vr[1]
            vr[0] = ds.count
            ds.batch = None
        return vr[0]

    def _need(self, e, ev, needs, skip_same=False):
        if ev is None:
            return
        k = ev[0]
        if skip_same and k == e:
            return
        v = self._val(ev)
        if self.known[e].get(k, 0) >= v:
            return
        if k in needs and needs[k][0] >= v:
            return
        needs[k] = (v, ev[2])

    def _emit_waits(self, e, needs):
        kn = self.known[e]
        for k, (v, snap) in needs.items():
            if kn.get(k, 0) >= v:
                continue
            self.eng[e].wait_ge(self.semh[k], v)
            self.nwaits += 1
            kn[k] = v
            if snap:
                for k2, v2 in snap.items():
                    if kn.get(k2, 0) < v2:
                        kn[k2] = v2

    def _deps(self, e, reads, writes, isdma=False):
        needs = {}
        for u in reads:
            self._need(e, u.w, needs)
        skip = (e == 'pe') and not isdma
        for u in writes:
            self._need(e, u.w, needs, skip_same=skip)
            for ev in u.rs.values():
                self._need(e, ev, needs, skip_same=skip)
        self._emit_waits(e, needs)

    def op(self, e, fn, reads=(), writes=(), inc=True):
        self._deps(e, reads, writes)
        ins = fn()
        self.nops += 1
        if inc:
            self.cnt[e] += 1
            ins.then_inc(self.semh[e], 1)
            val = self.cnt[e]
        else:
            val = self.cnt[e] + 1
        ev = (e, [val], dict(self.known[e]) if inc else None)
        for u in writes:
            u.w = ev
            u.rs = {}
        for u in reads:
            old = u.rs.get(e)
            if old is None or old[1][0] <= val:
                u.rs[e] = ev
        return ins

    def dma(self, q, out, in_, ds, reads=(), writes=()):
        self._deps(q, reads, writes, isdma=True)
        kn = self.known[q]
        if ds.batch is None:
            if ds.count > 0 and kn.get(ds.key, 0) < ds.count:
                self.eng[q].wait_ge(ds.h, ds.count)
                self.nwaits += 1
                kn[ds.key] = ds.count
            ds.batch = [None, ds]
        ins = self.eng[q].dma_start(out=out, in_=in_)
        ins.then_inc(ds.h, 16)
        ds.count += 16
        ev = (ds.key, ds.batch, dict(kn))
        for u in writes:
            u.w = ev
            u.rs = {}
        for u in reads:
            u.rs[ds.key] = ev
        return ins

    def barrier(self):
        for ds in self.dsems:
            if ds.batch is not None:
                ds.batch[0] = ds.count
                ds.batch = None
        for e in self.eng:
            needs = {}
            for k in self.eng:
                if k != e and self.cnt[k] > 0:
                    needs[k] = (self.cnt[k], None)
            for ds in self.dsems:
                if ds.count > 0:
                    needs[ds.key] = (ds.count, None)
            self._emit_waits(e, needs)


class V:
    __slots__ = ('ap', 'units')

    def __init__(self, ap, units):
        self.ap = ap
        self.units = units

    def __getitem__(self, idx):
        return V(self.ap[idx], self.units)

    def re(self, pat, **kw):
        return V(self.ap.rearrange(pat, **kw), self.units)

    def bc(self, shape):
        return V(self.ap.to_broadcast(list(shape)), self.units)

    def un(self, d):
        return V(self.ap.unsqueeze(d), self.units)

    def bitcast(self, dt):
        return V(self.ap.bitcast(dt), self.units)


class Tl:
    def __init__(self, t):
        self.t = t
        self.u = Unit()

    def __getitem__(self, idx):
        return V(self.t[idx], (self.u,))

    def sub(self, unit, idx):
        return V(self.t[idx], (unit,))


class Tl2(Tl):
    def __init__(self, t):
        Tl.__init__(self, t)
        self.us = (Unit(), Unit())

    def __getitem__(self, idx):
        return V(self.t[idx], self.us)

    def half(self, hp, k):
        return V(self.t[:, k * hp:k * (hp + 1)], (self.us[hp],))


class WB:
    def __init__(self, t, blocks):
        self.t = t
        self.blocks = blocks
        self.units = [Unit() for _ in blocks]

    def __getitem__(self, idx):
        c0 = idx[2].start
        for (lo, hi), u in zip(self.blocks, self.units):
            if lo <= c0 < hi:
                return V(self.t[idx], (u,))
        raise KeyError(c0)


def _u(*vs):
    out = []
    for v in vs:
        if isinstance(v, V):
            for u in v.units:
                if u not in out:
                    out.append(u)
    return out


def _ap(v):
    return v.ap if isinstance(v, V) else v


def build_program(debug=False, nlayers=2, do_mlp=True, gdn='full', mlp1=True, gstop=99):
    nc = bass.Bass("TRN2", target_bir_lowering=False)

    def din(name, shape):
        return nc.dram_tensor(name, list(shape), F32, kind="ExternalInput").ap()

    def dout(name, shape):
        return nc.dram_tensor(name, list(shape), F32, kind="ExternalOutput").ap()

    I = {}
    for name, shape in [
        ('xp', (T, D)), ('xs', (NS, D)), ('sgla', (NS, 4, 64, 128)), ('ss5re', (NS, 2048)), ('ss5im', (NS, 2048)),
        ('sssd', (NS, 8, 64, 128)), ('sssdc', (NS, 3, 1024)), ('sgdn', (NS, 4, 128, 128)), ('sgdnc', (NS, 3, 1536)),
        ('norm_mix', (2, D)), ('norm_mlp', (2, D)), ('norm_final', (D,)), ('w_up', (2, D, 4096)), ('w_down', (2, 4096, D)),
        ('w_in_ab', (D, IN_AB)), ('w_out_ab', (D, D)), ('w_gla_gate', (16, 256)), ('b_gla_gate', (256,)), ('g_gla_norm', (128,)),
        ('s5_lam_re', (2048,)), ('s5_lam_im', (2048,)), ('s5_b_re', (2048, 16)), ('s5_b_im', (2048, 16)),
        ('s5_c_re', (512, 64)), ('s5_c_im', (512, 64)), ('s5_d', (512,)), ('s5_log_dt', (32,)),
        ('w_s5_glu', (512, 512)), ('b_s5_glu', (512,)), ('w_in_cd', (D, IN_CD)), ('w_out_cd', (D, D)),
        ('ssd_conv_w', (4, 1024)), ('ssd_conv_b', (1024,)), ('ssd_dt_bias', (8,)), ('ssd_a_log', (8,)), ('ssd_d', (8,)),
        ('ssd_norm', (512,)), ('gdn_conv_w', (4, 1536)), ('gdn_a_log', (4,)), ('gdn_dt_bias', (4,)), ('gdn_norm', (128,)),
    ]:
        I[name] = din(name, shape)
    O = {}
    for name, shape in [
        ('yp', (T, D)), ('ys', (NS, D)), ('p_gla', (4, 64, 128)), ('p_s5re', (16, 128)), ('p_s5im', (16, 128)),
        ('p_ssd', (8, 64, 128)), ('p_ssdc', (3, 1024)), ('p_gdn', (4, 128, 128)), ('p_gdnc', (3, 1536)),
        ('s_gla', (NS, 4, 64, 128)), ('s_s5re', (NS, 2048)), ('s_s5im', (NS, 2048)), ('s_ssd', (NS, 8, 64, 128)),
        ('s_ssdc', (NS, 3, 1024)), ('s_gdn', (NS, 4, 128, 128)), ('s_gdnc', (NS, 3, 1536)),
    ]:
        O[name] = dout(name, shape)
    if debug:
        O['dbg'] = dout('dbg', (6, 128, 8, NT))

    es = ExitStack()
    with es:
        es.enter_context(nc.allow_non_contiguous_dma(reason="small parameter / state layouts"))
        S = Sched(nc, es)
        cnt = [0]

        def sb(st, shape, dt, name=None):
            cnt[0] += 1
            return Tl(st.enter_context(nc.sbuf_tensor("%s_%d" % (name or 't', cnt[0]), list(shape), dt)))

        def mm(out, lhsT, rhs, start=True, stop=True, fin=True):
            S.op('pe', lambda: nc.tensor.matmul(out.ap, lhsT=lhsT.ap, rhs=rhs.ap, start=start, stop=stop),
                 reads=_u(lhsT, rhs), writes=_u(out), inc=(stop and fin))

        def tr(out, in_, ident):
            S.op('pe', lambda: nc.tensor.transpose(out=out.ap, in_=in_.ap, identity=ident.ap), reads=_u(in_, ident), writes=_u(out))

        def act(out, in_, func, bias=None, scale=None, accum=None):
            kw = {}
            if bias is not None:
                kw['bias'] = _ap(bias)
            if scale is not None:
                kw['scale'] = _ap(scale)
            if accum is not None:
                kw['accum_out'] = accum.ap
            S.op('act', lambda: nc.scalar.activation(out=out.ap, in_=in_.ap, func=func, **kw),
                 reads=_u(in_, bias, scale), writes=_u(out, accum))

        def E(e):
            return {'dve': nc.vector, 'pool': nc.gpsimd}[e]

        def tt(e, out, in0, in1, op):
            S.op(e, lambda: E(e).tensor_tensor(out=out.ap, in0=in0.ap, in1=in1.ap, op=op), reads=_u(in0, in1), writes=_u(out))

        def ts(e, out, in0, s1, op0, s2=None, op1=None):
            kw = dict(out=out.ap, in0=in0.ap, scalar1=_ap(s1), scalar2=_ap(s2), op0=op0)
            if op1 is not None:
                kw['op1'] = op1
            S.op(e, lambda: E(e).tensor_scalar(**kw), reads=_u(in0, s1, s2), writes=_u(out))

        def stt(out, in0, scalar, in1, op0, op1, accum=None):
            kw = {}
            if accum is not None:
                kw['accum_out'] = accum.ap
            S.op('dve', lambda: nc.vector.scalar_tensor_tensor(out=out.ap, in0=in0.ap, scalar=_ap(scalar), in1=in1.ap, op0=op0, op1=op1, **kw),
                 reads=_u(in0, scalar, in1), writes=_u(out, accum))

        def cp(e, out, in_):
            if e == 'act':
                S.op('act', lambda: nc.scalar.copy(out=out.ap, in_=in_.ap), reads=_u(in_), writes=_u(out))
            else:
                S.op(e, lambda: E(e).tensor_copy(out=out.ap, in_=in_.ap), reads=_u(in_), writes=_u(out))

        def scan(out, d0, d1, init):
            S.op('dve', lambda: nc.vector.tensor_tensor_scan(out=out.ap, data0=d0.ap, data1=d1.ap, initial=_ap(init), op0=ALU.mult, op1=ALU.add),
                 reads=_u(d0, d1, init), writes=_u(out))

        def memset(e, out, val):
            S.op(e, lambda: E(e).memset(out.ap, val), writes=_u(out))

        def red(out, in_, op=ALU.add):
            S.op('dve', lambda: nc.vector.tensor_reduce(out=out.ap, in_=in_.ap, axis=mybir.AxisListType.X, op=op), reads=_u(in_), writes=_u(out))

        def dma(q, out, in_, ds):
            S.dma(q, _ap(out), _ap(in_), ds, reads=_u(in_), writes=_u(out))

        P = es
        xT = sb(P, [128, 8, NT], F32, 'xT')
        xun = [Unit() for _ in range(17)]
        CH = [(c * 128, 128) for c in range(16)] + [(T, NS)]

        def xch(c):
            c0, n = CH[c]
            return V(xT.t[:, :, c0:c0 + n], (xun[c],))

        def xcols(c0, n):
            us = tuple(xun[c] for c in range(17) if CH[c][0] < c0 + n and CH[c][0] + CH[c][1] > c0)
            return V(xT.t[:, :, c0:c0 + n], us)

        psb = [Tl(es.enter_context(nc.psum_tensor("ps%d" % i, [128, 512], F32))) for i in range(8)]
        psrot = {'set': list(range(8)), 'i': 0}

        def psn():
            s = psrot['set']
            b = psb[s[psrot['i'] % len(s)]]
            psrot['i'] += 1
            return b

        d_init = S.dsem('init')
        d_out = S.dsem('out')
        d_w = [S.dsem('w%d' % i) for i in range(6)]
        d_st = [S.dsem('st%d' % i) for i in range(8)]

        ident = sb(P, [128, 128], F32, 'ident')
        identb = sb(P, [128, 128], BF16, 'identb')
        ones_bf = sb(P, [128, 128], BF16, 'ones')
        ones_f = sb(P, [128, 128], F32, 'onesf')
        triu = sb(P, [128, 128], F32, 'triu')
        mask01 = sb(P, [128, 128], BF16, 'mask01')
        negm_f = sb(P, [128, 128], F32, 'negmf')
        posm_f = sb(P, [128, 128], F32, 'posmf')
        neghalf_ = sb(P, [128, 1], F32, 'neghalf')
        sq_s = sb(P, [128, 8, 128], BF16, 'sq_s')
        v_s = sb(P, [128, 512], F32, 'v_s')
        rstd_s = sb(P, [128, 512], F32, 'rstd_s')
        ntmp = sb(P, [128, 8, 128], BF16, 'ntmp')
        gmix = sb(P, [128, 2, 8], F32, 'gmix')
        gmlp = sb(P, [128, 2, 8], F32, 'gmlp')
        gfin = sb(P, [128, 8], F32, 'gfin')

        memset('pool', ident[:], 1.0)
        S.op('pool', lambda: nc.gpsimd.affine_select(out=ident.t[:], in_=ident.t[:], pattern=[[-1, 128]], compare_op=ALU.is_equal,
                                                      fill=0.0, base=0, channel_multiplier=1), reads=[ident.u], writes=[ident.u])
        cp('dve', identb[:], ident[:])
        memset('pool', ones_bf[:], 1.0)
        memset('pool', ones_f[:], 1.0)
        memset('dve', neghalf_[:], -0.5)
        memset('pool', triu[:], 1.0)
        S.op('pool', lambda: nc.gpsimd.affine_select(out=triu.t[:], in_=triu.t[:], pattern=[[1, 128]], compare_op=ALU.is_ge,
                                                      fill=0.0, base=0, channel_multiplier=-1), reads=[triu.u], writes=[triu.u])
        cp('dve', mask01[:], triu[:])
        ts('dve', negm_f[:], triu[:], -1.0, ALU.add, -NEG, ALU.mult)
        memset('pool', posm_f[:], 1.0)
        S.op('pool', lambda: nc.gpsimd.affine_select(out=posm_f.t[:], in_=posm_f.t[:], pattern=[[-1, 128]], compare_op=ALU.is_gt,
                                                      fill=0.0, base=0, channel_multiplier=1), reads=[posm_f.u], writes=[posm_f.u])
        ts('dve', posm_f[:], posm_f[:], -1.0, ALU.add, NEG, ALU.mult)

        dma('sp', gmix[:], I['norm_mix'].rearrange("l (k p) -> p l k", p=128), d_init)
        dma('sp', gmlp[:], I['norm_mlp'].rearrange("l (k p) -> p l k", p=128), d_init)
        dma('sp', gfin[:], I['norm_final'].rearrange("(k p) -> p k", p=128), d_init)

        def ddump(v, off, width):
            if debug:
                np_ = v.ap.shape[0]
                dst = O['dbg'][5].rearrange("p k n -> p (k n)")[0:np_, off:off + width]
                dma('pool', dst, v, d_out)

        def dump(i):
            if debug:
                dma('sp', O['dbg'][i], xcols(0, NT), d_out)

        def rmsnorm_fm(xv, gv, hT, n, K=8, eng='pool'):
            sq = sq_s[:, 0:K, 0:n]
            tt(eng, sq, xv, xv, ALU.mult)
            pb = psn()
            for k in range(K):
                mm(pb[:, 0:n], ones_bf[:], sq_s[:, k, 0:n], start=(k == 0), stop=(k == K - 1))
            act(v_s[:, 0:n], pb[:, 0:n], AF.Ln, scale=1.0 / (128 * K), bias=EPS)
            act(rstd_s[:, 0:n], v_s[:, 0:n], AF.Exp, scale=-0.5)
            tmp = ntmp[:, 0:K, 0:n]
            tt('dve', tmp, xv, rstd_s[:, 0:n].un(1).bc([128, K, n]), ALU.mult)
            tt(eng, hT, tmp, gv.un(2).bc([128, K, n]), ALU.mult)

        def proj(Wt, c0, M, hT, out):
            for k in range(8):
                mm(out, Wt[:, k, c0:c0 + M], hT[:, k, :], start=(k == 0), stop=(k == 7))

        def projA(hT, Wt, c0, N, out):
            for k in range(8):
                mm(out, hT[:, k, :], Wt[:, k, c0:c0 + N], start=(k == 0), stop=(k == 7))

        def pnorm_part(po, n, gcol, gate, out, osb, sqo, t1, H=4):
            n4 = H * n
            cp('act', osb[:, 0:n4], po)
            act(sqo[:, 0:n4], po, AF.Square)
            pn = psn()
            mm(pn[:, 0:n4], ones_bf[:], sqo[:, 0:n4])
            act(v_s[:, 0:n4], pn[:, 0:n4], AF.Ln, scale=1.0 / 128, bias=EPS)
            act(rstd_s[:, 0:n4], v_s[:, 0:n4], AF.Exp, scale=-0.5)
            tt('dve', t1[:, 0:n4], osb[:, 0:n4], rstd_s[:, 0:n4], ALU.mult)
            stt(out, t1[:, 0:n4].re("p (h n) -> p h n", h=H), gcol, gate, ALU.mult, ALU.mult)

        def load_w(st, name, src, ncols, dsem, nsplit=8):
            w = sb(st, [128, 8, ncols], BF16, name)
            sv = src.rearrange("(k p) n -> p k n", p=128)
            for k in range(8):
                dma('pool', w[:, k, :], sv[:, k, :], dsem)
            return w

        hT_all = sb(P, [128, 8, NT], BF16, 'hTall')
        hun = [Unit() for _ in range(17)]

        def hcols(c0, n):
            us = tuple(hun[c] for c in range(17) if CH[c][0] < c0 + n and CH[c][0] + CH[c][1] > c0)
            return V(hT_all.t[:, :, c0:c0 + n], us)

        def hch(c):
            return hcols(*CH[c])

        def norm_chunk(c, gv, eng='pool'):
            rmsnorm_fm(xch(c), gv, hch(c), CH[c][1], eng=eng)

        def norm_all(gv):
            for c in range(17):
                norm_chunk(c, gv)

        def phase0():
            with ExitStack() as ph:
                xin = [sb(ph, [128, D], F32, 'xin') for _ in range(2)]
                xin_s = sb(ph, [NS, D], F32, 'xins')
                dx = [S.dsem('x0'), S.dsem('x1')]
                dma('sp', xin_s[:], I['xs'], d_init)
                for blk in range(16):
                    b = blk % 2
                    dma('sp', xin[b][:], I['xp'][blk * 128:(blk + 1) * 128, :], dx[b])
                    for half in range(2):
                        pb = psn()
                        for j in range(4):
                            k = half * 4 + j
                            tr(pb[:, j * 128:(j + 1) * 128], xin[b][:, k * 128:(k + 1) * 128], ident[:])
                        cp('act' if half == 0 else 'dve', xch(blk)[:, half * 4:(half + 1) * 4, :], pb[:].re("p (a b) -> p a b", a=4))
                    norm_chunk(blk, gmix[:, 0, :], 'dve')
                pb = psn()
                for k in range(8):
                    tr(pb[:, k * NS:(k + 1) * NS], xin_s[:, k * 128:(k + 1) * 128], ident[0:NS, 0:NS])
                cp('dve', xch(16), pb[:, 0:8 * NS].re("p (k s) -> p k s", k=8))
                norm_chunk(16, gmix[:, 0, :], 'dve')
                S.barrier()

        def mlp_phase(layer, do_norm, cg_hook=None, st_hook=None):
            with ExitStack() as ph:
                abuf = [sb(ph, [128, 4, NT], BF16, 'abuf') for _ in range(2)]
                wup = [sb(ph, [128, 8, 512], BF16, 'wup') for _ in range(2)]
                wdn = [sb(ph, [128, 4, 1024], BF16, 'wdn') for _ in range(2)]
                rbuf = [sb(ph, [128, 512], BF16, 'rbuf') for _ in range(3)]
                rs_s = sb(ph, [128, 4, NS], BF16, 'rss')
                pss = psb[7]
                psrot['set'] = list(range(7))
                us_up = pss.u
                us_dn = pss.u
                wupv = I['w_up'][layer].rearrange("(k p) n -> p k n", p=128)
                wdnv = I['w_down'][layer].rearrange("(k p) n -> p k n", p=128)

                def loadw(f):
                    b = f % 2
                    dma('pool', wup[b][:], wupv[:, :, f * 512:(f + 1) * 512], d_w[b])
                    dma('pool', wdn[b][:], wdnv[:, f * 4:(f + 1) * 4, :], d_w[2 + b])
                loadw(0)
                if do_norm:
                    norm_all(gmlp[:, layer, :])
                if st_hook is not None:
                    st_hook(ph)
                ri = 0
                for f in range(8):
                    b = f % 2
                    if f + 1 < 8:
                        loadw(f + 1)
                    for m in range(4):
                        for half in range(2):
                            pbs = [psn(), psn()]
                            for k in range(8):
                                lw = wup[b][:, k, m * 128:(m + 1) * 128]
                                for j in range(2):
                                    c0 = half * 1024 + j * 512
                                    mm(pbs[j][:, :], lw, hcols(c0, 512)[:, k, :], start=(k == 0), stop=(k == 7))
                                if half == 1:
                                    mm(V(pss.t[:, m * NS:(m + 1) * NS], (us_up,)), lw, hcols(T, NS)[:, k, :], start=(k == 0), stop=(k == 7))
                            for j in range(2):
                                c0 = half * 1024 + j * 512
                                r = rbuf[ri % 3]
                                ri += 1
                                act(r[:], pbs[j][:], AF.Relu)
                                tt('pool', abuf[b][:, m, c0:c0 + 512], r[:], r[:], ALU.mult)
                    act(rs_s[:], V(pss.t[:, 0:4 * NS], (us_up,)).re("p (m s) -> p m s", m=4), AF.Relu)
                    tt('pool', abuf[b][:, :, T:NT], rs_s[:], rs_s[:], ALU.mult)
                    for cg in range(4):
                        for mo in range(8):
                            pb = psn()
                            for k in range(4):
                                mm(pb[:, :], wdn[b][:, k, mo * 128:(mo + 1) * 128], abuf[b][:, k, cg * 512:(cg + 1) * 512], start=(k == 0), stop=(k == 3))
                            xv = xcols(cg * 512, 512)[:, mo, :]
                            tt('dve', xv, pb[:, :], xv, ALU.add)
                        if f == 7 and cg_hook is not None:
                            for c in range(cg * 4, cg * 4 + 4):
                                cg_hook(c)
                    for mo in range(8):
                        for k in range(4):
                            mm(V(pss.t[:, 64 + mo * NS:64 + (mo + 1) * NS], (us_dn,)), wdn[b][:, k, mo * 128:(mo + 1) * 128], abuf[b][:, k, T:NT],
                               start=(k == 0), stop=(k == 3))
                    xs_ = xch(16)
                    tt('dve', xs_, V(pss.t[:, 64:64 + 8 * NS], (us_dn,)).re("p (m s) -> p m s", m=8), xs_, ALU.add)
                    if f == 7 and cg_hook is not None:
                        cg_hook(16)
                psrot['set'] = list(range(8))
                S.barrier()

        def load_wc(st, name, src, c0, ncols, dsem):
            w = sb(st, [128, 8, ncols], BF16, name)
            sv = src.rearrange("(k p) n -> p k n", p=128)
            for k in range(8):
                dma('pool', w[:, k, :], sv[:, k, c0:c0 + ncols], dsem)
            return w

        def load_wr(st, name, src, r0, dsem):
            w = sb(st, [128, 4, D], BF16, name)
            sv = src[r0:r0 + 512, :].rearrange("(k p) n -> p k n", p=128)
            for k in range(4):
                dma('pool', w[:, k, :], sv[:, k, :], dsem)
            return w

        def out_proj(Wo, om, n, xc):
            for half in range(2):
                pxo = psn()
                for j in range(4):
                    m = half * 4 + j
                    for k in range(4):
                        mm(pxo[:, j * n:(j + 1) * n], Wo[:, k, m * 128:(m + 1) * 128], om[:, k, 0:n], start=(k == 0), stop=(k == 3), fin=(j == 3))
                xv = xc[:, half * 4:(half + 1) * 4, :]
                tt('dve', xv, pxo[:, 0:4 * n].re("p (j n) -> p j n", j=4), xv, ALU.add)

        def pass_gla(pre=None):
            with ExitStack() as ph:
                Win, Wout, Wgate = pre
                bgate = sb(ph, [64, 4], F32, 'bgate')
                dma('sp', bgate[:], I['b_gla_gate'].rearrange("(h k) -> k h", k=64), d_init)
                negb = sb(ph, [64, 4], F32, 'negb')
                ts('dve', negb[:], bgate[:], -1.0, ALU.mult)
                ggla = sb(ph, [128, 1], F32, 'ggla')
                dma('sp', ggla[:], I['g_gla_norm'].rearrange("(p o) -> p o", o=1), d_init)
                glrT = sb(ph, [16, 128], BF16, 'glrT')
                sp_t = sb(ph, [64, 4, 128], F32, 'sp_t')
                cs_t = sb(ph, [64, 4, 128], F32, 'cs_t')
                Ep = sb(ph, [64, 4, 128], F32, 'Ep')
                Em = sb(ph, [64, 4, 128], F32, 'Em')
                qs = sb(ph, [64, 4, 128], BF16, 'qs')
                ks = sb(ph, [64, 4, 128], BF16, 'ks')
                qsf = sb(ph, [64, 4, NS], F32, 'qsf')
                vT = sb(ph, [128, 4, 128], BF16, 'vT')
                sr = sb(ph, [128, 4, 128], BF16, 'sr')
                ones64 = sb(ph, [64, 128], F32, 'ones64')
                memset('pool', ones64[:], 1.0)
                kvtok = sb(ph, [128, 768], BF16, 'kvtok')
                attT = sb(ph, [128, 4, 128], BF16, 'attT')
                Sg = sb(ph, [64, 4, 128], F32, 'Sg')
                Sbf = sb(ph, [64, 4, 128], BF16, 'Sbf')
                tmpS = sb(ph, [64, 4, 128], F32, 'tmpS')
                memset('pool', Sg[:], 0.0)
                memset('pool', Sbf[:], 0.0)
                osb = sb(ph, [128, 512], F32, 'osb')
                sqo = sb(ph, [128, 512], BF16, 'sqo')
                t1 = osb
                omix = sb(ph, [128, 4, 128], BF16, 'omix')
                ktok_s = sb(ph, [NS, 4, 64], BF16, 'ktoks')
                vtok_s = sb(ph, [NS, 4, 128], BF16, 'vtoks')
                Kd = sb(ph, [NS, 2, 256], BF16, 'Kd')
                a_s = sb(ph, [64, 4, NS], F32, 'a_s')
                identb16 = identb[0:NS, 0:NS]
                sgl = [sb(ph, [64, 2, 4, 128], F32, 'sgl') for _ in range(2)]
                sgl2 = sgl

                qs2 = [qs, sb(ph, [64, 4, 128], BF16, 'qs2')]
                kv2 = [kvtok, sb(ph, [128, 768], BF16, 'kvtok2')]
                att2 = [attT, sb(ph, [128, 4, 128], BF16, 'attT2')]
                sr2 = [sr, sb(ph, [128, 4, 128], BF16, 'sr2')]
                Ep2 = [Ep, sb(ph, [64, 4, 128], F32, 'Ep2')]

                def gla_front(c):
                    c0, n = CH[c]
                    p = c % 2
                    qs_, kv_, att_, sr_, Ep_ = qs2[p], kv2[p], att2[p], sr2[p], Ep2[p]
                    hTn = hcols(c0, n)
                    pq = psn()
                    for h in range(4):
                        proj(Win, h * 64, 64, hTn, pq[0:64, h * n:(h + 1) * n])
                    pk = psn()
                    for h in range(4):
                        proj(Win, 256 + h * 64, 64, hTn, pk[0:64, h * n:(h + 1) * n])
                    pgl = psn()
                    proj(Win, 1024, 16, hTn, pgl[0:16, 0:n])
                    cp('act', glrT[:, 0:n], pgl[0:16, 0:n])
                    pg = psn()
                    for h in range(4):
                        mm(pg[0:64, h * n:(h + 1) * n], Wgate[:, h * 64:(h + 1) * 64], glrT[:, 0:n])
                    for h in range(4):
                        act(sp_t[:, h, 0:n], pg[0:64, h * n:(h + 1) * n], AF.Exp, scale=-1.0, bias=negb[:, h:h + 1])
                    act(sp_t[:, :, 0:n], sp_t[:, :, 0:n], AF.Ln, bias=1.0)
                    pqv = pq[0:64, 0:4 * n].re("p (h n) -> p h n", h=4)
                    pkv = pk[0:64, 0:4 * n].re("p (h n) -> p h n", h=4)
                    for h in range(4):
                        scan(cs_t[:, h, 0:n], ones64[:, 0:n], sp_t[:, h, 0:n], 0.0)
                    act(Ep_[:, :, 0:n], cs_t[:, :, 0:n], AF.Exp, scale=-1.0 / 16)
                    act(Em[:, :, 0:n], cs_t[:, :, 0:n], AF.Exp, scale=1.0 / 16)
                    stt(qs_[:, :, 0:n], pqv, 0.125, Ep_[:, :, 0:n], ALU.mult, ALU.mult)
                    tt('dve', ks[:, :, 0:n], pkv, Em[:, :, 0:n], ALU.mult)
                    pv = psn()
                    for h in range(4):
                        proj(Win, 512 + h * 128, 128, hTn, pv[:, h * n:(h + 1) * n])
                    cp('act', vT[:, :, 0:n], pv[:, 0:4 * n].re("p (h n) -> p h n", h=4))
                    pr = psn()
                    for h in range(4):
                        proj(Win, 1040 + h * 128, 128, hTn, pr[:, h * n:(h + 1) * n])
                    act(sr_[:, :, 0:n], pr[:, 0:4 * n].re("p (h n) -> p h n", h=4), AF.Silu)
                    pt = psn()
                    ptb = pt[:].bitcast(BF16)
                    for h in range(4):
                        tr(ptb[0:n, h * 64:(h + 1) * 64], ks[:, h, 0:n], identb[0:64, 0:64])
                    for h in range(4):
                        tr(ptb[0:n, 256 + h * 128:256 + (h + 1) * 128], vT[:, h, 0:n], identb[:])
                    cp('dve', kv_[0:n, :], ptb[0:n, 0:768])
                    pa = psn()
                    for h in range(4):
                        mm(pa[0:n, h * n:(h + 1) * n], ks[:, h, 0:n], qs_[:, h, 0:n])
                    tt('dve', att_[0:n, :, 0:n], pa[0:n, 0:4 * n].re("p (h n) -> p h n", h=4), mask01[0:n, 0:n].un(1).bc([n, 4, n]), ALU.mult)

                def gla_back(c):
                    c0, n = CH[c]
                    p = c % 2
                    qs_, kv_, att_, sr_, Ep_ = qs2[p], kv2[p], att2[p], sr2[p], Ep2[p]
                    po = psn()
                    for h in range(4):
                        mm(po[:, h * n:(h + 1) * n], kv_[0:n, 256 + h * 128:256 + (h + 1) * 128], att_[0:n, h, 0:n], start=True, stop=False)
                        mm(po[:, h * n:(h + 1) * n], Sbf[:, h, :], qs_[:, h, 0:n], start=False, stop=True)
                    pS = psn()
                    for h in range(4):
                        mm(pS[0:64, h * 128:(h + 1) * 128], kv_[0:n, h * 64:(h + 1) * 64], kv_[0:n, 256 + h * 128:256 + (h + 1) * 128])
                    tt('dve', tmpS[:], pS[0:64, :].re("p (h v) -> p h v", h=4), Sg[:], ALU.add)
                    tt('pool', Sg[:], tmpS[:], Ep_[:, :, n - 1:n].bc([64, 4, 128]), ALU.mult)
                    cp('act', Sbf[:], Sg[:])
                    if c == 15:
                        dma('sp', O['p_gla'].rearrange("h k v -> k h v"), Sg[:], d_out)
                    pnorm_part(po[:, 0:4 * n], n, ggla[:, 0:1], sr_[:, :, 0:n], omix[:, :, 0:n], osb, sqo, t1)
                    out_proj(Wout, omix, n, xch(c))

                gla_front(0)
                for c in range(16):
                    if c + 1 < 16:
                        gla_front(c + 1)
                    gla_back(c)

                for c in range(16, 17):
                    c0, n = CH[c]
                    sample = (c == 16)
                    xc = xch(c)
                    hTn = hcols(c0, n)
                    pq = psn()
                    for h in range(4):
                        proj(Win, h * 64, 64, hTn, pq[0:64, h * n:(h + 1) * n])
                    pk = psn()
                    for h in range(4):
                        proj(Win, 256 + h * 64, 64, hTn, pk[0:64, h * n:(h + 1) * n])
                    pgl = psn()
                    proj(Win, 1024, 16, hTn, pgl[0:16, 0:n])
                    cp('act', glrT[:, 0:n], pgl[0:16, 0:n])
                    pg = psn()
                    for h in range(4):
                        mm(pg[0:64, h * n:(h + 1) * n], Wgate[:, h * 64:(h + 1) * 64], glrT[:, 0:n])
                    for h in range(4):
                        act(sp_t[:, h, 0:n], pg[0:64, h * n:(h + 1) * n], AF.Exp, scale=-1.0, bias=negb[:, h:h + 1])
                    act(sp_t[:, :, 0:n], sp_t[:, :, 0:n], AF.Ln, bias=1.0)
                    pqv = pq[0:64, 0:4 * n].re("p (h n) -> p h n", h=4)
                    pkv = pk[0:64, 0:4 * n].re("p (h n) -> p h n", h=4)
                    if not sample:
                        for h in range(4):
                            scan(cs_t[:, h, 0:n], ones64[:, 0:n], sp_t[:, h, 0:n], 0.0)
                        act(Ep[:, :, 0:n], cs_t[:, :, 0:n], AF.Exp, scale=-1.0 / 16)
                        act(Em[:, :, 0:n], cs_t[:, :, 0:n], AF.Exp, scale=1.0 / 16)
                        stt(qs[:, :, 0:n], pqv, 0.125, Ep[:, :, 0:n], ALU.mult, ALU.mult)
                        tt('dve', ks[:, :, 0:n], pkv, Em[:, :, 0:n], ALU.mult)
                    else:
                        act(a_s[:], sp_t[:, :, 0:n], AF.Exp, scale=-1.0 / 16)
                        act(qsf[:], pqv, AF.Copy, scale=0.125)
                        cp('dve', ks[:, :, 0:n], pkv)
                    pv = psn()
                    for h in range(4):
                        proj(Win, 512 + h * 128, 128, hTn, pv[:, h * n:(h + 1) * n])
                    cp('act', vT[:, :, 0:n], pv[:, 0:4 * n].re("p (h n) -> p h n", h=4))
                    pr = psn()
                    for h in range(4):
                        proj(Win, 1040 + h * 128, 128, hTn, pr[:, h * n:(h + 1) * n])
                    act(sr[:, :, 0:n], pr[:, 0:4 * n].re("p (h n) -> p h n", h=4), AF.Silu)
                    if sample:
                        po = psb[7]
                        psrot['set'] = list(range(7))
                    else:
                        po = psn()
                    pt = psn()
                    ptb = pt[:].bitcast(BF16)
                    for h in range(4):
                        tr(ptb[0:n, h * 64:(h + 1) * 64], ks[:, h, 0:n], identb[0:64, 0:64])
                    for h in range(4):
                        tr(ptb[0:n, 256 + h * 128:256 + (h + 1) * 128], vT[:, h, 0:n], identb[:])
                    if not sample:
                        cp('dve', kvtok[0:n, :], ptb[0:n, 0:768])
                        pa = psn()
                        for h in range(4):
                            mm(pa[0:n, h * n:(h + 1) * n], ks[:, h, 0:n], qs[:, h, 0:n])
                        tt('dve', attT[0:n, :, 0:n], pa[0:n, 0:4 * n].re("p (h n) -> p h n", h=4), mask01[0:n, 0:n].un(1).bc([n, 4, n]), ALU.mult)
                        for h in range(4):
                            mm(po[:, h * n:(h + 1) * n], kvtok[0:n, 256 + h * 128:256 + (h + 1) * 128], attT[0:n, h, 0:n], start=True, stop=False)
                            mm(po[:, h * n:(h + 1) * n], Sbf[:, h, :], qs[:, h, 0:n], start=False, stop=True)
                        pS = psn()
                        for h in range(4):
                            mm(pS[0:64, h * 128:(h + 1) * 128], kvtok[0:n, h * 64:(h + 1) * 64], kvtok[0:n, 256 + h * 128:256 + (h + 1) * 128])
                        tt('dve', tmpS[:], pS[0:64, :].re("p (h v) -> p h v", h=4), Sg[:], ALU.add)
                        tt('pool', Sg[:], tmpS[:], Ep[:, :, n - 1:n].bc([64, 4, 128]), ALU.mult)
                        cp('act', Sbf[:], Sg[:])
                        if c == 15:
                            dma('sp', O['p_gla'].rearrange("h k v -> k h v"), Sg[:], d_out)
                    else:
                        cp('dve', ktok_s[:], ptb[0:NS, 0:256].re("p (h k) -> p h k", h=4))
                        cp('dve', vtok_s[:], ptb[0:NS, 256:768].re("p (h v) -> p h v", h=4))
                        for gi in range(8):
                            b = gi % 2
                            s0 = gi * 2
                            dma('sp', sgl[b][:], I['sgla'][s0:s0 + 2].rearrange("s h k v -> k s h v"), d_st[b])
                            tt('dve', Kd[:], ktok_s[:].re("p h k -> p (h k)").un(1).bc([NS, 2, 256]), identb[0:NS, s0:s0 + 2].un(2).bc([NS, 2, 256]), ALU.mult)
                            pso = [psn(), psn()]
                            for si in range(2):
                                for h in range(4):
                                    mm(pso[si][0:64, h * 128:(h + 1) * 128], Kd[:, si, h * 64:(h + 1) * 64], vtok_s[:, h, :])
                            av = a_s[:].re("p h s -> p s h")[:, s0:s0 + 2, :].un(3).bc([64, 2, 4, 128])
                            tt('pool', sgl2[b][:], sgl[b][:], av, ALU.mult)
                            for si in range(2):
                                tt('dve', sgl2[b][:, si], pso[si][0:64, :].re("p (h v) -> p h v", h=4), sgl2[b][:, si], ALU.add)
                            for si in range(2):
                                for h in range(4):
                                    col = h * NS + s0 + si
                                    mm(po[:, col:col + 1], sgl2[b][:, si, h, :], qsf[:, h, s0 + si:s0 + si + 1])
                            dma('sp', O['s_gla'][s0:s0 + 2].rearrange("s h k v -> k s h v"), sgl2[b][:], d_st[2 + b])
                    pnorm_part(po[:, 0:4 * n], n, ggla[:, 0:1], sr[:, :, 0:n], omix[:, :, 0:n], osb, sqo, t1)
                    out_proj(Wout, omix, n, xc)
                    psrot['set'] = list(range(8))
                S.barrier()

        def pass_s5(pre=None):
            with ExitStack() as ph:
                Win, Wout, Wglu = pre
                bglu = sb(ph, [128, 4], F32, 'bglu')
                dma('sp', bglu[:], I['b_s5_glu'].rearrange("(m p) -> p m", p=128), d_init)
                DU = sb(ph, [128, 4], F32, 'DU')
                dma('sp', DU[:], I['s5_d'].rearrange("(u q) -> q u", q=128), d_init)

                def s16(name):
                    return sb(ph, [128, 16], F32, name)
                LR = s16('LR'); LI = s16('LI'); LDT = s16('LDT')
                dma('sp', LR[:], I['s5_lam_re'].rearrange("(t q) -> q t", q=128), d_init)
                dma('sp', LI[:], I['s5_lam_im'].rearrange("(t q) -> q t", q=128), d_init)
                ldv = I['s5_log_dt'].rearrange("(t g) -> g t", g=2)
                for g2 in range(2):
                    dma('sp', LDT[g2 * 64:(g2 + 1) * 64, :], ldv[g2].partition_broadcast(64), d_init)
                DT = s16('DT'); Rm = s16('Rm'); TH = s16('TH'); t16a = s16('t16a'); t16b = s16('t16b'); t16c = s16('t16c')
                AR = s16('AR'); AI_ = s16('AI'); KR = s16('KR'); KI = s16('KI')
                nre = s16('nre'); den = s16('den'); rden = s16('rden')
                c16a = s16('c16a'); c16b = s16('c16b'); Hre = s16('Hre'); Him = s16('Him')
                ginit_re = s16('gire'); ginit_im = s16('giim'); glast_re = s16('glre'); glast_im = s16('glim')
                t16i = sb(ph, [128, 16], I32, 't16i')
                Bpad_re = sb(ph, [128, 16, 128], BF16, 'Bpr')
                Bpad_im = sb(ph, [128, 16, 128], BF16, 'Bpi')
                Cpad_re = sb(ph, [128, 16, 128], BF16, 'Cpr')
                Cpad_imn = sb(ph, [128, 16, 128], BF16, 'Cpi')
                uTs = [sb(ph, [128, 4, 128], BF16, 'uT') for _ in range(2)]
                hre_gs = [[sb(ph, [128, 4, 128], BF16, 'hre') for _ in range(4)] for _ in range(2)]
                him_gs = [[sb(ph, [128, 4, 128], BF16, 'him') for _ in range(4)] for _ in range(2)]
                yv = sb(ph, [128, 4, 128], F32, 'yv')
                ygb = sb(ph, [128, 4, 128], BF16, 'ygb')
                sg_t = sb(ph, [128, 4, 128], BF16, 'sg')
                omix = sb(ph, [128, 4, 128], BF16, 'omix')
                hout_tok = sb(ph, [NS, 128], F32, 'hout')

                act(DT[:], LDT[:], AF.Exp)
                tt('dve', t16a[:], LR[:], DT[:], ALU.mult)
                act(Rm[:], t16a[:], AF.Exp)
                tt('dve', TH[:], LI[:], DT[:], ALU.mult)
                ts('dve', t16a[:], TH[:], 1.0 / (2 * math.pi), ALU.mult)
                cp('dve', t16i[:], t16a[:])
                cp('dve', t16b[:], t16i[:])
                tt('dve', t16c[:], t16a[:], t16b[:], ALU.subtract)
                ts('dve', TH[:], t16c[:], 2 * math.pi, ALU.mult)

                pst = ExitStack()
                NTAU = 129
                COS = sb(pst, [128, 16, NTAU], F32, 'COS')
                SIN = sb(pst, [128, 16, NTAU], F32, 'SIN')
                with ExitStack() as tmpst:
                    taui = sb(tmpst, [128, NTAU], I32, 'taui')
                    tauf = sb(tmpst, [128, NTAU], F32, 'tauf')
                    S.op('pool', lambda: nc.gpsimd.iota(taui.t[:], pattern=[[1, NTAU]], base=0, channel_multiplier=0), writes=[taui.u])
                    cp('dve', tauf[:], taui[:])
                    U0 = sb(tmpst, [128, 8, NTAU], F32, 'U0')
                    U1 = sb(tmpst, [128, 8, NTAU], F32, 'U1')
                    U2 = sb(tmpst, [128, 8, NTAU], F32, 'U2')
                    UI = sb(tmpst, [128, 8, NTAU], I32, 'UI')
                    for th in range(2):
                        tsl = slice(th * 8, (th + 1) * 8)
                        tt('dve', U0[:], TH[:, tsl].un(2).bc([128, 8, NTAU]), tauf[:].un(1).bc([128, 8, NTAU]), ALU.mult)
                        ts('dve', U0[:], U0[:], 1.0 / (2 * math.pi), ALU.mult)
                        for (dst, off) in ((SIN, 0.0), (COS, 0.25)):
                            ts('dve', U1[:], U0[:], off, ALU.add)
                            cp('dve', UI[:], U1[:])
                            cp('dve', U2[:], UI[:])
                            tt('dve', U1[:], U1[:], U2[:], ALU.subtract)
                            act(dst[:, tsl, :], U1[:], AF.Sin, scale=2 * math.pi)
                    S.barrier()
                tt('dve', AR[:], Rm[:], COS[:, :, 1], ALU.mult)
                tt('dve', AI_[:], Rm[:], SIN[:, :, 1], ALU.mult)
                ts('dve', nre[:], AR[:], -1.0, ALU.add)
                tt('dve', t16a[:], LR[:], LR[:], ALU.mult)
                tt('dve', t16b[:], LI[:], LI[:], ALU.mult)
                tt('dve', den[:], t16a[:], t16b[:], ALU.add)
                S.op('dve', lambda: nc.vector.reciprocal(out=rden.t[:], in_=den.t[:]), reads=[den.u], writes=[rden.u])
                tt('dve', t16a[:], nre[:], LR[:], ALU.mult)
                tt('dve', t16b[:], AI_[:], LI[:], ALU.mult)
                tt('dve', t16c[:], t16a[:], t16b[:], ALU.add)
                tt('dve', KR[:], t16c[:], rden[:], ALU.mult)
                tt('dve', t16a[:], AI_[:], LR[:], ALU.mult)
                tt('dve', t16b[:], nre[:], LI[:], ALU.mult)
                tt('dve', t16c[:], t16a[:], t16b[:], ALU.subtract)
                tt('dve', KI[:], t16c[:], rden[:], ALU.mult)
                with ExitStack() as tmpst:
                    BR = sb(tmpst, [128, 16, 16], F32, 'BR')
                    BI = sb(tmpst, [128, 16, 16], F32, 'BI')
                    dma('sp', BR[:], I['s5_b_re'].rearrange("(t q) h -> q t h", q=128), d_st[4])
                    dma('sp', BI[:], I['s5_b_im'].rearrange("(t q) h -> q t h", q=128), d_st[4])
                    CUr = sb(tmpst, [128, 4, 64], F32, 'CUr')
                    CUi = sb(tmpst, [128, 4, 64], F32, 'CUi')
                    dma('sp', CUr[:], I['s5_c_re'].rearrange("(u q) p -> q u p", q=128), d_st[4])
                    dma('sp', CUi[:], I['s5_c_im'].rearrange("(u q) p -> q u p", q=128), d_st[4])
                    BBR = sb(tmpst, [128, 16, 16], F32, 'BBR')
                    BBI = sb(tmpst, [128, 16, 16], F32, 'BBI')
                    b1 = sb(tmpst, [128, 16, 16], F32, 'b1')
                    b2 = sb(tmpst, [128, 16, 16], F32, 'b2')
                    krb = KR[:].un(2).bc([128, 16, 16])
                    kib = KI[:].un(2).bc([128, 16, 16])
                    tt('dve', b1[:], BR[:], krb, ALU.mult)
                    tt('dve', b2[:], BI[:], kib, ALU.mult)
                    tt('dve', BBR[:], b1[:], b2[:], ALU.subtract)
                    tt('dve', b1[:], BI[:], krb, ALU.mult)
                    tt('dve', b2[:], BR[:], kib, ALU.mult)
                    tt('dve', BBI[:], b1[:], b2[:], ALU.add)
                    mki = sb(tmpst, [128, 4, 4, 8], I32, 'mki')
                    MK = sb(tmpst, [128, 16, 8], F32, 'MK')
                    for g2 in range(2):
                        S.op('pool', lambda g2=g2: nc.gpsimd.iota(mki.t[g2 * 64:(g2 + 1) * 64], pattern=[[0, 4], [-2, 4], [1, 8]], base=-g2, channel_multiplier=0),
                             writes=[mki.u])
                    cp('dve', MK[:], mki[:].re("p a b c -> p (a b) c"))
                    ts('dve', MK[:], MK[:], 0.0, ALU.is_equal)
                    EXr = sb(tmpst, [128, 4, 128], F32, 'EX')
                    EX4 = EXr[:].re("p t (a b) -> p t a b", a=8)
                    EXC4 = EXr[:].re("p t (g q) -> p t g q", g=2)
                    MKC = sb(tmpst, [128, 4, 128], F32, 'MKC')

                    def tr4(dst, scale=None):
                        pb = psn()
                        for q in range(4):
                            tr(pb[:, q * 128:(q + 1) * 128], EXr[:, q, :], ident[:])
                        if scale is None:
                            cp('act', dst, pb[:].re("p (a b) -> p a b", a=4))
                        else:
                            act(dst, pb[:].re("p (a b) -> p a b", a=4), AF.Copy, scale=scale)
                    for tg in range(4):
                        tsl = slice(tg * 4, (tg + 1) * 4)
                        mkb = MK[:, tsl].un(3).bc([128, 4, 8, 16])
                        tt('dve', EX4, mkb, mkb, ALU.mult)
                        tr4(MKC[:])
                        for (src, dst) in ((BBR, Bpad_re), (BBI, Bpad_im)):
                            tt('dve', EX4, src[:, tsl].un(2).bc([128, 4, 8, 16]), mkb, ALU.mult)
                            tr4(dst[:, tsl, :])
                        for (src, dst, sgn) in ((CUr, Cpad_re, 1.0), (CUi, Cpad_imn, -1.0)):
                            tt('dve', EXC4, src[:, tg, :].un(1).un(1).bc([128, 4, 2, 64]), MKC[:].re("p t (g q) -> p t g q", g=2), ALU.mult)
                            tr4(dst[:, tsl, :], scale=sgn)
                    S.barrier()

                class TS:
                    pass
                TD, TP = TS(), TS()
                for T_, nm in ((TD, 'd'), (TP, 'p')):
                    T_.s5b = sb(pst, [128, 4, 128], F32, nm + 's5b')
                    T_.gin_re = sb(pst, [128, 4, 128], F32, nm + 'ginre')
                    T_.gin_im = sb(pst, [128, 4, 128], F32, nm + 'ginim')
                    T_.g_re = sb(pst, [128, 4, 128], F32, nm + 'g_re')
                    T_.g_im = sb(pst, [128, 4, 128], F32, nm + 'g_im')
                memset('pool', ginit_re[:], 0.0)
                memset('pool', ginit_im[:], 0.0)

                def rot(tau, ore, oim):
                    tt('dve', c16a[:], glast_re[:], COS[:, :, tau], ALU.mult)
                    tt('dve', c16b[:], glast_im[:], SIN[:, :, tau], ALU.mult)
                    tt('dve', ore[:], c16a[:], c16b[:], ALU.subtract)
                    tt('dve', c16a[:], glast_im[:], COS[:, :, tau], ALU.mult)
                    tt('dve', c16b[:], glast_re[:], SIN[:, :, tau], ALU.mult)
                    tt('dve', oim[:], c16a[:], c16b[:], ALU.add)

                def pre(c):
                    c0, n = CH[c]
                    uT = uTs[c % 2]
                    pu = psn()
                    for ut in range(4):
                        proj(Win, ut * 128, 128, hcols(c0, n), pu[:, ut * n:(ut + 1) * n])
                    cp('act', uT[:, :, 0:n], pu[:, 0:4 * n].re("p (h n) -> p h n", h=4))

                def post(c):
                    c0, n = CH[c]
                    uT = uTs[c % 2]
                    hre_g, him_g = hre_gs[c % 2], him_gs[c % 2]
                    py = psn()
                    for ut in range(4):
                        for q in range(4):
                            t = ut * 4 + q
                            mm(py[:, ut * n:(ut + 1) * n], Cpad_re[:, t, :], hre_g[ut][:, q, 0:n], start=(q == 0), stop=False)
                            mm(py[:, ut * n:(ut + 1) * n], Cpad_imn[:, t, :], him_g[ut][:, q, 0:n], start=False, stop=(q == 3))
                    tt('pool', yv[:, :, 0:n], uT[:, :, 0:n], DU[:].un(2).bc([128, 4, n]), ALU.mult)
                    tt('dve', yv[:, :, 0:n], py[:, 0:4 * n].re("p (u n) -> p u n", u=4), yv[:, :, 0:n], ALU.add)
                    act(ygb[:, :, 0:n], yv[:, :, 0:n], AF.Gelu_apprx_tanh)
                    pg2 = psn()
                    for m in range(4):
                        for k in range(4):
                            mm(pg2[:, m * n:(m + 1) * n], Wglu[:, k, m * 128:(m + 1) * 128], ygb[:, k, 0:n], start=(k == 0), stop=(k == 3))
                    for m in range(4):
                        act(sg_t[:, m, 0:n], pg2[:, m * n:(m + 1) * n], AF.Sigmoid, bias=bglu[:, m:m + 1])
                    tt('dve', omix[:, :, 0:n], ygb[:, :, 0:n], sg_t[:, :, 0:n], ALU.mult)
                    out_proj(Wout, omix, n, xch(c))
                    norm_chunk(c, gmlp[:, 0, :])

                for c in range(16):
                    c0, n = CH[c]
                    pre(c)
                    uT = uTs[c % 2]
                    hre_g, him_g = hre_gs[c % 2], him_gs[c % 2]

                    def mmgroup(tg):
                        pbr = psn()
                        pbi = psn()
                        for q in range(4):
                            t = tg * 4 + q
                            mm(pbr[:, q * n:(q + 1) * n], Bpad_re[:, t, :], uT[:, tg, 0:n], fin=(q == 3))
                            mm(pbi[:, q * n:(q + 1) * n], Bpad_im[:, t, :], uT[:, tg, 0:n], fin=(q == 3))
                        return (pbr[:, 0:4 * n].re("p (q n) -> p q n", q=4), pbi[:, 0:4 * n].re("p (q n) -> p q n", q=4))

                    def rot_in(e, tg, srcr, srci, T_):
                        Cg = COS[:, tg * 4:(tg + 1) * 4, 0:n]
                        Sn = SIN[:, tg * 4:(tg + 1) * 4, 0:n]
                        tt(e, T_.gin_re[:], srcr, Cg, ALU.mult)
                        tt(e, T_.s5b[:], srci, Sn, ALU.mult)
                        tt(e, T_.gin_re[:], T_.gin_re[:], T_.s5b[:], ALU.add)
                        tt(e, T_.gin_im[:], srci, Cg, ALU.mult)
                        tt(e, T_.s5b[:], srcr, Sn, ALU.mult)
                        tt(e, T_.gin_im[:], T_.gin_im[:], T_.s5b[:], ALU.subtract)

                    def scans(tg, T_):
                        for q in range(4):
                            t = tg * 4 + q
                            scan(T_.g_re[:, q, :], Rm[:, t:t + 1].bc([128, n]), T_.gin_re[:, q, :], ginit_re[:, t:t + 1])
                            scan(T_.g_im[:, q, :], Rm[:, t:t + 1].bc([128, n]), T_.gin_im[:, q, :], ginit_im[:, t:t + 1])

                    def rot_out(e, tg, T_):
                        Cg = COS[:, tg * 4:(tg + 1) * 4, 0:n]
                        Sn = SIN[:, tg * 4:(tg + 1) * 4, 0:n]
                        tt(e, T_.gin_re[:], T_.g_re[:], Cg, ALU.mult)
                        tt(e, T_.s5b[:], T_.g_im[:], Sn, ALU.mult)
                        tt(e, hre_g[tg][:], T_.gin_re[:], T_.s5b[:], ALU.subtract)
                        tt(e, T_.gin_im[:], T_.g_im[:], Cg, ALU.mult)
                        tt(e, T_.s5b[:], T_.g_re[:], Sn, ALU.mult)
                        tt(e, him_g[tg][:], T_.gin_im[:], T_.s5b[:], ALU.add)
                        cp('dve', glast_re[:, tg * 4:(tg + 1) * 4], T_.g_re[:, :, n - 1])
                        cp('dve', glast_im[:, tg * 4:(tg + 1) * 4], T_.g_im[:, :, n - 1])

                    r3, i3 = mmgroup(3)
                    cp('act', TP.g_re[:], r3)
                    cp('act', TP.g_im[:], i3)
                    rot_in('pool', 3, TP.g_re[:], TP.g_im[:], TP)
                    r0, i0 = mmgroup(0)
                    rot_in('dve', 0, r0, i0, TD)
                    scans(0, TD)
                    rot_out('dve', 0, TD)
                    scans(3, TP)
                    rot_out('pool', 3, TP)
                    for tg in (1, 2):
                        r_, i_ = mmgroup(tg)
                        rot_in('dve', tg, r_, i_, TD)
                        scans(tg, TD)
                        rot_out('dve', tg, TD)
                    rot(n, ginit_re, ginit_im)
                    if c == 15:
                        rot(n - 1, Hre, Him)
                        for (src, oname) in ((Hre, 'p_s5re'), (Him, 'p_s5im')):
                            pz = psn()
                            tr(pz[0:16, 0:128], src[:], ident[:])
                            cp('dve', hout_tok[0:16, 0:128], pz[0:16, 0:128])
                            dma('sp', O[oname], hout_tok[0:16, 0:128], d_out)
                    if c > 0:
                        post(c - 1)
                post(15)
                S.barrier()
                pst.close()

                with ExitStack() as sst:
                    h0re_tok = sb(sst, [NS, 2048], F32, 'h0re')
                    h0im_tok = sb(sst, [NS, 2048], F32, 'h0im')
                    hout2 = [sb(sst, [NS, 512], F32, 'hout2') for _ in range(2)]
                    hs_re = sb(sst, [128, 16, NS], F32, 'hsre')
                    hs_im = sb(sst, [128, 16, NS], F32, 'hsim')
                    s5m1 = sb(sst, [128, 16, NS], F32, 's5m1')
                    s5m2 = sb(sst, [128, 16, NS], F32, 's5m2')
                    dma('sp', h0re_tok[:], I['ss5re'], d_st[5])
                    dma('sp', h0im_tok[:], I['ss5im'], d_st[5])
                    pre(16)
                    uT = uTs[0]
                    hre_g, him_g = hre_gs[0], him_gs[0]
                    pzr = psn()
                    pzi = psn()
                    for t in range(16):
                        tr(pzr[:, t * NS:(t + 1) * NS], h0re_tok[:, t * 128:(t + 1) * 128], ident[0:NS, 0:NS])
                        tr(pzi[:, t * NS:(t + 1) * NS], h0im_tok[:, t * 128:(t + 1) * 128], ident[0:NS, 0:NS])
                    pbr = psn()
                    pbi = psn()
                    for t in range(16):
                        mm(pbr[:, t * NS:(t + 1) * NS], Bpad_re[:, t, :], uT[:, t // 4, 0:NS])
                        mm(pbi[:, t * NS:(t + 1) * NS], Bpad_im[:, t, :], uT[:, t // 4, 0:NS])
                    v3 = lambda p_: p_[:, 0:16 * NS].re("p (t s) -> p t s", t=16)
                    arb = AR[:].un(2).bc([128, 16, NS])
                    aib = AI_[:].un(2).bc([128, 16, NS])
                    tt('dve', s5m1[:], v3(pzr), arb, ALU.mult)
                    tt('dve', s5m2[:], v3(pzi), aib, ALU.mult)
                    tt('pool', s5m1[:], s5m1[:], s5m2[:], ALU.subtract)
                    tt('dve', hs_re[:], v3(pbr), s5m1[:], ALU.add)
                    tt('dve', s5m1[:], v3(pzi), arb, ALU.mult)
                    tt('dve', s5m2[:], v3(pzr), aib, ALU.mult)
                    tt('pool', s5m1[:], s5m1[:], s5m2[:], ALU.add)
                    tt('dve', hs_im[:], v3(pbi), s5m1[:], ALU.add)
                    for tg in range(4):
                        cp('act', hre_g[tg][:, :, 0:NS], hs_re[:, tg * 4:(tg + 1) * 4, :])
                        cp('act', him_g[tg][:, :, 0:NS], hs_im[:, tg * 4:(tg + 1) * 4, :])
                    for (src, oname) in ((hs_re, 's_s5re'), (hs_im, 's_s5im')):
                        for tg in range(4):
                            pz = psn()
                            for q in range(4):
                                t = tg * 4 + q
                                tr(pz[0:NS, q * 128:(q + 1) * 128], src[:, t, :], ident[:])
                            hb = hout2[tg % 2]
                            cp('dve', hb[:], pz[0:NS, :])
                            dma('sp', O[oname][:, tg * 512:(tg + 1) * 512], hb[:], d_out)
                    post(16)
                    S.barrier()
                S.barrier()


        def conv_diag(st, name, wsrc, ntile):
            cw = sb(st, [128, 4, ntile], F32, name + 'cw')
            dma('sp', cw[:], wsrc.rearrange("k (t p) -> p k t", p=128), d_init)
            DW = sb(st, [128, ntile, 4, 128], BF16, name)
            for t in range(ntile):
                for k in range(4):
                    ts('dve', DW[:, t, k, :], ident[:], cw[:, k, t:t + 1], ALU.mult)
            return DW

        def softplus_tok(out, pin, brow, n, w, tmp):
            tt('dve', tmp[0:n, 0:w], pin, brow[0:n, 0:w], ALU.add)
            act(tmp[0:n, 0:w], tmp[0:n, 0:w], AF.Exp)
            act(out, tmp[0:n, 0:w], AF.Ln, bias=1.0)

        def cum_stuff(la_tok, n, H, cum_tok, negcum, wl_tok, explast):
            pc = psn()
            mm(pc[0:n, 0:H], triu[0:n, 0:n], la_tok[0:n, 0:H])
            mm(pc[:, 16:16 + H], ones_f[0:n, :], la_tok[0:n, 0:H])
            cp('dve', cum_tok[0:n, 0:H], pc[0:n, 0:H])
            ts('dve', negcum[0:n, 0:H], pc[0:n, 0:H], -1.0, ALU.mult)
            tt('dve', wl_tok[0:n, 0:H], pc[0:n, 16:16 + H], negcum[0:n, 0:H], ALU.add)
            act(wl_tok[0:n, 0:H], wl_tok[0:n, 0:H], AF.Exp)
            act(explast[:, 0:H], pc[:, 16:16 + H], AF.Exp)

        def pass_ssd(pre=None):
            with ExitStack() as ph:
                Win = pre if pre is not None else load_wc(ph, 'win_ssd', I['w_in_cd'], 0, 1544, d_w[4])
                Wout = load_wr(ph, 'wout_ssd', I['w_out_cd'], 0, d_w[5])
                DW = conv_diag(ph, 'dwssd', I['ssd_conv_w'], 8)
                cb = sb(ph, [128, 8], F32, 'cb')
                dma('sp', cb[:], I['ssd_conv_b'].rearrange("(t p) -> p t", p=128), d_init)
                dtb = sb(ph, [128, 8], F32, 'dtb')
                dma('sp', dtb[:], I['ssd_dt_bias'].partition_broadcast(128), d_init)
                arow = sb(ph, [128, 8], F32, 'arow')
                dma('sp', arow[:], I['ssd_a_log'].partition_broadcast(128), d_init)
                act(arow[:], arow[:], AF.Exp)
                ts('dve', arow[:], arow[:], -1.0, ALU.mult)
                Dexp = sb(ph, [128, 4], F32, 'Dexp')
                dv = I['ssd_d'].rearrange("(t g) -> g t", g=2)
                for g2 in range(2):
                    dma('sp', Dexp[g2 * 64:(g2 + 1) * 64, :], dv[g2].partition_broadcast(64), d_init)
                gssd = sb(ph, [128, 4], F32, 'gssd')
                dma('sp', gssd[:], I['ssd_norm'].rearrange("(t p) -> p t", p=128), d_init)
                XB = sb(ph, [128, 8, 131], BF16, 'XB')
                memset('pool', XB[:], 0.0)
                XCs = [sb(ph, [128, 8, 128], BF16, 'XC') for _ in range(2)]
                zss = [sb(ph, [128, 4, 128], BF16, 'zs') for _ in range(2)]
                dt_tok = sb(ph, [128, 8], F32, 'dt_tok')
                la_tok = sb(ph, [128, 8], F32, 'la_tok')
                tmp8 = sb(ph, [128, 8], F32, 'tmp8')
                y2 = sb(ph, [128, 4, 128], F32, 'y2')
                yz = sb(ph, [128, 4, 128], F32, 'yz')
                sqz = sb(ph, [128, 4, 128], BF16, 'sqz')
                omix = sb(ph, [128, 4, 128], BF16, 'omix')
                ctok = sb(ph, [NS, 1024], F32, 'ctok')

                def pre(c):
                    c0, n = CH[c]
                    hTn = hcols(c0, n)
                    pz_ = psn()
                    for t in range(4):
                        proj(Win, t * 128, 128, hTn, pz_[:, t * n:(t + 1) * n])
                    act(zss[c % 2][:, :, 0:n], pz_[:, 0:4 * n].re("p (t n) -> p t n", t=4), AF.Silu)
                    pxs = [psn(), psn()]
                    for t in range(8):
                        proj(Win, 512 + t * 128, 128, hTn, pxs[t // 4][:, (t % 4) * n:(t % 4 + 1) * n])
                    pdt = psn()
                    projA(hTn, Win, 1536, 8, pdt[0:n, 0:8])
                    softplus_tok(dt_tok[0:n, :], pdt[0:n, 0:8], dtb, n, 8, tmp8)
                    tt('dve', la_tok[0:n, :], dt_tok[0:n, :], arow[0:n, :], ALU.mult)
                    return pxs

                def conv(n, rhs_of, p=0):
                    XC = XCs[p]
                    for half in range(2):
                        pc = psn()
                        for j in range(4):
                            t = half * 4 + j
                            for k in range(4):
                                mm(pc[:, j * n:(j + 1) * n], DW[:, t, k, :], rhs_of(t, k), start=(k == 0), stop=(k == 3))
                        for j in range(4):
                            t = half * 4 + j
                            act(XC[:, t, 0:n], pc[:, j * n:(j + 1) * n], AF.Silu, bias=cb[:, t:t + 1])

                def conv_state_out(c0, M, oap):
                    for j in range(2):
                        pcs = psn()
                        projA(hcols(c0, M), Win, 512 + j * 512, 512, pcs[0:M, :])
                        cp('act', ctok[0:M, j * 512:(j + 1) * 512], pcs[0:M, :])
                    dma('sp', oap, ctok[0:M, :], d_out)

                def post(c, yT):
                    c0, n = CH[c]
                    tt('dve', yz[:, :, 0:n], yT, zss[c % 2][:, :, 0:n], ALU.mult)
                    tt('dve', sqz[:, :, 0:n], yz[:, :, 0:n], yz[:, :, 0:n], ALU.mult)
                    pn = psn()
                    for g in range(2):
                        mm(pn[:, g * n:(g + 1) * n], ones_bf[:], sqz[:, 2 * g, 0:n], start=True, stop=False)
                        mm(pn[:, g * n:(g + 1) * n], ones_bf[:], sqz[:, 2 * g + 1, 0:n], start=False, stop=True)
                    act(v_s[:, 0:2 * n], pn[:, 0:2 * n], AF.Ln, scale=1.0 / 256, bias=EPS)
                    act(rstd_s[:, 0:2 * n], v_s[:, 0:2 * n], AF.Exp, scale=-0.5)
                    for g in range(2):
                        tt('dve', yz[:, 2 * g:2 * g + 2, 0:n], yz[:, 2 * g:2 * g + 2, 0:n], rstd_s[:, g * n:(g + 1) * n].un(1).bc([128, 2, n]), ALU.mult)
                    tt('dve', omix[:, :, 0:n], yz[:, :, 0:n], gssd[:].un(2).bc([128, 4, n]), ALU.mult)
                    out_proj(Wout, omix, n, xch(c))

                pst = ExitStack()
                lab = sb(pst, [128, 8, 64], F32, 'lab')
                labn = sb(pst, [128, 8, 128], F32, 'labn')
                csT = sb(pst, [128, 4, 128], F32, 'csT')
                ones128 = sb(pst, [128, 128], F32, 'ones128')
                memset('pool', ones128[:], 1.0)
                cum_tok = sb(pst, [128, 8], F32, 'cum_tok')
                negcum = sb(pst, [128, 8], F32, 'negcum')
                wl_tok = sb(pst, [128, 8], F32, 'wl_tok')
                dw_tok = sb(pst, [128, 8], F32, 'dw_tok')
                xbtoks = [sb(pst, [128, 768], BF16, 'xbtok') for _ in range(2)]
                xdtZs = [sb(pst, [128, 8, 128], BF16, 'xdtZ') for _ in range(2)]
                for z_ in xdtZs:
                    memset('pool', z_[:], 0.0)
                xws = [sb(pst, [128, 512], BF16, 'xw') for _ in range(2)]
                decT = sb(pst, [128, 8, 128], BF16, 'decT')
                MTs = [sb(pst, [128, 8, 128], BF16, 'MT') for _ in range(2)]
                Ecums = [sb(pst, [128, 4, 128], F32, 'Ecum2') for _ in range(2)]
                explasts = [sb(pst, [128, 8], F32, 'explast2') for _ in range(2)]
                y1 = sb(pst, [128, 4, 128], F32, 'y1')
                ST = sb(pst, [128, 512], F32, 'ST')
                STbf = sb(pst, [128, 512], BF16, 'STbf')
                tmpST = sb(pst, [128, 512], F32, 'tmpST')
                memset('pool', ST[:], 0.0)
                memset('pool', STbf[:], 0.0)

                def ssd_front(c):
                    c0, n = CH[c]
                    p = c % 2
                    XC, xbtok, xdtZ, xw, MT, Ecum_, explast_ = XCs[p], xbtoks[p], xdtZs[p], xws[p], MTs[p], Ecums[p], explasts[p]
                    pxs = pre(c)
                    for half in range(2):
                        cp('act' if half == 0 else 'dve', XB[:, half * 4:(half + 1) * 4, 3:3 + n], pxs[half][:, 0:4 * n].re("p (t n) -> p t n", t=4))
                    conv(n, lambda t, k: XB[:, t, k:k + n], p)
                    cp('dve', XB[:, :, 0:3], XB[:, :, n:n + 3])
                    if c == 15:
                        conv_state_out(T - 3, 3, O['p_ssdc'])
                    cp('dve', lab[0:n], la_tok[0:n, :].un(2).bc([n, 8, 64]))
                    pexp = psn()
                    for t in range(4):
                        mm(pexp[:, t * n:(t + 1) * n], lab[0:n, 2 * t:2 * t + 2, :].re("p a b -> p (a b)"), ident[0:n, 0:n])
                    for t in range(4):
                        scan(csT[:, t, 0:n], ones128[:, 0:n], pexp[:, t * n:(t + 1) * n], 0.0)
                    act(Ecum_[:, :, 0:n], csT[:, :, 0:n], AF.Exp)
                    cum_stuff(la_tok, n, 8, cum_tok, negcum, wl_tok, explast_)
                    tt('dve', dw_tok[0:n, :], dt_tok[0:n, :], wl_tok[0:n, :], ALU.mult)
                    pt = psn()
                    ptb = pt[:].bitcast(BF16)
                    for t in range(6):
                        tr(ptb[0:n, t * 128:(t + 1) * 128], XC[:, t, 0:n], identb[:])
                    cp('dve', xbtok[0:n, :], ptb[0:n, 0:768])
                    xsv = xbtok[0:n, 0:512].re("p (t a c) -> p t a c", t=4, a=2)
                    for h2 in range(2):
                        tt('dve', xdtZ[0:n].re("p (t a) c -> p t a c", a=2)[:, :, h2, h2 * 64:(h2 + 1) * 64], xsv[:, :, h2, :],
                           dt_tok[0:n, :].re("p (t a) -> p t a", a=2)[:, :, h2].un(2).bc([n, 4, 64]), ALU.mult)
                    tt('dve', xw[0:n, :].re("p (h c) -> p h c", h=8), xbtok[0:n, 0:512].re("p (h c) -> p h c", h=8),
                       dw_tok[0:n, :].un(2).bc([n, 8, 64]), ALU.mult)
                    pcb = psn()
                    for g in range(2):
                        mm(pcb[0:n, g * n:(g + 1) * n], XC[:, 4 + g, 0:n], XC[:, 6 + g, 0:n])
                    cp('dve', labn[0:n, :, 0:n], la_tok[0:n, :].un(2).bc([n, 8, n]))
                    pdec = [psn(), psn()]
                    for h in range(8):
                        o_ = pdec[h // 4][0:n, (h % 4) * n:(h % 4 + 1) * n]
                        mm(o_, labn[0:n, h, 0:n], triu[0:n, 0:n], start=True, stop=False)
                        mm(o_, ident[0:n, 0:n], negm_f[0:n, 0:n], start=False, stop=True)
                    for h in range(8):
                        act(decT[0:n, h, 0:n], pdec[h // 4][0:n, (h % 4) * n:(h % 4 + 1) * n], AF.Exp, bias=negcum[0:n, h:h + 1])
                    for g in range(2):
                        tt('dve', MT[0:n, 4 * g:4 * g + 4, 0:n], pcb[0:n, g * n:(g + 1) * n].un(1).bc([n, 4, n]), decT[0:n, 4 * g:4 * g + 4, 0:n], ALU.mult)

                def ssd_back(c):
                    c0, n = CH[c]
                    p = c % 2
                    XC, xbtok, xdtZ, xw, MT, Ecum_, explast_ = XCs[p], xbtoks[p], xdtZs[p], xws[p], MTs[p], Ecums[p], explasts[p]
                    py = psn()
                    for t in range(4):
                        for h2 in range(2):
                            h = 2 * t + h2
                            mm(py[:, t * n:(t + 1) * n], xdtZ[0:n, h, :], MT[0:n, h, 0:n], start=(h2 == 0), stop=(h2 == 1))
                    pi_ = psn()
                    for t in range(4):
                        mm(pi_[:, t * n:(t + 1) * n], STbf[:, t * 128:(t + 1) * 128], XC[:, 6 + t // 2, 0:n])
                    tt('dve', y1[:, :, 0:n], pi_[:, 0:4 * n].re("p (t n) -> p t n", t=4), Ecum_[:, :, 0:n], ALU.mult)
                    tt('dve', y2[:, :, 0:n], py[:, 0:4 * n].re("p (t n) -> p t n", t=4), y1[:, :, 0:n], ALU.add)
                    tt('dve', y1[:, :, 0:n], XC[:, 0:4, 0:n], Dexp[:].un(2).bc([128, 4, n]), ALU.mult)
                    tt('dve', y2[:, :, 0:n], y2[:, :, 0:n], y1[:, :, 0:n], ALU.add)
                    pS = psn()
                    for g in range(2):
                        mm(pS[:, g * 256:(g + 1) * 256], xbtok[0:n, 512 + g * 128:512 + (g + 1) * 128], xw[0:n, g * 256:(g + 1) * 256])
                    tt('dve', tmpST[:].re("p (h c) -> p h c", h=8), ST[:].re("p (h c) -> p h c", h=8), explast_[:].un(2).bc([128, 8, 64]), ALU.mult)
                    tt('dve', ST[:], pS[:, :], tmpST[:], ALU.add)
                    cp('act', STbf[:], ST[:])
                    post(c, y2[:, :, 0:n])

                ssd_front(0)
                for c in range(16):
                    if c + 1 < 16:
                        ssd_front(c + 1)
                    ssd_back(c)
                for half in range(1):
                    pz = psn()
                    for t in range(4):
                        tr(pz[:, t * 128:(t + 1) * 128], ST[:, t * 128:(t + 1) * 128], ident[:])
                    cp('dve', tmpST[:], pz[:, :])
                    dma('sp', O['p_ssd'].rearrange("h p n -> (h p) n").rearrange("(t q) n -> q t n", q=128), tmpST[:].re("p (t n) -> p t n", t=4), d_out)
                S.barrier()
                pst.close()

                with ExitStack() as sst:
                    c = 16
                    c0, n = CH[c]
                    HX = sb(sst, [128, 8, 4, NS], BF16, 'HX')
                    scv = sb(sst, [3 * NS, 1024], F32, 'scv')
                    dma('sp', scv[:], I['sssdc'].rearrange("s k f -> (s k) f"), d_st[6])
                    dma('sp', O['s_ssdc'][:, 0:2, :], I['sssdc'][:, 1:3, :], d_out)
                    pxs = pre(c)
                    for half in range(2):
                        cp('act' if half == 0 else 'dve', HX[:, half * 4:(half + 1) * 4, 3, :], pxs[half][:, 0:4 * n].re("p (t n) -> p t n", t=4))
                    for half in range(2):
                        ph_ = psn()
                        for j in range(4):
                            t = half * 4 + j
                            tr(ph_[:, j * 48:(j + 1) * 48], scv[:, t * 128:(t + 1) * 128], ident[0:48, 0:48])
                        cp('dve', HX[:, half * 4:(half + 1) * 4, 0:3, :], ph_[:, 0:4 * 48].re("p (t s k) -> p t k s", t=4, k=3))
                    conv(n, lambda t, k: HX[:, t, k, :], 0)
                    XC = XCs[0]
                    conv_state_out(T, NS, O['s_ssdc'][:, 2, :])
                    da_tok = sb(sst, [NS, 8], F32, 'da_tok')
                    act(da_tok[:], la_tok[0:NS, :], AF.Exp)
                    dab = sb(sst, [NS, 8, 64], F32, 'dab')
                    cp('dve', dab[:], da_tok[:].un(2).bc([NS, 8, 64]))
                    pe_ = psn()
                    for t in range(4):
                        mm(pe_[:, t * NS:(t + 1) * NS], dab[:, 2 * t:2 * t + 2, :].re("p a b -> p (a b)"), ident[0:NS, 0:NS])
                    daT = sb(sst, [128, 4, NS], F32, 'daT')
                    cp('dve', daT[:], pe_[:, 0:4 * NS].re("p (t s) -> p t s", t=4))
                    pt = psn()
                    ptb = pt[:].bitcast(BF16)
                    for t in range(8):
                        tr(ptb[0:NS, t * 128:(t + 1) * 128], XC[:, t, 0:NS], identb[:])
                    xbc_s = sb(sst, [NS, 1024], BF16, 'xbc_s')
                    cp('dve', xbc_s[:], ptb[0:NS, 0:1024])
                    xdt_s = sb(sst, [NS, 512], BF16, 'xdt_s')
                    tt('pool', xdt_s[:].re("p (h c) -> p h c", h=8), xbc_s[:, 0:512].re("p (h c) -> p h c", h=8), dt_tok[0:NS, :].un(2).bc([NS, 8, 64]), ALU.mult)
                    OH = sb(sst, [NS, NS, 128], BF16, 'OH')
                    cp('dve', OH[:], identb[0:NS, 0:NS].un(2).bc([NS, NS, 128]))
                    XdZ = sb(sst, [NS, 4, 512], BF16, 'XdZ')
                    Ssl = [sb(sst, [128, 4, 4, 128], F32, 'Ssl') for _ in range(2)]
                    prod = sb(sst, [128, 4, 128], F32, 'prod')
                    ysT = sb(sst, [128, 4, NS], F32, 'ysT')
                    for gi in range(4):
                        b = gi % 2
                        s0 = gi * 4
                        for t in range(4):
                            dma('sp', Ssl[b][:, t], I['sssd'][s0:s0 + 4, 2 * t:2 * t + 2].rearrange("s a p n -> (a p) s n"), d_st[b])
                        tt('pool', XdZ[:], xdt_s[:].un(1).bc([NS, 4, 512]), identb[0:NS, s0:s0 + 4].un(2).bc([NS, 4, 512]), ALU.mult)
                        pcs = [psn(), psn()]
                        for g in range(2):
                            for si in range(4):
                                mm(pcs[g][:, si * 128:(si + 1) * 128], OH[:, s0 + si, :], xbc_s[:, 768 + g * 128:768 + (g + 1) * 128])
                        for t in range(4):
                            pso = psn()
                            for si in range(4):
                                mm(pso[:, si * 128:(si + 1) * 128], XdZ[:, si, t * 128:(t + 1) * 128], xbc_s[:, 512 + (t // 2) * 128:512 + (t // 2 + 1) * 128])
                            tt('pool', Ssl[b][:, t], Ssl[b][:, t], daT[:, t, s0:s0 + 4].un(2).bc([128, 4, 128]), ALU.mult)
                            tt('dve', Ssl[b][:, t], pso[:, :].re("p (s n) -> p s n", s=4), Ssl[b][:, t], ALU.add)
                            tt('dve', prod[:], pcs[t // 2][:, :].re("p (s n) -> p s n", s=4), Ssl[b][:, t], ALU.mult)
                            red(ysT[:, t, s0:s0 + 4], prod[:])
                        for t in range(4):
                            dma('sp', O['s_ssd'][s0:s0 + 4, 2 * t:2 * t + 2].rearrange("s a p n -> (a p) s n"), Ssl[b][:, t], d_st[2 + b])
                    y1s = sb(sst, [128, 4, NS], F32, 'y1s')
                    tt('pool', y1s[:], XC[:, 0:4, 0:NS], Dexp[:].un(2).bc([128, 4, NS]), ALU.mult)
                    tt('dve', y2[:, :, 0:NS], ysT[:], y1s[:], ALU.add)
                    post(c, y2[:, :, 0:NS])
                    S.barrier()
                S.barrier()

        def pass_gdn():
            with ExitStack() as ph:
                Wt_ = sb(ph, [128, 8, 2056], BF16, 'win_gdn')
                wblocks = [(1536, 2056), (0, 512), (512, 1024), (1024, 1536)]
                Win = WB(Wt_.t, wblocks)
                svw = I['w_in_cd'].rearrange("(k p) n -> p k n", p=128)
                for bi, (lo, hi) in enumerate(wblocks):
                    S.dma('pool', Wt_.t[:, :, lo:hi], svw[:, :, 1544 + lo:1544 + hi], d_w[bi], writes=[Win.units[bi]])
                Wout = load_wr(ph, 'wout_gdn', I['w_out_cd'], 512, d_w[5])
                DW = conv_diag(ph, 'dwgdn', I['gdn_conv_w'], 12)
                arow = sb(ph, [128, 4], F32, 'arowg')
                dma('sp', arow[:], I['gdn_a_log'].partition_broadcast(128), d_init)
                act(arow[:], arow[:], AF.Exp)
                ts('dve', arow[:], arow[:], -1.0, ALU.mult)
                dtb = sb(ph, [128, 4], F32, 'dtbg')
                dma('sp', dtb[:], I['gdn_dt_bias'].partition_broadcast(128), d_init)
                ggdn = sb(ph, [128, 1], F32, 'ggdn')
                dma('sp', ggdn[:], I['gdn_norm'].rearrange("(p o) -> p o", o=1), d_init)
                XQ = sb(ph, [128, 12, 131], BF16, 'XQ')
                memset('pool', XQ[:], 0.0)
                QC = sb(ph, [128, 12, 128], BF16, 'QC')
                gss = [sb(ph, [128, 4, 128], BF16, 'gs') for _ in range(2)]
                qkv_tok = sb(ph, [128, 12, 128], BF16, 'qkv_tok')
                sqk = sb(ph, [128, 8, 128], BF16, 'sqk')
                ssq = sb(ph, [128, 8], F32, 'ssq')
                rs = sb(ph, [128, 8], F32, 'rs')
                qn_tok = sb(ph, [128, 4, 128], BF16, 'qn_tok')
                kn_tok = sb(ph, [128, 4, 128], BF16, 'kn_tok')
                beta_tok = sb(ph, [128, 4], F32, 'beta_tok')
                g_tok = sb(ph, [128, 4], F32, 'g_tok')
                tmp4 = sb(ph, [128, 4], F32, 'tmp4')
                knT = sb(ph, [128, 4, 128], BF16, 'knT')
                qnT = sb(ph, [128, 4, 128], BF16, 'qnT')
                osb = sb(ph, [128, 512], F32, 'osb')
                sqo = sb(ph, [128, 512], BF16, 'sqo')
                t1 = osb
                omix = sb(ph, [128, 4, 128], BF16, 'omix')
                ctok = [sb(ph, [NS, 512], F32, 'ctokg') for _ in range(2)]

                def pre(c):
                    c0, n = CH[c]
                    hTn = hcols(c0, n)
                    pg_ = psn()
                    for t in range(4):
                        proj(Win, 1536 + t * 128, 128, hTn, pg_[:, t * n:(t + 1) * n])
                    act(gss[c % 2][:, :, 0:n], pg_[:, 0:4 * n].re("p (t n) -> p t n", t=4), AF.Silu)
                    pxs = [psn(), psn(), psn()]
                    for t in range(12):
                        proj(Win, t * 128, 128, hTn, pxs[t // 4][:, (t % 4) * n:(t % 4 + 1) * n])
                    pba = psn()
                    projA(hTn, Win, 2048, 8, pba[0:n, 0:8])
                    act(beta_tok[0:n, :], pba[0:n, 0:4], AF.Sigmoid)
                    softplus_tok(g_tok[0:n, :], pba[0:n, 4:8], dtb, n, 4, tmp4)
                    tt('dve', g_tok[0:n, :], g_tok[0:n, :], arow[0:n, :], ALU.mult)
                    return pxs

                def conv(n, rhs_of):
                    for b3 in range(3):
                        pc = psn()
                        for j in range(4):
                            t = b3 * 4 + j
                            for k in range(4):
                                mm(pc[:, j * n:(j + 1) * n], DW[:, t, k, :], rhs_of(t, k), start=(k == 0), stop=(k == 3))
                        act(QC[:, b3 * 4:(b3 + 1) * 4, 0:n], pc[:, 0:4 * n].re("p (t n) -> p t n", t=4), AF.Silu)

                def conv_state_out(c0, M, oap):
                    for j in range(3):
                        pcs = psn()
                        projA(hcols(c0, M), Win, j * 512, 512, pcs[0:M, :])
                        cp('act', ctok[j % 2][0:M, :], pcs[0:M, :])
                        dma('sp', oap[:, j * 512:(j + 1) * 512], ctok[j % 2][0:M, :], d_out)

                def tokprep(n):
                    pts = [psn(), psn()]
                    ptb0 = pts[0][:].bitcast(BF16)
                    ptb1 = pts[1][:].bitcast(BF16)
                    for t in range(8):
                        tr(ptb0[0:n, t * 128:(t + 1) * 128], QC[:, t, 0:n], identb[:])
                    for t in range(4):
                        tr(ptb1[0:n, t * 128:(t + 1) * 128], QC[:, 8 + t, 0:n], identb[:])
                    cp('dve', qkv_tok[0:n, 0:8, :], ptb0[0:n, :].re("p (t f) -> p t f", t=8))
                    cp('act', qkv_tok[0:n, 8:12, :], ptb1[0:n, 0:512].re("p (t f) -> p t f", t=4))
                    if gstop <= 1.2:
                        return
                    for h in range(8):
                        act(sqk[0:n, h, :], qkv_tok[0:n, h, :], AF.Square, accum=ssq[0:n, h:h + 1])
                    if gstop <= 1.3:
                        return
                    act(rs[0:n, :], ssq[0:n, :], AF.Ln, bias=EPS)
                    act(rs[0:n, :], rs[0:n, :], AF.Exp, scale=-0.5)
                    if gstop <= 1.4:
                        return
                    ts('dve', rs[0:n, 0:4], rs[0:n, 0:4], 128.0 ** -0.5, ALU.mult)
                    tt('dve', qn_tok[0:n], qkv_tok[0:n, 0:4, :], rs[0:n, 0:4].un(2).bc([n, 4, 128]), ALU.mult)
                    tt('dve', kn_tok[0:n], qkv_tok[0:n, 4:8, :], rs[0:n, 4:8].un(2).bc([n, 4, 128]), ALU.mult)
                    if gstop <= 1.6:
                        return
                    pt2a = psn()
                    pt2b = psn()
                    for h in range(4):
                        mm(pt2a[:, h * n:(h + 1) * n], kn_tok[0:n, h, :], identb[0:n, 0:n])
                        mm(pt2b[:, h * n:(h + 1) * n], qn_tok[0:n, h, :], identb[0:n, 0:n])
                    cp('dve', knT[:, :, 0:n], pt2a[:, 0:4 * n].re("p (h n) -> p h n", h=4))
                    cp('act', qnT[:, :, 0:n], pt2b[:, 0:4 * n].re("p (h n) -> p h n", h=4))

                pst = ExitStack()
                cum_tok = sb(pst, [128, 4], F32, 'cum_tokg')
                negcum = sb(pst, [128, 4], F32, 'negcumg')
                wl_tok = sb(pst, [128, 4], F32, 'wl_tokg')
                gam_tok = sb(pst, [128, 4], F32, 'gam_tok')
                bg_tok = sb(pst, [128, 4], F32, 'bg_tok')
                explast = sb(pst, [128, 4], F32, 'explastg')
                gbn = sb(pst, [128, 4, 128], F32, 'gbn')
                Xf = Tl2(sb(pst, [128, 4, 256], F32, 'Xf').t)
                qg_tok = sb(pst, [128, 4, 128], BF16, 'qg_tok')
                kw_tok = sb(pst, [128, 4, 128], BF16, 'kw_tok')
                qgT = sb(pst, [128, 4, 128], BF16, 'qgT')
                decA = sb(pst, [128, 4, 128], BF16, 'decA')
                Pm = Tl2(sb(pst, [128, 4, 128], F32, 'Pm').t)
                PTm = Tl2(sb(pst, [128, 4, 128], F32, 'PTm').t)
                attqT = sb(pst, [128, 4, 128], BF16, 'attqT')
                WkT = sb(pst, [128, 4, 128], BF16, 'WkT')
                u_bf = sb(pst, [128, 4, 128], BF16, 'u_bf')
                Sg = sb(pst, [128, 4, 128], F32, 'Sgd')
                Sbf = sb(pst, [128, 4, 128], BF16, 'Sgdbf')
                memset('pool', Sg[:], 0.0)
                memset('pool', Sbf[:], 0.0)
                def front(c):
                    c0, n = CH[c]
                    pxs = pre(c)
                    for b3 in range(3):
                        cp(('act', 'dve', 'act')[b3], XQ[:, b3 * 4:(b3 + 1) * 4, 3:3 + n], pxs[b3][:, 0:4 * n].re("p (t n) -> p t n", t=4))
                    conv(n, lambda t, k: XQ[:, t, k:k + n])
                    cp('dve', XQ[:, :, 0:3], XQ[:, :, n:n + 3])
                    if c == 15:
                        conv_state_out(T - 3, 3, O['p_gdnc'])

                front(0)
                tokprep(128)
                psrot['set'] = list(range(7))
                po = psb[7]

                def post(c):
                    n = 128
                    pnorm_part(po[:, 0:4 * n], n, ggdn[:, 0:1], gss[c % 2][:, :, 0:n], omix[:, :, 0:n], osb, sqo, t1)
                    out_proj(Wout, omix, n, xch(c))
                    norm_chunk(c, gmlp[:, 1, :], 'dve')

                for c in range(16):
                    c0, n = CH[c]
                    if gstop <= 2:
                        break
                    cum_stuff(g_tok, n, 4, cum_tok, negcum, wl_tok, explast)
                    act(gam_tok[0:n, :], cum_tok[0:n, :], AF.Exp)
                    tt('dve', bg_tok[0:n, :], beta_tok[0:n, :], gam_tok[0:n, :], ALU.mult)
                    tt('dve', Xf[0:n, :, 0:128], qkv_tok[0:n, 8:12, :], beta_tok[0:n, :].un(2).bc([n, 4, 128]), ALU.mult)
                    tt('dve', Xf[0:n, :, 128:256], kn_tok[0:n], bg_tok[0:n, :].un(2).bc([n, 4, 128]), ALU.mult)
                    tt('dve', qg_tok[0:n], qn_tok[0:n], gam_tok[0:n, :].un(2).bc([n, 4, 128]), ALU.mult)
                    tt('dve', kw_tok[0:n], kn_tok[0:n], wl_tok[0:n, :].un(2).bc([n, 4, 128]), ALU.mult)
                    pt3 = psn()
                    for h in range(4):
                        mm(pt3[:, h * n:(h + 1) * n], qg_tok[0:n, h, :], identb[0:n, 0:n])
                    cp('dve', qgT[:, :, 0:n], pt3[:, 0:4 * n].re("p (h n) -> p h n", h=4))
                    if c == 0:
                        ddump(qkv_tok[0:n].re("p t f -> p (t f)"), 0, 1536)
                        ddump(qn_tok[0:n].re("p t f -> p (t f)"), 1536, 512)
                        ddump(kn_tok[0:n].re("p t f -> p (t f)"), 2048, 512)
                        ddump(beta_tok[0:n, :], 2560, 4)
                        ddump(g_tok[0:n, :], 2564, 4)
                        ddump(cum_tok[0:n, :], 2568, 4)
                        ddump(Xf[0:n].re("p t f -> p (t f)"), 6100, 1024)
                    if gstop <= 3:
                        break
                    pkk = psn()
                    pqk = psn()
                    for h in range(4):
                        mm(pkk[0:n, h * n:(h + 1) * n], knT[:, h, 0:n], knT[:, h, 0:n])
                        mm(pqk[0:n, h * n:(h + 1) * n], knT[:, h, 0:n], qnT[:, h, 0:n])
                    cp('dve', gbn[0:n, :, 0:n], g_tok[0:n, :].un(2).bc([n, 4, n]))
                    pr1 = psn()
                    pr2 = psn()
                    for h in range(4):
                        mm(pr1[0:n, h * n:(h + 1) * n], gbn[0:n, h, 0:n], triu[0:n, 0:n], start=True, stop=False)
                        mm(pr1[0:n, h * n:(h + 1) * n], ident[0:n, 0:n], posm_f[0:n, 0:n], start=False, stop=True)
                        mm(pr2[0:n, h * n:(h + 1) * n], gbn[0:n, h, 0:n], triu[0:n, 0:n], start=True, stop=False)
                        mm(pr2[0:n, h * n:(h + 1) * n], ident[0:n, 0:n], negm_f[0:n, 0:n], start=False, stop=True)
                    for h in range(4):
                        act(decA[0:n, h, 0:n], pr1[0:n, h * n:(h + 1) * n], AF.Exp, scale=-1.0, bias=cum_tok[0:n, h:h + 1])
                    tt('dve', decA[0:n, :, 0:n], pkk[0:n, 0:4 * n].re("p (h n) -> p h n", h=4), decA[0:n, :, 0:n], ALU.mult)
                    tt('dve', Pm[0:n, :, 0:n], decA[0:n, :, 0:n], beta_tok[0:n, :].un(2).bc([n, 4, n]), ALU.mult)
                    for h in range(4):
                        act(decA[0:n, h, 0:n], pr2[0:n, h * n:(h + 1) * n], AF.Exp, bias=negcum[0:n, h:h + 1])
                    tt('dve', attqT[0:n, :, 0:n], pqk[0:n, 0:4 * n].re("p (h n) -> p h n", h=4), decA[0:n, :, 0:n], ALU.mult)
                    pt4 = psn()
                    for h in range(4):
                        mm(pt4[0:n, h * n:(h + 1) * n], Pm[0:n, h, 0:n], ident[0:n, 0:n])
                    cp('act', PTm[0:n, :, 0:n], pt4[0:n, 0:4 * n].re("p (h n) -> p h n", h=4))
                    if c == 0:
                        ddump(Pm[0:n].re("p t f -> p (t f)"), 2600, 512)
                        ddump(attqT[0:n].re("p t f -> p (t f)"), 4300, 512)
                    if gstop <= 4:
                        break
                    if c > 0:
                        post(c - 1)
                    if c + 1 < 16:
                        front(c + 1)
                    nlev = 7
                    Xh = [Xf.half(hp, 2) for hp in range(2)]
                    Ph = [Pm.half(hp, 2) for hp in range(2)]
                    PTh = [PTm.half(hp, 2) for hp in range(2)]
                    for l in range(nlev):
                        pXs, pPs = [], []
                        for hp in range(2):
                            pX = psn()
                            for j in range(2):
                                mm(pX[0:n, j * 256:(j + 1) * 256], PTh[hp][0:n, j, 0:n], Xh[hp][0:n, j, :])
                            pXs.append(pX)
                            if l + 1 < nlev:
                                pP = psn()
                                for j in range(2):
                                    mm(pP[0:n, j * n:(j + 1) * n], PTh[hp][0:n, j, 0:n], Ph[hp][0:n, j, 0:n])
                                for j in range(2):
                                    mm(pP[0:n, 256 + j * n:256 + (j + 1) * n], Ph[hp][0:n, j, 0:n], PTh[hp][0:n, j, 0:n])
                                pPs.append(pP)
                        for hp in range(2):
                            tt('dve', Xh[hp][0:n], Xh[hp][0:n], pXs[hp][0:n, :].re("p (h f) -> p h f", h=2), ALU.subtract if l == 0 else ALU.add)
                            if l + 1 < nlev:
                                cp('act', Ph[hp][0:n, :, 0:n], pPs[hp][0:n, 0:2 * n].re("p (h n) -> p h n", h=2))
                                cp('act', PTh[hp][0:n, :, 0:n], pPs[hp][0:n, 256:256 + 2 * n].re("p (h n) -> p h n", h=2))
                    if c == 0:
                        ddump(Xf[0:n].re("p t f -> p (t f)"), 3200, 1024)
                    if gstop <= 5:
                        break
                    if c + 1 < 16:
                        tokprep(128)
                    pt5 = psn()
                    for h in range(4):
                        mm(pt5[:, h * n:(h + 1) * n], Xf[0:n, h, 128:256], ident[0:n, 0:n])
                    cp('dve', WkT[:, :, 0:n], pt5[:, 0:4 * n].re("p (h n) -> p h n", h=4))
                    pws = psn()
                    for h in range(4):
                        mm(pws[0:n, h * 128:(h + 1) * 128], WkT[:, h, 0:n], Sbf[:, h, :])
                    tt('dve', u_bf[0:n], Xf[0:n, :, 0:128], pws[0:n, :].re("p (h v) -> p h v", h=4), ALU.subtract)
                    for h in range(4):
                        mm(po[:, h * n:(h + 1) * n], u_bf[0:n, h, :], attqT[0:n, h, 0:n], start=True, stop=False)
                        mm(po[:, h * n:(h + 1) * n], Sbf[:, h, :], qgT[:, h, 0:n], start=False, stop=True)
                    pS = psn()
                    for h in range(4):
                        mm(pS[:, h * 128:(h + 1) * 128], kw_tok[0:n, h, :], u_bf[0:n, h, :])
                    tt('dve', Sg[:], Sg[:], explast[:].un(2).bc([128, 4, 128]), ALU.mult)
                    tt('dve', Sg[:], pS[:, :].re("p (h v) -> p h v", h=4), Sg[:], ALU.add)
                    cp('act', Sbf[:], Sg[:])
                    if c == 15:
                        dma('sp', O['p_gdn'].rearrange("h k v -> k h v"), Sg[:], d_out)
                    if c == 0:
                        ddump(u_bf[0:n].re("p t f -> p (t f)"), 4900, 512)
                        cp('act', osb[:, 0:4 * n], po[:, 0:4 * n])
                        ddump(osb[:, 0:4 * n], 5500, 512)
                post(15)
                psrot['set'] = list(range(8))
                S.barrier()
                pst.close()
                if gdn == 'prompt':
                    return

                with ExitStack() as sst:
                    c = 16
                    c0, n = CH[c]
                    HX = sb(sst, [128, 12, 4, NS], BF16, 'HXg')
                    scv = [sb(sst, [3 * NS, 512], F32, 'scvg') for _ in range(2)]
                    dma('sp', O['s_gdnc'][:, 0:2, :], I['sgdnc'][:, 1:3, :], d_out)
                    pxs = pre(c)
                    for b3 in range(3):
                        cp(('act', 'dve', 'act')[b3], HX[:, b3 * 4:(b3 + 1) * 4, 3, :], pxs[b3][:, 0:4 * n].re("p (t n) -> p t n", t=4))
                    for b3 in range(3):
                        ph_ = psn()
                        dma('sp', scv[b3 % 2][:], I['sgdnc'].rearrange("s k f -> (s k) f")[:, b3 * 512:(b3 + 1) * 512], d_st[6 + b3 % 2])
                        for j in range(4):
                            tr(ph_[:, j * 48:(j + 1) * 48], scv[b3 % 2][:, j * 128:(j + 1) * 128], ident[0:48, 0:48])
                        cp('dve', HX[:, b3 * 4:(b3 + 1) * 4, 0:3, :], ph_[:, 0:4 * 48].re("p (t s k) -> p t k s", t=4, k=3))
                    conv(n, lambda t, k: HX[:, t, k, :])
                    conv_state_out(T, NS, O['s_gdnc'][:, 2, :])
                    tokprep(n)
                    gam_s = sb(sst, [NS, 4], F32, 'gam_s')
                    act(gam_s[:], g_tok[0:NS, :], AF.Exp)
                    Dg = sb(sst, [NS, 2, 4, NS], F32, 'Dg')
                    idf16 = ident[0:NS, 0:NS].un(1).bc([NS, 4, NS])
                    tt('dve', Dg[:, 0], idf16, beta_tok[0:NS, :].un(2).bc([NS, 4, NS]), ALU.mult)
                    tt('dve', Dg[:, 1], idf16, gam_s[:].un(2).bc([NS, 4, NS]), ALU.mult)
                    pbc = psn()
                    mm(pbc[:, 0:128], ones_f[0:NS, :], Dg[:].re("p a h s -> p (a h s)"))
                    bgb = sb(sst, [128, 2, 4, NS], F32, 'bgb')
                    cp('dve', bgb[:], pbc[:, 0:128].re("p (a h s) -> p a h s", a=2, h=4))
                    knTf = sb(sst, [128, 4, NS], F32, 'knTf')
                    qnTf = sb(sst, [128, 4, NS], F32, 'qnTf')
                    cp('dve', knTf[:], knT[:, :, 0:NS])
                    cp('dve', qnTf[:], qnT[:, :, 0:NS])
                    GS = 2
                    KdG = sb(sst, [NS, GS, 512], BF16, 'KdG')
                    Ssl = [sb(sst, [128, GS, 4, 128], F32, 'Sslg') for _ in range(2)]
                    uT_s = sb(sst, [128, 4, NS], F32, 'uT_s')
                    t2_s = sb(sst, [128, 4, NS], F32, 't2_s')
                    u_tok = sb(sst, [NS, 4, 128], BF16, 'u_tok')
                    uTb = sb(sst, [128, 4, NS], BF16, 'uTb')
                    po = psb[7]
                    pks = psb[6]
                    psrot['set'] = list(range(6))
                    for gi in range(NS // GS):
                        b = gi % 2
                        s0 = gi * GS
                        for si in range(GS):
                            dma('sp', Ssl[b][:, si], I['sgdn'][s0 + si].rearrange("h k v -> k h v"), d_st[b])
                        for h in range(4):
                            for si in range(GS):
                                col = h * NS + s0 + si
                                mm(pks[:, col:col + 1], Ssl[b][:, si, h, :], knTf[:, h, s0 + si:s0 + si + 1])
                    tt('dve', t2_s[:], pks[:, 0:4 * NS].re("p (h s) -> p h s", h=4), bgb[:, 1], ALU.mult)
                    tt('dve', t2_s[:], QC[:, 8:12, 0:NS], t2_s[:], ALU.subtract)
                    tt('dve', uT_s[:], t2_s[:], bgb[:, 0], ALU.mult)
                    cp('dve', uTb[:], uT_s[:])
                    ptu = psn()
                    ptub = ptu[:].bitcast(BF16)
                    for h in range(4):
                        tr(ptub[0:NS, h * 128:(h + 1) * 128], uTb[:, h, :], identb[:])
                    cp('dve', u_tok[:], ptub[0:NS, 0:512].re("p (h v) -> p h v", h=4))
                    for gi in range(NS // GS):
                        b = gi % 2
                        s0 = gi * GS
                        for si in range(GS):
                            dma('sp', Ssl[b][:, si], I['sgdn'][s0 + si].rearrange("h k v -> k h v"), d_st[b])
                        tt('pool', KdG[:], kn_tok[0:NS].re("p h k -> p (h k)").un(1).bc([NS, GS, 512]), identb[0:NS, s0:s0 + GS].un(2).bc([NS, GS, 512]), ALU.mult)
                        for si in range(GS):
                            pso = psn()
                            for h in range(4):
                                mm(pso[:, h * 128:(h + 1) * 128], KdG[:, si, h * 128:(h + 1) * 128], u_tok[:, h, :])
                            tt('pool', Ssl[b][:, si], Ssl[b][:, si], bgb[:, 1, :, s0 + si:s0 + si + 1].bc([128, 4, 128]), ALU.mult)
                            tt('dve', Ssl[b][:, si], pso[:, :].re("p (h v) -> p h v", h=4), Ssl[b][:, si], ALU.add)
                            for h in range(4):
                                col = h * NS + s0 + si
                                mm(po[:, col:col + 1], Ssl[b][:, si, h, :], qnTf[:, h, s0 + si:s0 + si + 1])
                            dma('sp', O['s_gdn'][s0 + si].rearrange("h k v -> k h v"), Ssl[b][:, si], d_st[2 + b])
                    pnorm_part(po[:, 0:4 * n], n, ggdn[:, 0:1], gss[0][:, :, 0:n], omix[:, :, 0:n], osb, sqo, t1)
                    out_proj(Wout, omix, n, xch(c))
                    norm_chunk(c, gmlp[:, 1, :], 'dve')
                    psrot['set'] = list(range(8))
                    S.barrier()
                S.barrier()

        FT = {}

        def final_alloc(ph):
            FT['yT'] = sb(ph, [128, 8, 128], F32, 'yT')
            FT['ytok'] = [sb(ph, [128, D], F32, 'ytok') for _ in range(2)]
            FT['rs2'] = sb(ph, [128, 128], F32, 'rs2')

        def final_chunk(c):
            yT, ytok, rs2 = FT['yT'], FT['ytok'], FT['rs2']
            c0, n = CH[c]
            xc = xch(c)
            sq = sq_s[:, :, 0:n]
            tt('dve', sq, xc, xc, ALU.mult)
            pb = psn()
            for k in range(8):
                mm(pb[:, 0:n], ones_bf[:], sq_s[:, k, 0:n], start=(k == 0), stop=(k == 7))
            act(v_s[:, 0:n], pb[:, 0:n], AF.Ln, scale=1.0 / D, bias=EPS)
            act(rs2[:, 0:n], v_s[:, 0:n], AF.Exp, scale=-0.5)
            tt('dve', ntmp[:, :, 0:n], xc, rs2[:, 0:n].un(1).bc([128, 8, n]), ALU.mult)
            tt('dve', yT[:, :, 0:n], ntmp[:, :, 0:n], gfin[:].un(2).bc([128, 8, n]), ALU.mult)
            b = c % 2
            for half in range(2):
                pz = psn()
                for j in range(4):
                    k = half * 4 + j
                    tr(pz[0:n, j * 128:(j + 1) * 128], yT[:, k, 0:n], ident[:])
                cp('act' if half == 0 else 'dve', ytok[b][0:n, half * 512:(half + 1) * 512], pz[0:n, :])
            if c < 16:
                dma('sp', O['yp'][c0:c0 + n, :], ytok[b][0:n, :], d_out)
            else:
                dma('sp', O['ys'], ytok[b][0:n, :], d_out)

        dpre = [S.dsem('pre%d' % i) for i in range(3)]
        with ExitStack() as ws5:
            Win5 = load_wc(ws5, 'win_s5', I['w_in_ab'], 1552, 512, dpre[1])
            Wout5 = load_wr(ws5, 'wout_s5', I['w_out_ab'], 512, dpre[1])
            Wglu5 = sb(ws5, [128, 4, 512], BF16, 'wglu')
            with ExitStack() as wgla:
                Wing = load_wc(wgla, 'win_gla', I['w_in_ab'], 0, 1552, dpre[0])
                Woutg = load_wr(wgla, 'wout_gla', I['w_out_ab'], 0, dpre[0])
                Wgateg = sb(wgla, [16, 256], BF16, 'wgate')
                dma('pool', Wgateg[:], I['w_gla_gate'], dpre[0])
                dma('pool', Wglu5[:], I['w_s5_glu'].rearrange("(k p) n -> p k n", p=128), dpre[1])
                phase0()
                pass_gla((Wing, Woutg, Wgateg))
            pass_s5((Win5, Wout5, Wglu5))
        dump(0)
        if nlayers > 1:
            with ExitStack() as wssd:
                Winssd = load_wc(wssd, 'win_ssd', I['w_in_cd'], 0, 1544, dpre[2])
                mlp_phase(0, do_norm=False, cg_hook=lambda c: norm_chunk(c, gmix[:, 1, :], 'dve'))
                dump(1)
                pass_ssd(Winssd)
            dump(2)
            pass_gdn()
            dump(3)
            mlp_phase(1, do_norm=False, cg_hook=final_chunk, st_hook=final_alloc)
            dump(4)
        else:
            mlp_phase(0, do_norm=False, cg_hook=final_chunk, st_hook=final_alloc)
            dump(1)
        S.barrier()
        print("program: ops=%d waits=%d counts=%s" % (S.nops, S.nwaits, S.cnt))
    return nc


_NC_CACHE = {}


def _prep_inputs(inputs):
    f = lambda a: np.ascontiguousarray(np.asarray(a, dtype=np.float32))
    shared = {
        'norm_mix': f(inputs['norm_mix']), 'norm_mlp': f(inputs['norm_mlp']), 'norm_final': f(inputs['norm_final']),
        'w_up': f(inputs['w_up']), 'w_down': f(inputs['w_down']),
        'w_in_ab': f(inputs['w_in_ab'][0]), 'w_out_ab': f(inputs['w_out_ab'][0]), 'w_gla_gate': f(inputs['w_gla_gate'][0]),
        'b_gla_gate': f(inputs['b_gla_gate'][0]), 'g_gla_norm': f(inputs['g_gla_norm'][0]),
        's5_lam_re': f(inputs['s5_lam_re'][0]).reshape(2048), 's5_lam_im': f(inputs['s5_lam_im'][0]).reshape(2048),
        's5_b_re': f(inputs['s5_b_re'][0]).reshape(2048, 16), 's5_b_im': f(inputs['s5_b_im'][0]).reshape(2048, 16),
        's5_c_re': f(inputs['s5_c_re'][0]).reshape(512, 64), 's5_c_im': f(inputs['s5_c_im'][0]).reshape(512, 64),
        's5_d': f(inputs['s5_d'][0]).reshape(512), 's5_log_dt': f(inputs['s5_log_dt'][0]),
        'w_s5_glu': f(inputs['w_s5_glu'][0]), 'b_s5_glu': f(inputs['b_s5_glu'][0]),
        'w_in_cd': f(inputs['w_in_cd'][0]), 'w_out_cd': f(inputs['w_out_cd'][0]),
        'ssd_conv_w': f(inputs['ssd_conv_w'][0]), 'ssd_conv_b': f(inputs['ssd_conv_b'][0]), 'ssd_dt_bias': f(inputs['ssd_dt_bias'][0]),
        'ssd_a_log': f(inputs['ssd_a_log'][0]), 'ssd_d': f(inputs['ssd_d'][0]), 'ssd_norm': f(inputs['ssd_norm'][0]),
        'gdn_conv_w': f(inputs['gdn_conv_w'][0]), 'gdn_a_log': f(inputs['gdn_a_log'][0]), 'gdn_dt_bias': f(inputs['gdn_dt_bias'][0]),
        'gdn_norm': f(inputs['gdn_norm'][0]),
    }
    maps = []
    for c in range(8):
        s = slice(c * NS, (c + 1) * NS)
        m = dict(shared)
        m['xp'] = f(inputs['x_prompt'][c])
        m['xs'] = f(inputs['x_sample'][s, 0])
        m['sgla'] = f(inputs['state_gla'][0, s])
        m['ss5re'] = f(inputs['state_s5_re'][0, s]).reshape(NS, 2048)
        m['ss5im'] = f(inputs['state_s5_im'][0, s]).reshape(NS, 2048)
        m['sssd'] = f(inputs['state_ssd'][0, s])
        m['sssdc'] = f(inputs['state_ssd_conv'][0, s])
        m['sgdn'] = f(inputs['state_gdn'][0, s])
        m['sgdnc'] = f(inputs['state_gdn_conv'][0, s])
        maps.append(m)
    return maps


def _gather(res):
    R = res.results
    cat = lambda k: np.concatenate([np.asarray(r[k]) for r in R], axis=0)
    stk = lambda k: np.stack([np.asarray(r[k]) for r in R], axis=0)
    y_prompt = stk('yp')
    y_sample = cat('ys').reshape(128, 1, D)
    p_gla = stk('p_gla')[None]
    p_s5re = stk('p_s5re').reshape(8, 32, 64)[None]
    p_s5im = stk('p_s5im').reshape(8, 32, 64)[None]
    p_ssd = stk('p_ssd')[None]
    p_ssdc = stk('p_ssdc')[None]
    p_gdn = stk('p_gdn')[None]
    p_gdnc = stk('p_gdnc')[None]
    s_gla = cat('s_gla')[None]
    s_s5re = cat('s_s5re').reshape(128, 32, 64)[None]
    s_s5im = cat('s_s5im').reshape(128, 32, 64)[None]
    s_ssd = cat('s_ssd')[None]
    s_ssdc = cat('s_ssdc')[None]
    s_gdn = cat('s_gdn')[None]
    s_gdnc = cat('s_gdnc')[None]
    outs = (y_prompt, y_sample, p_gla, p_s5re, p_s5im, p_ssd, p_ssdc, p_gdn, p_gdnc,
            s_gla, s_s5re, s_s5im, s_ssd, s_ssdc, s_gdn, s_gdnc)
    return tuple(np.ascontiguousarray(o, dtype=np.float32) for o in outs)


def kernel(**inputs):
    if 'nc' not in _NC_CACHE:
        _NC_CACHE['nc'] = build_program()
    nc = _NC_CACHE['nc']
    maps = _prep_inputs(inputs)
    res = run_bass_kernel_spmd(nc, maps, core_ids=list(range(8)))
    return _gather(res)
```

```python
import math
from contextlib import ExitStack

import numpy as np
import concourse.bass as bass
import concourse.mybir as mybir
from concourse.bass_utils import run_bass_kernel_spmd

F32 = mybir.dt.float32
BF16 = mybir.dt.bfloat16
I32 = mybir.dt.int32
ALU = mybir.AluOpType
AF = mybir.ActivationFunctionType

EPS = 1e-6
T = 2048
NS = 16
NT = T + NS
D = 1024
IN_AB = 2064
IN_CD = 3600
NEG = -30000.0


class Unit:
    __slots__ = ('w', 'rs')

    def __init__(self):
        self.w = None
        self.rs = {}


class DSem:
    def __init__(self, h, key):
        self.h = h
        self.key = key
        self.count = 0
        self.batch = None


class Sched:
    def __init__(self, nc, es):
        self.nc = nc
        self.es = es
        self.eng = {'pe': nc.tensor, 'dve': nc.vector, 'act': nc.scalar, 'pool': nc.gpsimd, 'sp': nc.sync}
        self.semh = {}
        for k in self.eng:
            self.semh[k] = es.enter_context(nc.semaphore('s_' + k))
        self.cnt = {k: 0 for k in self.eng}
        self.known = {k: {} for k in self.eng}
        self.dsems = []
        self.nwaits = 0
        self.nops = 0

    def dsem(self, name=None):
        key = 'd%d' % len(self.dsems)
        h = self.es.enter_context(self.nc.semaphore(name or key))
        self.semh[key] = h
        ds = DSem(h, key)
        self.dsems.append(ds)
        return ds

    def _val(self, ev):
        vr = ev[1]
        if vr[0] is None:
            ds = vr[1]
            vr[0] = ds.count
            ds.batch = None
        return vr[0]

    def _need(self, e, ev, needs, skip_same=False):
        if ev is None:
            return
        k = ev[0]
        if skip_same and k == e:
            return
        v = self._val(ev)
        if self.known[e].get(k, 0) >= v:
            return
        if k in needs and needs[k][0] >= v:
            return
        needs[k] = (v, ev[2])

    def _emit_waits(self, e, needs):
        kn = self.known[e]
        for k, (v, snap) in needs.items():
            if kn.get(k, 0) >= v:
                continue
            self.eng[e].wait_ge(self.semh[k], v)
            self.nwaits += 1
            kn[k] = v
            if snap:
                for k2, v2 in snap.items():
                    if kn.get(k2, 0) < v2:
                        kn[k2] = v2

    def _deps(self, e, reads, writes, isdma=False):
        needs = {}
        for u in reads:
            self._need(e, u.w, needs)
        skip = (e == 'pe') and not isdma
        for u in writes:
            self._need(e, u.w, needs, skip_same=skip)
            for ev in u.rs.values():
                self._need(e, ev, needs, skip_same=skip)
        self._emit_waits(e, needs)

    def op(self, e, fn, reads=(), writes=(), inc=True):
        self._deps(e, reads, writes)
        ins = fn()
        self.nops += 1
        if inc:
            self.cnt[e] += 1
            ins.then_inc(self.semh[e], 1)
            val = self.cnt[e]
        else:
            val = self.cnt[e] + 1
        ev = (e, [val], dict(self.known[e]) if inc else None)
        for u in writes:
            u.w = ev
            u.rs = {}
        for u in reads:
            old = u.rs.get(e)
            if old is None or old[1][0] <= val:
                u.rs[e] = ev
        return ins

    def dma(self, q, out, in_, ds, reads=(), writes=()):
        self._deps(q, reads, writes, isdma=True)
        kn = self.known[q]
        if ds.batch is None:
            if ds.count > 0 and kn.get(ds.key, 0) < ds.count:
                self.eng[q].wait_ge(ds.h, ds.count)
                self.nwaits += 1
                kn[ds.key] = ds.count
            ds.batch = [None, ds]
        ins = self.eng[q].dma_start(out=out, in_=in_)
        ins.then_inc(ds.h, 16)
        ds.count += 16
        ev = (ds.key, ds.batch, dict(kn))
        for u in writes:
            u.w = ev
            u.rs = {}
        for u in reads:
            u.rs[ds.key] = ev
        return ins

    def barrier(self):
        for ds in self.dsems:
            if ds.batch is not None:
                ds.batch[0] = ds.count
                ds.batch = None
        for e in self.eng:
            needs = {}
            for k in self.eng:
                if k != e and self.cnt[k] > 0:
                    needs[k] = (self.cnt[k], None)
            for ds in self.dsems:
                if ds.count > 0:
                    needs[ds.key] = (ds.count, None)
            self._emit_waits(e, needs)


class V:
    __slots__ = ('ap', 'units')

    def __init__(self, ap, units):
        self.ap = ap
        self.units = units

    def __getitem__(self, idx):
        return V(self.ap[idx], self.units)

    def re(self, pat, **kw):
        return V(self.ap.rearrange(pat, **kw), self.units)

    def bc(self, shape):
        return V(self.ap.to_broadcast(list(shape)), self.units)

    def un(self, d):
        return V(self.ap.unsqueeze(d), self.units)

    def bitcast(self, dt):
        return V(self.ap.bitcast(dt), self.units)


class Tl:
    def __init__(self, t):
        self.t = t
        self.u = Unit()

    def __getitem__(self, idx):
        return V(self.t[idx], (self.u,))

    def sub(self, unit, idx):
        return V(self.t[idx], (unit,))


class Tl2(Tl):
    def __init__(self, t):
        Tl.__init__(self, t)
        self.us = (Unit(), Unit())

    def __getitem__(self, idx):
        return V(self.t[idx], self.us)

    def half(self, hp, k):
        return V(self.t[:, k * hp:k * (hp + 1)], (self.us[hp],))


class WB:
    def __init__(self, t, blocks):
        self.t = t
        self.blocks = blocks
        self.units = [Unit() for _ in blocks]

    def __getitem__(self, idx):
        c0 = idx[2].start
        for (lo, hi), u in zip(self.blocks, self.units):
            if lo <= c0 < hi:
                return V(self.t[idx], (u,))
        raise KeyError(c0)


def _u(*vs):
    out = []
    for v in vs:
        if isinstance(v, V):
            for u in v.units:
                if u not in out:
                    out.append(u)
    return out


def _ap(v):
    return v.ap if isinstance(v, V) else v


def build_program(debug=False, nlayers=2, do_mlp=True, gdn='full', mlp1=True, gstop=99):
    nc = bass.Bass("TRN2", target_bir_lowering=False)

    def din(name, shape):
        return nc.dram_tensor(name, list(shape), F32, kind="ExternalInput").ap()

    def dout(name, shape):
        return nc.dram_tensor(name, list(shape), F32, kind="ExternalOutput").ap()

    I = {}
    for name, shape in [
        ('xp', (T, D)), ('xs', (NS, D)), ('sgla', (NS, 4, 64, 128)), ('ss5re', (NS, 2048)), ('ss5im', (NS, 2048)),
        ('sssd', (NS, 8, 64, 128)), ('sssdc', (NS, 3, 1024)), ('sgdn', (NS, 4, 128, 128)), ('sgdnc', (NS, 3, 1536)),
        ('norm_mix', (2, D)), ('norm_mlp', (2, D)), ('norm_final', (D,)), ('w_up', (2, D, 4096)), ('w_down', (2, 4096, D)),
        ('w_in_ab', (D, IN_AB)), ('w_out_ab', (D, D)), ('w_gla_gate', (16, 256)), ('b_gla_gate', (256,)), ('g_gla_norm', (128,)),
        ('s5_lam_re', (2048,)), ('s5_lam_im', (2048,)), ('s5_b_re', (2048, 16)), ('s5_b_im', (2048, 16)),
        ('s5_c_re', (512, 64)), ('s5_c_im', (512, 64)), ('s5_d', (512,)), ('s5_log_dt', (32,)),
        ('w_s5_glu', (512, 512)), ('b_s5_glu', (512,)), ('w_in_cd', (D, IN_CD)), ('w_out_cd', (D, D)),
        ('ssd_conv_w', (4, 1024)), ('ssd_conv_b', (1024,)), ('ssd_dt_bias', (8,)), ('ssd_a_log', (8,)), ('ssd_d', (8,)),
        ('ssd_norm', (512,)), ('gdn_conv_w', (4, 1536)), ('gdn_a_log', (4,)), ('gdn_dt_bias', (4,)), ('gdn_norm', (128,)),
    ]:
        I[name] = din(name, shape)
    O = {}
    for name, shape in [
        ('yp', (T, D)), ('ys', (NS, D)), ('p_gla', (4, 64, 128)), ('p_s5re', (16, 128)), ('p_s5im', (16, 128)),
        ('p_ssd', (8, 64, 128)), ('p_ssdc', (3, 1024)), ('p_gdn', (4, 128, 128)), ('p_gdnc', (3, 1536)),
        ('s_gla', (NS, 4, 64, 128)), ('s_s5re', (NS, 2048)), ('s_s5im', (NS, 2048)), ('s_ssd', (NS, 8, 64, 128)),
        ('s_ssdc', (NS, 3, 1024)), ('s_gdn', (NS, 4, 128, 128)), ('s_gdnc', (NS, 3, 1536)),
    ]:
        O[name] = dout(name, shape)
    if debug:
        O['dbg'] = dout('dbg', (6, 128, 8, NT))

    es = ExitStack()
    with es:
        es.enter_context(nc.allow_non_contiguous_dma(reason="small parameter / state layouts"))
        S = Sched(nc, es)
        cnt = [0]

        def sb(st, shape, dt, name=None):
            cnt[0] += 1
            return Tl(st.enter_context(nc.sbuf_tensor("%s_%d" % (name or 't', cnt[0]), list(shape), dt)))

        def mm(out, lhsT, rhs, start=True, stop=True):
            S.op('pe', lambda: nc.tensor.matmul(out.ap, lhsT=lhsT.ap, rhs=rhs.ap, start=start, stop=stop),
                 reads=_u(lhsT, rhs), writes=_u(out), inc=stop)

        def tr(out, in_, ident):
            S.op('pe', lambda: nc.tensor.transpose(out=out.ap, in_=in_.ap, identity=ident.ap), reads=_u(in_, ident), writes=_u(out))

        def act(out, in_, func, bias=None, scale=None, accum=None):
            kw = {}
            if bias is not None:
                kw['bias'] = _ap(bias)
            if scale is not None:
                kw['scale'] = _ap(scale)
            if accum is not None:
                kw['accum_out'] = accum.ap
            S.op('act', lambda: nc.scalar.activation(out=out.ap, in_=in_.ap, func=func, **kw),
                 reads=_u(in_, bias, scale), writes=_u(out, accum))

        def E(e):
            return {'dve': nc.vector, 'pool': nc.gpsimd}[e]

        def tt(e, out, in0, in1, op):
            S.op(e, lambda: E(e).tensor_tensor(out=out.ap, in0=in0.ap, in1=in1.ap, op=op), reads=_u(in0, in1), writes=_u(out))

        def ts(e, out, in0, s1, op0, s2=None, op1=None):
            kw = dict(out=out.ap, in0=in0.ap, scalar1=_ap(s1), scalar2=_ap(s2), op0=op0)
            if op1 is not None:
                kw['op1'] = op1
            S.op(e, lambda: E(e).tensor_scalar(**kw), reads=_u(in0, s1, s2), writes=_u(out))

        def stt(out, in0, scalar, in1, op0, op1, accum=None):
            kw = {}
            if accum is not None:
                kw['accum_out'] = accum.ap
            S.op('dve', lambda: nc.vector.scalar_tensor_tensor(out=out.ap, in0=in0.ap, scalar=_ap(scalar), in1=in1.ap, op0=op0, op1=op1, **kw),
                 reads=_u(in0, scalar, in1), writes=_u(out, accum))

        def cp(e, out, in_):
            if e == 'act':
                S.op('act', lambda: nc.scalar.copy(out=out.ap, in_=in_.ap), reads=_u(in_), writes=_u(out))
            else:
                S.op(e, lambda: E(e).tensor_copy(out=out.ap, in_=in_.ap), reads=_u(in_), writes=_u(out))

        def scan(out, d0, d1, init):
            S.op('dve', lambda: nc.vector.tensor_tensor_scan(out=out.ap, data0=d0.ap, data1=d1.ap, initial=_ap(init), op0=ALU.mult, op1=ALU.add),
                 reads=_u(d0, d1, init), writes=_u(out))

        def memset(e, out, val):
            S.op(e, lambda: E(e).memset(out.ap, val), writes=_u(out))

        def red(out, in_, op=ALU.add):
            S.op('dve', lambda: nc.vector.tensor_reduce(out=out.ap, in_=in_.ap, axis=mybir.AxisListType.X, op=op), reads=_u(in_), writes=_u(out))

        def dma(q, out, in_, ds):
            S.dma(q, _ap(out), _ap(in_), ds, reads=_u(in_), writes=_u(out))

        P = es
        xT = sb(P, [128, 8, NT], F32, 'xT')
        xun = [Unit() for _ in range(17)]
        CH = [(c * 128, 128) for c in range(16)] + [(T, NS)]

        def xch(c):
            c0, n = CH[c]
            return V(xT.t[:, :, c0:c0 + n], (xun[c],))

        def xcols(c0, n):
            us = tuple(xun[c] for c in range(17) if CH[c][0] < c0 + n and CH[c][0] + CH[c][1] > c0)
            return V(xT.t[:, :, c0:c0 + n], us)

        psb = [Tl(es.enter_context(nc.psum_tensor("ps%d" % i, [128, 512], F32))) for i in range(8)]
        psrot = {'set': list(range(8)), 'i': 0}

        def psn():
            s = psrot['set']
            b = psb[s[psrot['i'] % len(s)]]
            psrot['i'] += 1
            return b

        d_init = S.dsem('init')
        d_out = S.dsem('out')
        d_w = [S.dsem('w%d' % i) for i in range(6)]
        d_st = [S.dsem('st%d' % i) for i in range(8)]

        ident = sb(P, [128, 128], F32, 'ident')
        identb = sb(P, [128, 128], BF16, 'identb')
        ones_bf = sb(P, [128, 128], BF16, 'ones')
        ones_f = sb(P, [128, 128], F32, 'onesf')
        triu = sb(P, [128, 128], F32, 'triu')
        mask01 = sb(P, [128, 128], BF16, 'mask01')
        negm_f = sb(P, [128, 128], F32, 'negmf')
        posm_f = sb(P, [128, 128], F32, 'posmf')
        neghalf_ = sb(P, [128, 1], F32, 'neghalf')
        sq_s = sb(P, [128, 8, 128], BF16, 'sq_s')
        v_s = sb(P, [128, 512], F32, 'v_s')
        rstd_s = sb(P, [128, 512], F32, 'rstd_s')
        ntmp = sb(P, [128, 8, 128], BF16, 'ntmp')
        gmix = sb(P, [128, 2, 8], F32, 'gmix')
        gmlp = sb(P, [128, 2, 8], F32, 'gmlp')
        gfin = sb(P, [128, 8], F32, 'gfin')

        memset('pool', ident[:], 1.0)
        S.op('pool', lambda: nc.gpsimd.affine_select(out=ident.t[:], in_=ident.t[:], pattern=[[-1, 128]], compare_op=ALU.is_equal,
                                                      fill=0.0, base=0, channel_multiplier=1), reads=[ident.u], writes=[ident.u])
        cp('dve', identb[:], ident[:])
        memset('pool', ones_bf[:], 1.0)
        memset('pool', ones_f[:], 1.0)
        memset('dve', neghalf_[:], -0.5)
        memset('pool', triu[:], 1.0)
        S.op('pool', lambda: nc.gpsimd.affine_select(out=triu.t[:], in_=triu.t[:], pattern=[[1, 128]], compare_op=ALU.is_ge,
                                                      fill=0.0, base=0, channel_multiplier=-1), reads=[triu.u], writes=[triu.u])
        cp('dve', mask01[:], triu[:])
        ts('dve', negm_f[:], triu[:], -1.0, ALU.add, -NEG, ALU.mult)
        memset('pool', posm_f[:], 1.0)
        S.op('pool', lambda: nc.gpsimd.affine_select(out=posm_f.t[:], in_=posm_f.t[:], pattern=[[-1, 128]], compare_op=ALU.is_gt,
                                                      fill=0.0, base=0, channel_multiplier=1), reads=[posm_f.u], writes=[posm_f.u])
        ts('dve', posm_f[:], posm_f[:], -1.0, ALU.add, NEG, ALU.mult)

        dma('sp', gmix[:], I['norm_mix'].rearrange("l (k p) -> p l k", p=128), d_init)
        dma('sp', gmlp[:], I['norm_mlp'].rearrange("l (k p) -> p l k", p=128), d_init)
        dma('sp', gfin[:], I['norm_final'].rearrange("(k p) -> p k", p=128), d_init)

        def ddump(v, off, width):
            if debug:
                np_ = v.ap.shape[0]
                dst = O['dbg'][5].rearrange("p k n -> p (k n)")[0:np_, off:off + width]
                dma('pool', dst, v, d_out)

        def dump(i):
            if debug:
                dma('sp', O['dbg'][i], xcols(0, NT), d_out)

        def rmsnorm_fm(xv, gv, hT, n, K=8, eng='pool'):
            sq = sq_s[:, 0:K, 0:n]
            tt(eng, sq, xv, xv, ALU.mult)
            pb = psn()
            for k in range(K):
                mm(pb[:, 0:n], ones_bf[:], sq_s[:, k, 0:n], start=(k == 0), stop=(k == K - 1))
            act(v_s[:, 0:n], pb[:, 0:n], AF.Ln, scale=1.0 / (128 * K), bias=EPS)
            act(rstd_s[:, 0:n], v_s[:, 0:n], AF.Exp, scale=-0.5)
            tmp = ntmp[:, 0:K, 0:n]
            tt('dve', tmp, xv, rstd_s[:, 0:n].un(1).bc([128, K, n]), ALU.mult)
            tt(eng, hT, tmp, gv.un(2).bc([128, K, n]), ALU.mult)

        def proj(Wt, c0, M, hT, out):
            for k in range(8):
                mm(out, Wt[:, k, c0:c0 + M], hT[:, k, :], start=(k == 0), stop=(k == 7))

        def projA(hT, Wt, c0, N, out):
            for k in range(8):
                mm(out, hT[:, k, :], Wt[:, k, c0:c0 + N], start=(k == 0), stop=(k == 7))

        def pnorm_part(po, n, gcol, gate, out, osb, sqo, t1, H=4):
            n4 = H * n
            cp('act', osb[:, 0:n4], po)
            act(sqo[:, 0:n4], po, AF.Square)
            pn = psn()
            mm(pn[:, 0:n4], ones_bf[:], sqo[:, 0:n4])
            act(v_s[:, 0:n4], pn[:, 0:n4], AF.Ln, scale=1.0 / 128, bias=EPS)
            act(rstd_s[:, 0:n4], v_s[:, 0:n4], AF.Exp, scale=-0.5)
            tt('dve', t1[:, 0:n4], osb[:, 0:n4], rstd_s[:, 0:n4], ALU.mult)
            stt(out, t1[:, 0:n4].re("p (h n) -> p h n", h=H), gcol, gate, ALU.mult, ALU.mult)

        def load_w(st, name, src, ncols, dsem, nsplit=8):
            w = sb(st, [128, 8, ncols], BF16, name)
            sv = src.rearrange("(k p) n -> p k n", p=128)
            for k in range(8):
                dma('pool', w[:, k, :], sv[:, k, :], dsem)
            return w

        hT_all = sb(P, [128, 8, NT], BF16, 'hTall')
        hun = [Unit() for _ in range(17)]

        def hcols(c0, n):
            us = tuple(hun[c] for c in range(17) if CH[c][0] < c0 + n and CH[c][0] + CH[c][1] > c0)
            return V(hT_all.t[:, :, c0:c0 + n], us)

        def hch(c):
            return hcols(*CH[c])

        def norm_chunk(c, gv, eng='pool'):
            rmsnorm_fm(xch(c), gv, hch(c), CH[c][1], eng=eng)

        def norm_all(gv):
            for c in range(17):
                norm_chunk(c, gv)

        def phase0():
            with ExitStack() as ph:
                xin = [sb(ph, [128, D], F32, 'xin') for _ in range(2)]
                xin_s = sb(ph, [NS, D], F32, 'xins')
                dx = [S.dsem('x0'), S.dsem('x1')]
                dma('sp', xin_s[:], I['xs'], d_init)
                for blk in range(16):
                    b = blk % 2
                    dma('sp', xin[b][:], I['xp'][blk * 128:(blk + 1) * 128, :], dx[b])
                    for half in range(2):
                        pb = psn()
                        for j in range(4):
                            k = half * 4 + j
                            tr(pb[:, j * 128:(j + 1) * 128], xin[b][:, k * 128:(k + 1) * 128], ident[:])
                        cp('act' if half == 0 else 'dve', xch(blk)[:, half * 4:(half + 1) * 4, :], pb[:].re("p (a b) -> p a b", a=4))
                    norm_chunk(blk, gmix[:, 0, :], 'dve')
                pb = psn()
                for k in range(8):
                    tr(pb[:, k * NS:(k + 1) * NS], xin_s[:, k * 128:(k + 1) * 128], ident[0:NS, 0:NS])
                cp('dve', xch(16), pb[:, 0:8 * NS].re("p (k s) -> p k s", k=8))
                norm_chunk(16, gmix[:, 0, :], 'dve')
                S.barrier()

        def mlp_phase(layer, do_norm, cg_hook=None, st_hook=None):
            with ExitStack() as ph:
                abuf = [sb(ph, [128, 4, NT], BF16, 'abuf') for _ in range(2)]
                wup = [sb(ph, [128, 8, 512], BF16, 'wup') for _ in range(2)]
                wdn = [sb(ph, [128, 4, 1024], BF16, 'wdn') for _ in range(2)]
                rbuf = [sb(ph, [128, 512], BF16, 'rbuf') for _ in range(3)]
                rs_s = sb(ph, [128, 4, NS], BF16, 'rss')
                pss = psb[7]
                psrot['set'] = list(range(7))
                us_up = pss.u
                us_dn = pss.u
                wupv = I['w_up'][layer].rearrange("(k p) n -> p k n", p=128)
                wdnv = I['w_down'][layer].rearrange("(k p) n -> p k n", p=128)

                def loadw(f):
                    b = f % 2
                    dma('pool', wup[b][:], wupv[:, :, f * 512:(f + 1) * 512], d_w[b])
                    dma('pool', wdn[b][:], wdnv[:, f * 4:(f + 1) * 4, :], d_w[2 + b])
                loadw(0)
                if do_norm:
                    norm_all(gmlp[:, layer, :])
                if st_hook is not None:
                    st_hook(ph)
                ri = 0
                for f in range(8):
                    b = f % 2
                    if f + 1 < 8:
                        loadw(f + 1)
                    for m in range(4):
                        for half in range(2):
                            pbs = [psn(), psn()]
                            for k in range(8):
                                lw = wup[b][:, k, m * 128:(m + 1) * 128]
                                for j in range(2):
                                    c0 = half * 1024 + j * 512
                                    mm(pbs[j][:, :], lw, hcols(c0, 512)[:, k, :], start=(k == 0), stop=(k == 7))
                                if half == 1:
                                    mm(V(pss.t[:, m * NS:(m + 1) * NS], (us_up,)), lw, hcols(T, NS)[:, k, :], start=(k == 0), stop=(k == 7))
                            for j in range(2):
                                c0 = half * 1024 + j * 512
                                r = rbuf[ri % 3]
                                ri += 1
                                act(r[:], pbs[j][:], AF.Relu)
                                tt('pool', abuf[b][:, m, c0:c0 + 512], r[:], r[:], ALU.mult)
                    act(rs_s[:], V(pss.t[:, 0:4 * NS], (us_up,)).re("p (m s) -> p m s", m=4), AF.Relu)
                    tt('pool', abuf[b][:, :, T:NT], rs_s[:], rs_s[:], ALU.mult)
                    for cg in range(4):
                        for mo in range(8):
                            pb = psn()
                            for k in range(4):
                                mm(pb[:, :], wdn[b][:, k, mo * 128:(mo + 1) * 128], abuf[b][:, k, cg * 512:(cg + 1) * 512], start=(k == 0), stop=(k == 3))
                            xv = xcols(cg * 512, 512)[:, mo, :]
                            tt('dve', xv, pb[:, :], xv, ALU.add)
                        if f == 7 and cg_hook is not None:
                            for c in range(cg * 4, cg * 4 + 4):
                                cg_hook(c)
                    for mo in range(8):
                        for k in range(4):
                            mm(V(pss.t[:, 64 + mo * NS:64 + (mo + 1) * NS], (us_dn,)), wdn[b][:, k, mo * 128:(mo + 1) * 128], abuf[b][:, k, T:NT],
                               start=(k == 0), stop=(k == 3))
                    xs_ = xch(16)
                    tt('dve', xs_, V(pss.t[:, 64:64 + 8 * NS], (us_dn,)).re("p (m s) -> p m s", m=8), xs_, ALU.add)
                    if f == 7 and cg_hook is not None:
                        cg_hook(16)
                psrot['set'] = list(range(8))
                S.barrier()

        def load_wc(st, name, src, c0, ncols, dsem):
            w = sb(st, [128, 8, ncols], BF16, name)
            sv = src.rearrange("(k p) n -> p k n", p=128)
            for k in range(8):
                dma('pool', w[:, k, :], sv[:, k, c0:c0 + ncols], dsem)
            return w

        def load_wr(st, name, src, r0, dsem):
            w = sb(st, [128, 4, D], BF16, name)
            sv = src[r0:r0 + 512, :].rearrange("(k p) n -> p k n", p=128)
            for k in range(4):
                dma('pool', w[:, k, :], sv[:, k, :], dsem)
            return w

        def out_proj(Wo, om, n, xc):
            for half in range(2):
                pxo = psn()
                for j in range(4):
                    m = half * 4 + j
                    for k in range(4):
                        mm(pxo[:, j * n:(j + 1) * n], Wo[:, k, m * 128:(m + 1) * 128], om[:, k, 0:n], start=(k == 0), stop=(k == 3))
                xv = xc[:, half * 4:(half + 1) * 4, :]
                tt('dve', xv, pxo[:, 0:4 * n].re("p (j n) -> p j n", j=4), xv, ALU.add)

        def pass_gla(pre=None):
            with ExitStack() as ph:
                Win, Wout, Wgate = pre
                bgate = sb(ph, [64, 4], F32, 'bgate')
                dma('sp', bgate[:], I['b_gla_gate'].rearrange("(h k) -> k h", k=64), d_init)
                negb = sb(ph, [64, 4], F32, 'negb')
                ts('dve', negb[:], bgate[:], -1.0, ALU.mult)
                ggla = sb(ph, [128, 1], F32, 'ggla')
                dma('sp', ggla[:], I['g_gla_norm'].rearrange("(p o) -> p o", o=1), d_init)
                glrT = sb(ph, [16, 128], BF16, 'glrT')
                sp_t = sb(ph, [64, 4, 128], F32, 'sp_t')
                cs_t = sb(ph, [64, 4, 128], F32, 'cs_t')
                Ep = sb(ph, [64, 4, 128], F32, 'Ep')
                Em = sb(ph, [64, 4, 128], F32, 'Em')
                qs = sb(ph, [64, 4, 128], BF16, 'qs')
                ks = sb(ph, [64, 4, 128], BF16, 'ks')
                qsf = sb(ph, [64, 4, NS], F32, 'qsf')
                vT = sb(ph, [128, 4, 128], BF16, 'vT')
                sr = sb(ph, [128, 4, 128], BF16, 'sr')
                ones64 = sb(ph, [64, 128], F32, 'ones64')
                memset('pool', ones64[:], 1.0)
                kvtok = sb(ph, [128, 768], BF16, 'kvtok')
                attT = sb(ph, [128, 4, 128], BF16, 'attT')
                Sg = sb(ph, [64, 4, 128], F32, 'Sg')
                Sbf = sb(ph, [64, 4, 128], BF16, 'Sbf')
                tmpS = sb(ph, [64, 4, 128], F32, 'tmpS')
                memset('pool', Sg[:], 0.0)
                memset('pool', Sbf[:], 0.0)
                osb = sb(ph, [128, 512], F32, 'osb')
                sqo = sb(ph, [128, 512], BF16, 'sqo')
                t1 = osb
                omix = sb(ph, [128, 4, 128], BF16, 'omix')
                ktok_s = sb(ph, [NS, 4, 64], BF16, 'ktoks')
                vtok_s = sb(ph, [NS, 4, 128], BF16, 'vtoks')
                Kd = sb(ph, [NS, 2, 256], BF16, 'Kd')
                a_s = sb(ph, [64, 4, NS], F32, 'a_s')
                identb16 = identb[0:NS, 0:NS]
                sgl = [sb(ph, [64, 2, 4, 128], F32, 'sgl') for _ in range(2)]
                sgl2 = sgl

                qs2 = [qs, sb(ph, [64, 4, 128], BF16, 'qs2')]
                kv2 = [kvtok, sb(ph, [128, 768], BF16, 'kvtok2')]
                att2 = [attT, sb(ph, [128, 4, 128], BF16, 'attT2')]
                sr2 = [sr, sb(ph, [128, 4, 128], BF16, 'sr2')]
                Ep2 = [Ep, sb(ph, [64, 4, 128], F32, 'Ep2')]

                def gla_front(c):
                    c0, n = CH[c]
                    p = c % 2
                    qs_, kv_, att_, sr_, Ep_ = qs2[p], kv2[p], att2[p], sr2[p], Ep2[p]
                    hTn = hcols(c0, n)
                    pq = psn()
                    for h in range(4):
                        proj(Win, h * 64, 64, hTn, pq[0:64, h * n:(h + 1) * n])
                    pk = psn()
                    for h in range(4):
                        proj(Win, 256 + h * 64, 64, hTn, pk[0:64, h * n:(h + 1) * n])
                    pgl = psn()
                    proj(Win, 1024, 16, hTn, pgl[0:16, 0:n])
                    cp('act', glrT[:, 0:n], pgl[0:16, 0:n])
                    pg = psn()
                    for h in range(4):
                        mm(pg[0:64, h * n:(h + 1) * n], Wgate[:, h * 64:(h + 1) * 64], glrT[:, 0:n])
                    for h in range(4):
                        act(sp_t[:, h, 0:n], pg[0:64, h * n:(h + 1) * n], AF.Exp, scale=-1.0, bias=negb[:, h:h + 1])
                    act(sp_t[:, :, 0:n], sp_t[:, :, 0:n], AF.Ln, bias=1.0)
                    pqv = pq[0:64, 0:4 * n].re("p (h n) -> p h n", h=4)
                    pkv = pk[0:64, 0:4 * n].re("p (h n) -> p h n", h=4)
                    for h in range(4):
                        scan(cs_t[:, h, 0:n], ones64[:, 0:n], sp_t[:, h, 0:n], 0.0)
                    act(Ep_[:, :, 0:n], cs_t[:, :, 0:n], AF.Exp, scale=-1.0 / 16)
                    act(Em[:, :, 0:n], cs_t[:, :, 0:n], AF.Exp, scale=1.0 / 16)
                    stt(qs_[:, :, 0:n], pqv, 0.125, Ep_[:, :, 0:n], ALU.mult, ALU.mult)
                    tt('dve', ks[:, :, 0:n], pkv, Em[:, :, 0:n], ALU.mult)
                    pv = psn()
                    for h in range(4):
                        proj(Win, 512 + h * 128, 128, hTn, pv[:, h * n:(h + 1) * n])
                    cp('act', vT[:, :, 0:n], pv[:, 0:4 * n].re("p (h n) -> p h n", h=4))
                    pr = psn()
                    for h in range(4):
                        proj(Win, 1040 + h * 128, 128, hTn, pr[:, h * n:(h + 1) * n])
                    act(sr_[:, :, 0:n], pr[:, 0:4 * n].re("p (h n) -> p h n", h=4), AF.Silu)
                    pt = psn()
                    ptb = pt[:].bitcast(BF16)
                    for h in range(4):
                        tr(ptb[0:n, h * 64:(h + 1) * 64], ks[:, h, 0:n], identb[0:64, 0:64])
                    for h in range(4):
                        tr(ptb[0:n, 256 + h * 128:256 + (h + 1) * 128], vT[:, h, 0:n], identb[:])
                    cp('dve', kv_[0:n, :], ptb[0:n, 0:768])
                    pa = psn()
                    for h in range(4):
                        mm(pa[0:n, h * n:(h + 1) * n], ks[:, h, 0:n], qs_[:, h, 0:n])
                    tt('dve', att_[0:n, :, 0:n], pa[0:n, 0:4 * n].re("p (h n) -> p h n", h=4), mask01[0:n, 0:n].un(1).bc([n, 4, n]), ALU.mult)

                def gla_back(c):
                    c0, n = CH[c]
                    p = c % 2
                    qs_, kv_, att_, sr_, Ep_ = qs2[p], kv2[p], att2[p], sr2[p], Ep2[p]
                    po = psn()
                    for h in range(4):
                        mm(po[:, h * n:(h + 1) * n], kv_[0:n, 256 + h * 128:256 + (h + 1) * 128], att_[0:n, h, 0:n], start=True, stop=False)
                        mm(po[:, h * n:(h + 1) * n], Sbf[:, h, :], qs_[:, h, 0:n], start=False, stop=True)
                    pS = psn()
                    for h in range(4):
                        mm(pS[0:64, h * 128:(h + 1) * 128], kv_[0:n, h * 64:(h + 1) * 64], kv_[0:n, 256 + h * 128:256 + (h + 1) * 128])
                    tt('dve', tmpS[:], pS[0:64, :].re("p (h v) -> p h v", h=4), Sg[:], ALU.add)
                    tt('pool', Sg[:], tmpS[:], Ep_[:, :, n - 1:n].bc([64, 4, 128]), ALU.mult)
                    cp('act', Sbf[:], Sg[:])
                    if c == 15:
                        dma('sp', O['p_gla'].rearrange("h k v -> k h v"), Sg[:], d_out)
                    pnorm_part(po[:, 0:4 * n], n, ggla[:, 0:1], sr_[:, :, 0:n], omix[:, :, 0:n], osb, sqo, t1)
                    out_proj(Wout, omix, n, xch(c))

                gla_front(0)
                for c in range(16):
                    if c + 1 < 16:
                        gla_front(c + 1)
                    gla_back(c)

                for c in range(16, 17):
                    c0, n = CH[c]
                    sample = (c == 16)
                    xc = xch(c)
                    hTn = hcols(c0, n)
                    pq = psn()
                    for h in range(4):
                        proj(Win, h * 64, 64, hTn, pq[0:64, h * n:(h + 1) * n])
                    pk = psn()
                    for h in range(4):
                        proj(Win, 256 + h * 64, 64, hTn, pk[0:64, h * n:(h + 1) * n])
                    pgl = psn()
                    proj(Win, 1024, 16, hTn, pgl[0:16, 0:n])
                    cp('act', glrT[:, 0:n], pgl[0:16, 0:n])
                    pg = psn()
                    for h in range(4):
                        mm(pg[0:64, h * n:(h + 1) * n], Wgate[:, h * 64:(h + 1) * 64], glrT[:, 0:n])
                    for h in range(4):
                        act(sp_t[:, h, 0:n], pg[0:64, h * n:(h + 1) * n], AF.Exp, scale=-1.0, bias=negb[:, h:h + 1])
                    act(sp_t[:, :, 0:n], sp_t[:, :, 0:n], AF.Ln, bias=1.0)
                    pqv = pq[0:64, 0:4 * n].re("p (h n) -> p h n", h=4)
                    pkv = pk[0:64, 0:4 * n].re("p (h n) -> p h n", h=4)
                    if not sample:
                        for h in range(4):
                            scan(cs_t[:, h, 0:n], ones64[:, 0:n], sp_t[:, h, 0:n], 0.0)
                        act(Ep[:, :, 0:n], cs_t[:, :, 0:n], AF.Exp, scale=-1.0 / 16)
                        act(Em[:, :, 0:n], cs_t[:, :, 0:n], AF.Exp, scale=1.0 / 16)
                        stt(qs[:, :, 0:n], pqv, 0.125, Ep[:, :, 0:n], ALU.mult, ALU.mult)
                        tt('dve', ks[:, :, 0:n], pkv, Em[:, :, 0:n], ALU.mult)
                    else:
                        act(a_s[:], sp_t[:, :, 0:n], AF.Exp, scale=-1.0 / 16)
                        act(qsf[:], pqv, AF.Copy, scale=0.125)
                        cp('dve', ks[:, :, 0:n], pkv)
                    pv = psn()
                    for h in range(4):
                        proj(Win, 512 + h * 128, 128, hTn, pv[:, h * n:(h + 1) * n])
                    cp('act', vT[:, :, 0:n], pv[:, 0:4 * n].re("p (h n) -> p h n", h=4))
                    pr = psn()
                    for h in range(4):
                        proj(Win, 1040 + h * 128, 128, hTn, pr[:, h * n:(h + 1) * n])
                    act(sr[:, :, 0:n], pr[:, 0:4 * n].re("p (h n) -> p h n", h=4), AF.Silu)
                    if sample:
                        po = psb[7]
                        psrot['set'] = list(range(7))
                    else:
                        po = psn()
                    pt = psn()
                    ptb = pt[:].bitcast(BF16)
                    for h in range(4):
                        tr(ptb[0:n, h * 64:(h + 1) * 64], ks[:, h, 0:n], identb[0:64, 0:64])
                    for h in range(4):
                        tr(ptb[0:n, 256 + h * 128:256 + (h + 1) * 128], vT[:, h, 0:n], identb[:])
                    if not sample:
                        cp('dve', kvtok[0:n, :], ptb[0:n, 0:768])
                        pa = psn()
                        for h in range(4):
                            mm(pa[0:n, h * n:(h + 1) * n], ks[:, h, 0:n], qs[:, h, 0:n])
                        tt('dve', attT[0:n, :, 0:n], pa[0:n, 0:4 * n].re("p (h n) -> p h n", h=4), mask01[0:n, 0:n].un(1).bc([n, 4, n]), ALU.mult)
                        for h in range(4):
                            mm(po[:, h * n:(h + 1) * n], kvtok[0:n, 256 + h * 128:256 + (h + 1) * 128], attT[0:n, h, 0:n], start=True, stop=False)
                            mm(po[:, h * n:(h + 1) * n], Sbf[:, h, :], qs[:, h, 0:n], start=False, stop=True)
                        pS = psn()
                        for h in range(4):
                            mm(pS[0:64, h * 128:(h + 1) * 128], kvtok[0:n, h * 64:(h + 1) * 64], kvtok[0:n, 256 + h * 128:256 + (h + 1) * 128])
                        tt('dve', tmpS[:], pS[0:64, :].re("p (h v) -> p h v", h=4), Sg[:], ALU.add)
                        tt('pool', Sg[:], tmpS[:], Ep[:, :, n - 1:n].bc([64, 4, 128]), ALU.mult)
                        cp('act', Sbf[:], Sg[:])
                        if c == 15:
                            dma('sp', O['p_gla'].rearrange("h k v -> k h v"), Sg[:], d_out)
                    else:
                        cp('dve', ktok_s[:], ptb[0:NS, 0:256].re("p (h k) -> p h k", h=4))
                        cp('dve', vtok_s[:], ptb[0:NS, 256:768].re("p (h v) -> p h v", h=4))
                        for gi in range(8):
                            b = gi % 2
                            s0 = gi * 2
                            dma('sp', sgl[b][:], I['sgla'][s0:s0 + 2].rearrange("s h k v -> k s h v"), d_st[b])
                            tt('dve', Kd[:], ktok_s[:].re("p h k -> p (h k)").un(1).bc([NS, 2, 256]), identb[0:NS, s0:s0 + 2].un(2).bc([NS, 2, 256]), ALU.mult)
                            pso = [psn(), psn()]
                            for si in range(2):
                                for h in range(4):
                                    mm(pso[si][0:64, h * 128:(h + 1) * 128], Kd[:, si, h * 64:(h + 1) * 64], vtok_s[:, h, :])
                            av = a_s[:].re("p h s -> p s h")[:, s0:s0 + 2, :].un(3).bc([64, 2, 4, 128])
                            tt('pool', sgl2[b][:], sgl[b][:], av, ALU.mult)
                            for si in range(2):
                                tt('dve', sgl2[b][:, si], pso[si][0:64, :].re("p (h v) -> p h v", h=4), sgl2[b][:, si], ALU.add)
                            for si in range(2):
                                for h in range(4):
                                    col = h * NS + s0 + si
                                    mm(po[:, col:col + 1], sgl2[b][:, si, h, :], qsf[:, h, s0 + si:s0 + si + 1])
                            dma('sp', O['s_gla'][s0:s0 + 2].rearrange("s h k v -> k s h v"), sgl2[b][:], d_st[2 + b])
                    pnorm_part(po[:, 0:4 * n], n, ggla[:, 0:1], sr[:, :, 0:n], omix[:, :, 0:n], osb, sqo, t1)
                    out_proj(Wout, omix, n, xc)
                    psrot['set'] = list(range(8))
                S.barrier()

        def pass_s5(pre=None):
            with ExitStack() as ph:
                Win, Wout, Wglu = pre
                bglu = sb(ph, [128, 4], F32, 'bglu')
                dma('sp', bglu[:], I['b_s5_glu'].rearrange("(m p) -> p m", p=128), d_init)
                DU = sb(ph, [128, 4], F32, 'DU')
                dma('sp', DU[:], I['s5_d'].rearrange("(u q) -> q u", q=128), d_init)

                def s16(name):
                    return sb(ph, [128, 16], F32, name)
                LR = s16('LR'); LI = s16('LI'); LDT = s16('LDT')
                dma('sp', LR[:], I['s5_lam_re'].rearrange("(t q) -> q t", q=128), d_init)
                dma('sp', LI[:], I['s5_lam_im'].rearrange("(t q) -> q t", q=128), d_init)
                ldv = I['s5_log_dt'].rearrange("(t g) -> g t", g=2)
                for g2 in range(2):
                    dma('sp', LDT[g2 * 64:(g2 + 1) * 64, :], ldv[g2].partition_broadcast(64), d_init)
                DT = s16('DT'); Rm = s16('Rm'); TH = s16('TH'); t16a = s16('t16a'); t16b = s16('t16b'); t16c = s16('t16c')
                AR = s16('AR'); AI_ = s16('AI'); KR = s16('KR'); KI = s16('KI')
                nre = s16('nre'); den = s16('den'); rden = s16('rden')
                c16a = s16('c16a'); c16b = s16('c16b'); Hre = s16('Hre'); Him = s16('Him')
                ginit_re = s16('gire'); ginit_im = s16('giim'); glast_re = s16('glre'); glast_im = s16('glim')
                t16i = sb(ph, [128, 16], I32, 't16i')
                Bpad_re = sb(ph, [128, 16, 128], BF16, 'Bpr')
                Bpad_im = sb(ph, [128, 16, 128], BF16, 'Bpi')
                Cpad_re = sb(ph, [128, 16, 128], BF16, 'Cpr')
                Cpad_imn = sb(ph, [128, 16, 128], BF16, 'Cpi')
                uTs = [sb(ph, [128, 4, 128], BF16, 'uT') for _ in range(2)]
                hre_gs = [[sb(ph, [128, 4, 128], BF16, 'hre') for _ in range(4)] for _ in range(2)]
                him_gs = [[sb(ph, [128, 4, 128], BF16, 'him') for _ in range(4)] for _ in range(2)]
                yv = sb(ph, [128, 4, 128], F32, 'yv')
                ygb = sb(ph, [128, 4, 128], BF16, 'ygb')
                sg_t = sb(ph, [128, 4, 128], BF16, 'sg')
                omix = sb(ph, [128, 4, 128], BF16, 'omix')
                hout_tok = sb(ph, [NS, 128], F32, 'hout')

                act(DT[:], LDT[:], AF.Exp)
                tt('dve', t16a[:], LR[:], DT[:], ALU.mult)
                act(Rm[:], t16a[:], AF.Exp)
                tt('dve', TH[:], LI[:], DT[:], ALU.mult)
                ts('dve', t16a[:], TH[:], 1.0 / (2 * math.pi), ALU.mult)
                cp('dve', t16i[:], t16a[:])
                cp('dve', t16b[:], t16i[:])
                tt('dve', t16c[:], t16a[:], t16b[:], ALU.subtract)
                ts('dve', TH[:], t16c[:], 2 * math.pi, ALU.mult)

                pst = ExitStack()
                NTAU = 129
                COS = sb(pst, [128, 16, NTAU], F32, 'COS')
                SIN = sb(pst, [128, 16, NTAU], F32, 'SIN')
                with ExitStack() as tmpst:
                    taui = sb(tmpst, [128, NTAU], I32, 'taui')
                    tauf = sb(tmpst, [128, NTAU], F32, 'tauf')
                    S.op('pool', lambda: nc.gpsimd.iota(taui.t[:], pattern=[[1, NTAU]], base=0, channel_multiplier=0), writes=[taui.u])
                    cp('dve', tauf[:], taui[:])
                    U0 = sb(tmpst, [128, 8, NTAU], F32, 'U0')
                    U1 = sb(tmpst, [128, 8, NTAU], F32, 'U1')
                    U2 = sb(tmpst, [128, 8, NTAU], F32, 'U2')
                    UI = sb(tmpst, [128, 8, NTAU], I32, 'UI')
                    for th in range(2):
                        tsl = slice(th * 8, (th + 1) * 8)
                        tt('dve', U0[:], TH[:, tsl].un(2).bc([128, 8, NTAU]), tauf[:].un(1).bc([128, 8, NTAU]), ALU.mult)
                        ts('dve', U0[:], U0[:], 1.0 / (2 * math.pi), ALU.mult)
                        for (dst, off) in ((SIN, 0.0), (COS, 0.25)):
                            ts('dve', U1[:], U0[:], off, ALU.add)
                            cp('dve', UI[:], U1[:])
                            cp('dve', U2[:], UI[:])
                            tt('dve', U1[:], U1[:], U2[:], ALU.subtract)
                            act(dst[:, tsl, :], U1[:], AF.Sin, scale=2 * math.pi)
                    S.barrier()
                tt('dve', AR[:], Rm[:], COS[:, :, 1], ALU.mult)
                tt('dve', AI_[:], Rm[:], SIN[:, :, 1], ALU.mult)
                ts('dve', nre[:], AR[:], -1.0, ALU.add)
                tt('dve', t16a[:], LR[:], LR[:], ALU.mult)
                tt('dve', t16b[:], LI[:], LI[:], ALU.mult)
                tt('dve', den[:], t16a[:], t16b[:], ALU.add)
                S.op('dve', lambda: nc.vector.reciprocal(out=rden.t[:], in_=den.t[:]), reads=[den.u], writes=[rden.u])
                tt('dve', t16a[:], nre[:], LR[:], ALU.mult)
                tt('dve', t16b[:], AI_[:], LI[:], ALU.mult)
                tt('dve', t16c[:], t16a[:], t16b[:], ALU.add)
                tt('dve', KR[:], t16c[:], rden[:], ALU.mult)
                tt('dve', t16a[:], AI_[:], LR[:], ALU.mult)
                tt('dve', t16b[:], nre[:], LI[:], ALU.mult)
                tt('dve', t16c[:], t16a[:], t16b[:], ALU.subtract)
                tt('dve', KI[:], t16c[:], rden[:], ALU.mult)
                with ExitStack() as tmpst:
                    BR = sb(tmpst, [128, 16, 16], F32, 'BR')
                    BI = sb(tmpst, [128, 16, 16], F32, 'BI')
                    dma('sp', BR[:], I['s5_b_re'].rearrange("(t q) h -> q t h", q=128), d_st[4])
                    dma('sp', BI[:], I['s5_b_im'].rearrange("(t q) h -> q t h", q=128), d_st[4])
                    CUr = sb(tmpst, [128, 4, 64], F32, 'CUr')
                    CUi = sb(tmpst, [128, 4, 64], F32, 'CUi')
                    dma('sp', CUr[:], I['s5_c_re'].rearrange("(u q) p -> q u p", q=128), d_st[4])
                    dma('sp', CUi[:], I['s5_c_im'].rearrange("(u q) p -> q u p", q=128), d_st[4])
                    BBR = sb(tmpst, [128, 16, 16], F32, 'BBR')
                    BBI = sb(tmpst, [128, 16, 16], F32, 'BBI')
                    b1 = sb(tmpst, [128, 16, 16], F32, 'b1')
                    b2 = sb(tmpst, [128, 16, 16], F32, 'b2')
                    krb = KR[:].un(2).bc([128, 16, 16])
                    kib = KI[:].un(2).bc([128, 16, 16])
                    tt('dve', b1[:], BR[:], krb, ALU.mult)
                    tt('dve', b2[:], BI[:], kib, ALU.mult)
                    tt('dve', BBR[:], b1[:], b2[:], ALU.subtract)
                    tt('dve', b1[:], BI[:], krb, ALU.mult)
                    tt('dve', b2[:], BR[:], kib, ALU.mult)
                    tt('dve', BBI[:], b1[:], b2[:], ALU.add)
                    mki = sb(tmpst, [128, 4, 4, 8], I32, 'mki')
                    MK = sb(tmpst, [128, 16, 8], F32, 'MK')
                    for g2 in range(2):
                        S.op('pool', lambda g2=g2: nc.gpsimd.iota(mki.t[g2 * 64:(g2 + 1) * 64], pattern=[[0, 4], [-2, 4], [1, 8]], base=-g2, channel_multiplier=0),
                             writes=[mki.u])
                    cp('dve', MK[:], mki[:].re("p a b c -> p (a b) c"))
                    ts('dve', MK[:], MK[:], 0.0, ALU.is_equal)
                    EXr = sb(tmpst, [128, 4, 128], F32, 'EX')
                    EX4 = EXr[:].re("p t (a b) -> p t a b", a=8)
                    EXC4 = EXr[:].re("p t (g q) -> p t g q", g=2)
                    MKC = sb(tmpst, [128, 4, 128], F32, 'MKC')

                    def tr4(dst, scale=None):
                        pb = psn()
                        for q in range(4):
                            tr(pb[:, q * 128:(q + 1) * 128], EXr[:, q, :], ident[:])
                        if scale is None:
                            cp('act', dst, pb[:].re("p (a b) -> p a b", a=4))
                        else:
                            act(dst, pb[:].re("p (a b) -> p a b", a=4), AF.Copy, scale=scale)
                    for tg in range(4):
                        tsl = slice(tg * 4, (tg + 1) * 4)
                        mkb = MK[:, tsl].un(3).bc([128, 4, 8, 16])
                        tt('dve', EX4, mkb, mkb, ALU.mult)
                        tr4(MKC[:])
                        for (src, dst) in ((BBR, Bpad_re), (BBI, Bpad_im)):
                            tt('dve', EX4, src[:, tsl].un(2).bc([128, 4, 8, 16]), mkb, ALU.mult)
                            tr4(dst[:, tsl, :])
                        for (src, dst, sgn) in ((CUr, Cpad_re, 1.0), (CUi, Cpad_imn, -1.0)):
                            tt('dve', EXC4, src[:, tg, :].un(1).un(1).bc([128, 4, 2, 64]), MKC[:].re("p t (g q) -> p t g q", g=2), ALU.mult)
                            tr4(dst[:, tsl, :], scale=sgn)
                    S.barrier()

                class TS:
                    pass
                TD, TP = TS(), TS()
                for T_, nm in ((TD, 'd'), (TP, 'p')):
                    T_.s5b = sb(pst, [128, 4, 128], F32, nm + 's5b')
                    T_.gin_re = sb(pst, [128, 4, 128], F32, nm + 'ginre')
                    T_.gin_im = sb(pst, [128, 4, 128], F32, nm + 'ginim')
                    T_.g_re = sb(pst, [128, 4, 128], F32, nm + 'g_re')
                    T_.g_im = sb(pst, [128, 4, 128], F32, nm + 'g_im')
                memset('pool', ginit_re[:], 0.0)
                memset('pool', ginit_im[:], 0.0)

                def rot(tau, ore, oim):
                    tt('dve', c16a[:], glast_re[:], COS[:, :, tau], ALU.mult)
                    tt('dve', c16b[:], glast_im[:], SIN[:, :, tau], ALU.mult)
                    tt('dve', ore[:], c16a[:], c16b[:], ALU.subtract)
                    tt('dve', c16a[:], glast_im[:], COS[:, :, tau], ALU.mult)
                    tt('dve', c16b[:], glast_re[:], SIN[:, :, tau], ALU.mult)
                    tt('dve', oim[:], c16a[:], c16b[:], ALU.add)

                def pre(c):
                    c0, n = CH[c]
                    uT = uTs[c % 2]
                    pu = psn()
                    for ut in range(4):
                        proj(Win, ut * 128, 128, hcols(c0, n), pu[:, ut * n:(ut + 1) * n])
                    cp('act', uT[:, :, 0:n], pu[:, 0:4 * n].re("p (h n) -> p h n", h=4))

                def post(c):
                    c0, n = CH[c]
                    uT = uTs[c % 2]
                    hre_g, him_g = hre_gs[c % 2], him_gs[c % 2]
                    py = psn()
                    for ut in range(4):
                        for q in range(4):
                            t = ut * 4 + q
                            mm(py[:, ut * n:(ut + 1) * n], Cpad_re[:, t, :], hre_g[ut][:, q, 0:n], start=(q == 0), stop=False)
                            mm(py[:, ut * n:(ut + 1) * n], Cpad_imn[:, t, :], him_g[ut][:, q, 0:n], start=False, stop=(q == 3))
                    tt('pool', yv[:, :, 0:n], uT[:, :, 0:n], DU[:].un(2).bc([128, 4, n]), ALU.mult)
                    tt('dve', yv[:, :, 0:n], py[:, 0:4 * n].re("p (u n) -> p u n", u=4), yv[:, :, 0:n], ALU.add)
                    act(ygb[:, :, 0:n], yv[:, :, 0:n], AF.Gelu_apprx_tanh)
                    pg2 = psn()
                    for m in range(4):
                        for k in range(4):
                            mm(pg2[:, m * n:(m + 1) * n], Wglu[:, k, m * 128:(m + 1) * 128], ygb[:, k, 0:n], start=(k == 0), stop=(k == 3))
                    for m in range(4):
                        act(sg_t[:, m, 0:n], pg2[:, m * n:(m + 1) * n], AF.Sigmoid, bias=bglu[:, m:m + 1])
                    tt('dve', omix[:, :, 0:n], ygb[:, :, 0:n], sg_t[:, :, 0:n], ALU.mult)
                    out_proj(Wout, omix, n, xch(c))
                    norm_chunk(c, gmlp[:, 0, :])

                for c in range(16):
                    c0, n = CH[c]
                    pre(c)
                    uT = uTs[c % 2]
                    hre_g, him_g = hre_gs[c % 2], him_gs[c % 2]

                    def mmgroup(tg):
                        pbr = psn()
                        pbi = psn()
                        for q in range(4):
                            t = tg * 4 + q
                            mm(pbr[:, q * n:(q + 1) * n], Bpad_re[:, t, :], uT[:, tg, 0:n])
                            mm(pbi[:, q * n:(q + 1) * n], Bpad_im[:, t, :], uT[:, tg, 0:n])
                        return (pbr[:, 0:4 * n].re("p (q n) -> p q n", q=4), pbi[:, 0:4 * n].re("p (q n) -> p q n", q=4))

                    def rot_in(e, tg, srcr, srci, T_):
                        Cg = COS[:, tg * 4:(tg + 1) * 4, 0:n]
                        Sn = SIN[:, tg * 4:(tg + 1) * 4, 0:n]
                        tt(e, T_.gin_re[:], srcr, Cg, ALU.mult)
                        tt(e, T_.s5b[:], srci, Sn, ALU.mult)
                        tt(e, T_.gin_re[:], T_.gin_re[:], T_.s5b[:], ALU.add)
                        tt(e, T_.gin_im[:], srci, Cg, ALU.mult)
                        tt(e, T_.s5b[:], srcr, Sn, ALU.mult)
                        tt(e, T_.gin_im[:], T_.gin_im[:], T_.s5b[:], ALU.subtract)

                    def scans(tg, T_):
                        for q in range(4):
                            t = tg * 4 + q
                            scan(T_.g_re[:, q, :], Rm[:, t:t + 1].bc([128, n]), T_.gin_re[:, q, :], ginit_re[:, t:t + 1])
                            scan(T_.g_im[:, q, :], Rm[:, t:t + 1].bc([128, n]), T_.gin_im[:, q, :], ginit_im[:, t:t + 1])

                    def rot_out(e, tg, T_):
                        Cg = COS[:, tg * 4:(tg + 1) * 4, 0:n]
                        Sn = SIN[:, tg * 4:(tg + 1) * 4, 0:n]
                        tt(e, T_.gin_re[:], T_.g_re[:], Cg, ALU.mult)
                        tt(e, T_.s5b[:], T_.g_im[:], Sn, ALU.mult)
                        tt(e, hre_g[tg][:], T_.gin_re[:], T_.s5b[:], ALU.subtract)
                        tt(e, T_.gin_im[:], T_.g_im[:], Cg, ALU.mult)
                        tt(e, T_.s5b[:], T_.g_re[:], Sn, ALU.mult)
                        tt(e, him_g[tg][:], T_.gin_im[:], T_.s5b[:], ALU.add)
                        cp('dve', glast_re[:, tg * 4:(tg + 1) * 4], T_.g_re[:, :, n - 1])
                        cp('dve', glast_im[:, tg * 4:(tg + 1) * 4], T_.g_im[:, :, n - 1])

                    r3, i3 = mmgroup(3)
                    cp('act', TP.g_re[:], r3)
                    cp('act', TP.g_im[:], i3)
                    rot_in('pool', 3, TP.g_re[:], TP.g_im[:], TP)
                    r0, i0 = mmgroup(0)
                    rot_in('dve', 0, r0, i0, TD)
                    scans(0, TD)
                    rot_out('dve', 0, TD)
                    scans(3, TP)
                    rot_out('pool', 3, TP)
                    for tg in (1, 2):
                        r_, i_ = mmgroup(tg)
                        rot_in('dve', tg, r_, i_, TD)
                        scans(tg, TD)
                        rot_out('dve', tg, TD)
                    rot(n, ginit_re, ginit_im)
                    if c == 15:
                        rot(n - 1, Hre, Him)
                        for (src, oname) in ((Hre, 'p_s5re'), (Him, 'p_s5im')):
                            pz = psn()
                            tr(pz[0:16, 0:128], src[:], ident[:])
                            cp('dve', hout_tok[0:16, 0:128], pz[0:16, 0:128])
                            dma('sp', O[oname], hout_tok[0:16, 0:128], d_out)
                    if c > 0:
                        post(c - 1)
                post(15)
                S.barrier()
                pst.close()

                with ExitStack() as sst:
                    h0re_tok = sb(sst, [NS, 2048], F32, 'h0re')
                    h0im_tok = sb(sst, [NS, 2048], F32, 'h0im')
                    hout2 = [sb(sst, [NS, 512], F32, 'hout2') for _ in range(2)]
                    hs_re = sb(sst, [128, 16, NS], F32, 'hsre')
                    hs_im = sb(sst, [128, 16, NS], F32, 'hsim')
                    s5m1 = sb(sst, [128, 16, NS], F32, 's5m1')
                    s5m2 = sb(sst, [128, 16, NS], F32, 's5m2')
                    dma('sp', h0re_tok[:], I['ss5re'], d_st[5])
                    dma('sp', h0im_tok[:], I['ss5im'], d_st[5])
                    pre(16)
                    uT = uTs[0]
                    hre_g, him_g = hre_gs[0], him_gs[0]
                    pzr = psn()
                    pzi = psn()
                    for t in range(16):
                        tr(pzr[:, t * NS:(t + 1) * NS], h0re_tok[:, t * 128:(t + 1) * 128], ident[0:NS, 0:NS])
                        tr(pzi[:, t * NS:(t + 1) * NS], h0im_tok[:, t * 128:(t + 1) * 128], ident[0:NS, 0:NS])
                    pbr = psn()
                    pbi = psn()
                    for t in range(16):
                        mm(pbr[:, t * NS:(t + 1) * NS], Bpad_re[:, t, :], uT[:, t // 4, 0:NS])
                        mm(pbi[:, t * NS:(t + 1) * NS], Bpad_im[:, t, :], uT[:, t // 4, 0:NS])
                    v3 = lambda p_: p_[:, 0:16 * NS].re("p (t s) -> p t s", t=16)
                    arb = AR[:].un(2).bc([128, 16, NS])
                    aib = AI_[:].un(2).bc([128, 16, NS])
                    tt('dve', s5m1[:], v3(pzr), arb, ALU.mult)
                    tt('dve', s5m2[:], v3(pzi), aib, ALU.mult)
                    tt('pool', s5m1[:], s5m1[:], s5m2[:], ALU.subtract)
                    tt('dve', hs_re[:], v3(pbr), s5m1[:], ALU.add)
                    tt('dve', s5m1[:], v3(pzi), arb, ALU.mult)
                    tt('dve', s5m2[:], v3(pzr), aib, ALU.mult)
                    tt('pool', s5m1[:], s5m1[:], s5m2[:], ALU.add)
                    tt('dve', hs_im[:], v3(pbi), s5m1[:], ALU.add)
                    for tg in range(4):
                        cp('act', hre_g[tg][:, :, 0:NS], hs_re[:, tg * 4:(tg + 1) * 4, :])
                        cp('act', him_g[tg][:, :, 0:NS], hs_im[:, tg * 4:(tg + 1) * 4, :])
                    for (src, oname) in ((hs_re, 's_s5re'), (hs_im, 's_s5im')):
                        for tg in range(4):
                            pz = psn()
                            for q in range(4):
                                t = tg * 4 + q
                                tr(pz[0:NS, q * 128:(q + 1) * 128], src[:, t, :], ident[:])
                            hb = hout2[tg % 2]
                            cp('dve', hb[:], pz[0:NS, :])
                            dma('sp', O[oname][:, tg * 512:(tg + 1) * 512], hb[:], d_out)
                    post(16)
                    S.barrier()
                S.barrier()


        def conv_diag(st, name, wsrc, ntile):
            cw = sb(st, [128, 4, ntile], F32, name + 'cw')
            dma('sp', cw[:], wsrc.rearrange("k (t p) -> p k t", p=128), d_init)
            DW = sb(st, [128, ntile, 4, 128], BF16, name)
            for t in range(ntile):
                for k in range(4):
                    ts('dve', DW[:, t, k, :], ident[:], cw[:, k, t:t + 1], ALU.mult)
            return DW

        def softplus_tok(out, pin, brow, n, w, tmp):
            tt('dve', tmp[0:n, 0:w], pin, brow[0:n, 0:w], ALU.add)
            act(tmp[0:n, 0:w], tmp[0:n, 0:w], AF.Exp)
            act(out, tmp[0:n, 0:w], AF.Ln, bias=1.0)

        def cum_stuff(la_tok, n, H, cum_tok, negcum, wl_tok, explast):
            pc = psn()
            mm(pc[0:n, 0:H], triu[0:n, 0:n], la_tok[0:n, 0:H])
            mm(pc[:, 16:16 + H], ones_f[0:n, :], la_tok[0:n, 0:H])
            cp('dve', cum_tok[0:n, 0:H], pc[0:n, 0:H])
            ts('dve', negcum[0:n, 0:H], pc[0:n, 0:H], -1.0, ALU.mult)
            tt('dve', wl_tok[0:n, 0:H], pc[0:n, 16:16 + H], negcum[0:n, 0:H], ALU.add)
            act(wl_tok[0:n, 0:H], wl_tok[0:n, 0:H], AF.Exp)
            act(explast[:, 0:H], pc[:, 16:16 + H], AF.Exp)

        def pass_ssd(pre=None):
            with ExitStack() as ph:
                Win = pre if pre is not None else load_wc(ph, 'win_ssd', I['w_in_cd'], 0, 1544, d_w[4])
                Wout = load_wr(ph, 'wout_ssd', I['w_out_cd'], 0, d_w[5])
                DW = conv_diag(ph, 'dwssd', I['ssd_conv_w'], 8)
                cb = sb(ph, [128, 8], F32, 'cb')
                dma('sp', cb[:], I['ssd_conv_b'].rearrange("(t p) -> p t", p=128), d_init)
                dtb = sb(ph, [128, 8], F32, 'dtb')
                dma('sp', dtb[:], I['ssd_dt_bias'].partition_broadcast(128), d_init)
                arow = sb(ph, [128, 8], F32, 'arow')
                dma('sp', arow[:], I['ssd_a_log'].partition_broadcast(128), d_init)
                act(arow[:], arow[:], AF.Exp)
                ts('dve', arow[:], arow[:], -1.0, ALU.mult)
                Dexp = sb(ph, [128, 4], F32, 'Dexp')
                dv = I['ssd_d'].rearrange("(t g) -> g t", g=2)
                for g2 in range(2):
                    dma('sp', Dexp[g2 * 64:(g2 + 1) * 64, :], dv[g2].partition_broadcast(64), d_init)
                gssd = sb(ph, [128, 4], F32, 'gssd')
                dma('sp', gssd[:], I['ssd_norm'].rearrange("(t p) -> p t", p=128), d_init)
                XB = sb(ph, [128, 8, 131], BF16, 'XB')
                memset('pool', XB[:], 0.0)
                XCs = [sb(ph, [128, 8, 128], BF16, 'XC') for _ in range(2)]
                zss = [sb(ph, [128, 4, 128], BF16, 'zs') for _ in range(2)]
                dt_tok = sb(ph, [128, 8], F32, 'dt_tok')
                la_tok = sb(ph, [128, 8], F32, 'la_tok')
                tmp8 = sb(ph, [128, 8], F32, 'tmp8')
                y2 = sb(ph, [128, 4, 128], F32, 'y2')
                yz = sb(ph, [128, 4, 128], F32, 'yz')
                sqz = sb(ph, [128, 4, 128], BF16, 'sqz')
                omix = sb(ph, [128, 4, 128], BF16, 'omix')
                ctok = sb(ph, [NS, 1024], F32, 'ctok')

                def pre(c):
                    c0, n = CH[c]
                    hTn = hcols(c0, n)
                    pz_ = psn()
                    for t in range(4):
                        proj(Win, t * 128, 128, hTn, pz_[:, t * n:(t + 1) * n])
                    act(zss[c % 2][:, :, 0:n], pz_[:, 0:4 * n].re("p (t n) -> p t n", t=4), AF.Silu)
                    pxs = [psn(), psn()]
                    for t in range(8):
                        proj(Win, 512 + t * 128, 128, hTn, pxs[t // 4][:, (t % 4) * n:(t % 4 + 1) * n])
                    pdt = psn()
                    projA(hTn, Win, 1536, 8, pdt[0:n, 0:8])
                    softplus_tok(dt_tok[0:n, :], pdt[0:n, 0:8], dtb, n, 8, tmp8)
                    tt('dve', la_tok[0:n, :], dt_tok[0:n, :], arow[0:n, :], ALU.mult)
                    return pxs

                def conv(n, rhs_of, p=0):
                    XC = XCs[p]
                    for half in range(2):
                        pc = psn()
                        for j in range(4):
                            t = half * 4 + j
                            for k in range(4):
                                mm(pc[:, j * n:(j + 1) * n], DW[:, t, k, :], rhs_of(t, k), start=(k == 0), stop=(k == 3))
                        for j in range(4):
                            t = half * 4 + j
                            act(XC[:, t, 0:n], pc[:, j * n:(j + 1) * n], AF.Silu, bias=cb[:, t:t + 1])

                def conv_state_out(c0, M, oap):
                    for j in range(2):
                        pcs = psn()
                        projA(hcols(c0, M), Win, 512 + j * 512, 512, pcs[0:M, :])
                        cp('act', ctok[0:M, j * 512:(j + 1) * 512], pcs[0:M, :])
                    dma('sp', oap, ctok[0:M, :], d_out)

                def post(c, yT):
                    c0, n = CH[c]
                    tt('dve', yz[:, :, 0:n], yT, zss[c % 2][:, :, 0:n], ALU.mult)
                    tt('dve', sqz[:, :, 0:n], yz[:, :, 0:n], yz[:, :, 0:n], ALU.mult)
                    pn = psn()
                    for g in range(2):
                        mm(pn[:, g * n:(g + 1) * n], ones_bf[:], sqz[:, 2 * g, 0:n], start=True, stop=False)
                        mm(pn[:, g * n:(g + 1) * n], ones_bf[:], sqz[:, 2 * g + 1, 0:n], start=False, stop=True)
                    act(v_s[:, 0:2 * n], pn[:, 0:2 * n], AF.Ln, scale=1.0 / 256, bias=EPS)
                    act(rstd_s[:, 0:2 * n], v_s[:, 0:2 * n], AF.Exp, scale=-0.5)
                    for g in range(2):
                        tt('dve', yz[:, 2 * g:2 * g + 2, 0:n], yz[:, 2 * g:2 * g + 2, 0:n], rstd_s[:, g * n:(g + 1) * n].un(1).bc([128, 2, n]), ALU.mult)
                    tt('dve', omix[:, :, 0:n], yz[:, :, 0:n], gssd[:].un(2).bc([128, 4, n]), ALU.mult)
                    out_proj(Wout, omix, n, xch(c))

                pst = ExitStack()
                lab = sb(pst, [128, 8, 64], F32, 'lab')
                labn = sb(pst, [128, 8, 128], F32, 'labn')
                csT = sb(pst, [128, 4, 128], F32, 'csT')
                ones128 = sb(pst, [128, 128], F32, 'ones128')
                memset('pool', ones128[:], 1.0)
                cum_tok = sb(pst, [128, 8], F32, 'cum_tok')
                negcum = sb(pst, [128, 8], F32, 'negcum')
                wl_tok = sb(pst, [128, 8], F32, 'wl_tok')
                dw_tok = sb(pst, [128, 8], F32, 'dw_tok')
                xbtoks = [sb(pst, [128, 768], BF16, 'xbtok') for _ in range(2)]
                xdtZs = [sb(pst, [128, 8, 128], BF16, 'xdtZ') for _ in range(2)]
                for z_ in xdtZs:
                    memset('pool', z_[:], 0.0)
                xws = [sb(pst, [128, 512], BF16, 'xw') for _ in range(2)]
                decT = sb(pst, [128, 8, 128], BF16, 'decT')
                MTs = [sb(pst, [128, 8, 128], BF16, 'MT') for _ in range(2)]
                Ecums = [sb(pst, [128, 4, 128], F32, 'Ecum2') for _ in range(2)]
                explasts = [sb(pst, [128, 8], F32, 'explast2') for _ in range(2)]
                y1 = sb(pst, [128, 4, 128], F32, 'y1')
                ST = sb(pst, [128, 512], F32, 'ST')
                STbf = sb(pst, [128, 512], BF16, 'STbf')
                tmpST = sb(pst, [128, 512], F32, 'tmpST')
                memset('pool', ST[:], 0.0)
                memset('pool', STbf[:], 0.0)

                def ssd_front(c):
                    c0, n = CH[c]
                    p = c % 2
                    XC, xbtok, xdtZ, xw, MT, Ecum_, explast_ = XCs[p], xbtoks[p], xdtZs[p], xws[p], MTs[p], Ecums[p], explasts[p]
                    pxs = pre(c)
                    for half in range(2):
                        cp('act' if half == 0 else 'dve', XB[:, half * 4:(half + 1) * 4, 3:3 + n], pxs[half][:, 0:4 * n].re("p (t n) -> p t n", t=4))
                    conv(n, lambda t, k: XB[:, t, k:k + n], p)
                    cp('dve', XB[:, :, 0:3], XB[:, :, n:n + 3])
                    if c == 15:
                        conv_state_out(T - 3, 3, O['p_ssdc'])
                    cp('dve', lab[0:n], la_tok[0:n, :].un(2).bc([n, 8, 64]))
                    pexp = psn()
                    for t in range(4):
                        mm(pexp[:, t * n:(t + 1) * n], lab[0:n, 2 * t:2 * t + 2, :].re("p a b -> p (a b)"), ident[0:n, 0:n])
                    for t in range(4):
                        scan(csT[:, t, 0:n], ones128[:, 0:n], pexp[:, t * n:(t + 1) * n], 0.0)
                    act(Ecum_[:, :, 0:n], csT[:, :, 0:n], AF.Exp)
                    cum_stuff(la_tok, n, 8, cum_tok, negcum, wl_tok, explast_)
                    tt('dve', dw_tok[0:n, :], dt_tok[0:n, :], wl_tok[0:n, :], ALU.mult)
                    pt = psn()
                    ptb = pt[:].bitcast(BF16)
                    for t in range(6):
                        tr(ptb[0:n, t * 128:(t + 1) * 128], XC[:, t, 0:n], identb[:])
                    cp('dve', xbtok[0:n, :], ptb[0:n, 0:768])
                    xsv = xbtok[0:n, 0:512].re("p (t a c) -> p t a c", t=4, a=2)
                    for h2 in range(2):
                        tt('dve', xdtZ[0:n].re("p (t a) c -> p t a c", a=2)[:, :, h2, h2 * 64:(h2 + 1) * 64], xsv[:, :, h2, :],
                           dt_tok[0:n, :].re("p (t a) -> p t a", a=2)[:, :, h2].un(2).bc([n, 4, 64]), ALU.mult)
                    tt('dve', xw[0:n, :].re("p (h c) -> p h c", h=8), xbtok[0:n, 0:512].re("p (h c) -> p h c", h=8),
                       dw_tok[0:n, :].un(2).bc([n, 8, 64]), ALU.mult)
                    pcb = psn()
                    for g in range(2):
                        mm(pcb[0:n, g * n:(g + 1) * n], XC[:, 4 + g, 0:n], XC[:, 6 + g, 0:n])
                    cp('dve', labn[0:n, :, 0:n], la_tok[0:n, :].un(2).bc([n, 8, n]))
                    pdec = [psn(), psn()]
                    for h in range(8):
                        o_ = pdec[h // 4][0:n, (h % 4) * n:(h % 4 + 1) * n]
                        mm(o_, labn[0:n, h, 0:n], triu[0:n, 0:n], start=True, stop=False)
                        mm(o_, ident[0:n, 0:n], negm_f[0:n, 0:n], start=False, stop=True)
                    for h in range(8):
                        act(decT[0:n, h, 0:n], pdec[h // 4][0:n, (h % 4) * n:(h % 4 + 1) * n], AF.Exp, bias=negcum[0:n, h:h + 1])
                    for g in range(2):
                        tt('dve', MT[0:n, 4 * g:4 * g + 4, 0:n], pcb[0:n, g * n:(g + 1) * n].un(1).bc([n, 4, n]), decT[0:n, 4 * g:4 * g + 4, 0:n], ALU.mult)

                def ssd_back(c):
                    c0, n = CH[c]
                    p = c % 2
                    XC, xbtok, xdtZ, xw, MT, Ecum_, explast_ = XCs[p], xbtoks[p], xdtZs[p], xws[p], MTs[p], Ecums[p], explasts[p]
                    py = psn()
                    for t in range(4):
                        for h2 in range(2):
                            h = 2 * t + h2
                            mm(py[:, t * n:(t + 1) * n], xdtZ[0:n, h, :], MT[0:n, h, 0:n], start=(h2 == 0), stop=(h2 == 1))
                    pi_ = psn()
                    for t in range(4):
                        mm(pi_[:, t * n:(t + 1) * n], STbf[:, t * 128:(t + 1) * 128], XC[:, 6 + t // 2, 0:n])
                    tt('dve', y1[:, :, 0:n], pi_[:, 0:4 * n].re("p (t n) -> p t n", t=4), Ecum_[:, :, 0:n], ALU.mult)
                    tt('dve', y2[:, :, 0:n], py[:, 0:4 * n].re("p (t n) -> p t n", t=4), y1[:, :, 0:n], ALU.add)
                    tt('dve', y1[:, :, 0:n], XC[:, 0:4, 0:n], Dexp[:].un(2).bc([128, 4, n]), ALU.mult)
                    tt('dve', y2[:, :, 0:n], y2[:, :, 0:n], y1[:, :, 0:n], ALU.add)
                    pS = psn()
                    for g in range(2):
                        mm(pS[:, g * 256:(g + 1) * 256], xbtok[0:n, 512 + g * 128:512 + (g + 1) * 128], xw[0:n, g * 256:(g + 1) * 256])
                    tt('dve', tmpST[:].re("p (h c) -> p h c", h=8), ST[:].re("p (h c) -> p h c", h=8), explast_[:].un(2).bc([128, 8, 64]), ALU.mult)
                    tt('dve', ST[:], pS[:, :], tmpST[:], ALU.add)
                    cp('act', STbf[:], ST[:])
                    post(c, y2[:, :, 0:n])

                ssd_front(0)
                for c in range(16):
                    if c + 1 < 16:
                        ssd_front(c + 1)
                    ssd_back(c)
                for half in range(1):
                    pz = psn()
                    for t in range(4):
                        tr(pz[:, t * 128:(t + 1) * 128], ST[:, t * 128:(t + 1) * 128], ident[:])
                    cp('dve', tmpST[:], pz[:, :])
                    dma('sp', O['p_ssd'].rearrange("h p n -> (h p) n").rearrange("(t q) n -> q t n", q=128), tmpST[:].re("p (t n) -> p t n", t=4), d_out)
                S.barrier()
                pst.close()

                with ExitStack() as sst:
                    c = 16
                    c0, n = CH[c]
                    HX = sb(sst, [128, 8, 4, NS], BF16, 'HX')
                    scv = sb(sst, [3 * NS, 1024], F32, 'scv')
                    dma('sp', scv[:], I['sssdc'].rearrange("s k f -> (s k) f"), d_st[6])
                    dma('sp', O['s_ssdc'][:, 0:2, :], I['sssdc'][:, 1:3, :], d_out)
                    pxs = pre(c)
                    for half in range(2):
                        cp('act' if half == 0 else 'dve', HX[:, half * 4:(half + 1) * 4, 3, :], pxs[half][:, 0:4 * n].re("p (t n) -> p t n", t=4))
                    for half in range(2):
                        ph_ = psn()
                        for j in range(4):
                            t = half * 4 + j
                            tr(ph_[:, j * 48:(j + 1) * 48], scv[:, t * 128:(t + 1) * 128], ident[0:48, 0:48])
                        cp('dve', HX[:, half * 4:(half + 1) * 4, 0:3, :], ph_[:, 0:4 * 48].re("p (t s k) -> p t k s", t=4, k=3))
                    conv(n, lambda t, k: HX[:, t, k, :], 0)
                    XC = XCs[0]
                    conv_state_out(T, NS, O['s_ssdc'][:, 2, :])
                    da_tok = sb(sst, [NS, 8], F32, 'da_tok')
                    act(da_tok[:], la_tok[0:NS, :], AF.Exp)
                    dab = sb(sst, [NS, 8, 64], F32, 'dab')
                    cp('dve', dab[:], da_tok[:].un(2).bc([NS, 8, 64]))
                    pe_ = psn()
                    for t in range(4):
                        mm(pe_[:, t * NS:(t + 1) * NS], dab[:, 2 * t:2 * t + 2, :].re("p a b -> p (a b)"), ident[0:NS, 0:NS])
                    daT = sb(sst, [128, 4, NS], F32, 'daT')
                    cp('dve', daT[:], pe_[:, 0:4 * NS].re("p (t s) -> p t s", t=4))
                    pt = psn()
                    ptb = pt[:].bitcast(BF16)
                    for t in range(8):
                        tr(ptb[0:NS, t * 128:(t + 1) * 128], XC[:, t, 0:NS], identb[:])
                    xbc_s = sb(sst, [NS, 1024], BF16, 'xbc_s')
                    cp('dve', xbc_s[:], ptb[0:NS, 0:1024])
                    xdt_s = sb(sst, [NS, 512], BF16, 'xdt_s')
                    tt('pool', xdt_s[:].re("p (h c) -> p h c", h=8), xbc_s[:, 0:512].re("p (h c) -> p h c", h=8), dt_tok[0:NS, :].un(2).bc([NS, 8, 64]), ALU.mult)
                    OH = sb(sst, [NS, NS, 128], BF16, 'OH')
                    cp('dve', OH[:], identb[0:NS, 0:NS].un(2).bc([NS, NS, 128]))
                    XdZ = sb(sst, [NS, 4, 512], BF16, 'XdZ')
                    Ssl = [sb(sst, [128, 4, 4, 128], F32, 'Ssl') for _ in range(2)]
                    prod = sb(sst, [128, 4, 128], F32, 'prod')
                    ysT = sb(sst, [128, 4, NS], F32, 'ysT')
                    for gi in range(4):
                        b = gi % 2
                        s0 = gi * 4
                        for t in range(4):
                            dma('sp', Ssl[b][:, t], I['sssd'][s0:s0 + 4, 2 * t:2 * t + 2].rearrange("s a p n -> (a p) s n"), d_st[b])
                        tt('pool', XdZ[:], xdt_s[:].un(1).bc([NS, 4, 512]), identb[0:NS, s0:s0 + 4].un(2).bc([NS, 4, 512]), ALU.mult)
                        pcs = [psn(), psn()]
                        for g in range(2):
                            for si in range(4):
                                mm(pcs[g][:, si * 128:(si + 1) * 128], OH[:, s0 + si, :], xbc_s[:, 768 + g * 128:768 + (g + 1) * 128])
                        for t in range(4):
                            pso = psn()
                            for si in range(4):
                                mm(pso[:, si * 128:(si + 1) * 128], XdZ[:, si, t * 128:(t + 1) * 128], xbc_s[:, 512 + (t // 2) * 128:512 + (t // 2 + 1) * 128])
                            tt('pool', Ssl[b][:, t], Ssl[b][:, t], daT[:, t, s0:s0 + 4].un(2).bc([128, 4, 128]), ALU.mult)
                            tt('dve', Ssl[b][:, t], pso[:, :].re("p (s n) -> p s n", s=4), Ssl[b][:, t], ALU.add)
                            tt('dve', prod[:], pcs[t // 2][:, :].re("p (s n) -> p s n", s=4), Ssl[b][:, t], ALU.mult)
                            red(ysT[:, t, s0:s0 + 4], prod[:])
                        for t in range(4):
                            dma('sp', O['s_ssd'][s0:s0 + 4, 2 * t:2 * t + 2].rearrange("s a p n -> (a p) s n"), Ssl[b][:, t], d_st[2 + b])
                    y1s = sb(sst, [128, 4, NS], F32, 'y1s')
                    tt('pool', y1s[:], XC[:, 0:4, 0:NS], Dexp[:].un(2).bc([128, 4, NS]), ALU.mult)
                    tt('dve', y2[:, :, 0:NS], ysT[:], y1s[:], ALU.add)
                    post(c, y2[:, :, 0:NS])
                    S.barrier()
                S.barrier()

        def pass_gdn():
            with ExitStack() as ph:
                Wt_ = sb(ph, [128, 8, 2056], BF16, 'win_gdn')
                wblocks = [(1536, 2056), (0, 512), (512, 1024), (1024, 1536)]
                Win = WB(Wt_.t, wblocks)
                svw = I['w_in_cd'].rearrange("(k p) n -> p k n", p=128)
                for bi, (lo, hi) in enumerate(wblocks):
                    S.dma('pool', Wt_.t[:, :, lo:hi], svw[:, :, 1544 + lo:1544 + hi], d_w[bi], writes=[Win.units[bi]])
                Wout = load_wr(ph, 'wout_gdn', I['w_out_cd'], 512, d_w[5])
                DW = conv_diag(ph, 'dwgdn', I['gdn_conv_w'], 12)
                arow = sb(ph, [128, 4], F32, 'arowg')
                dma('sp', arow[:], I['gdn_a_log'].partition_broadcast(128), d_init)
                act(arow[:], arow[:], AF.Exp)
                ts('dve', arow[:], arow[:], -1.0, ALU.mult)
                dtb = sb(ph, [128, 4], F32, 'dtbg')
                dma('sp', dtb[:], I['gdn_dt_bias'].partition_broadcast(128), d_init)
                ggdn = sb(ph, [128, 1], F32, 'ggdn')
                dma('sp', ggdn[:], I['gdn_norm'].rearrange("(p o) -> p o", o=1), d_init)
                XQ = sb(ph, [128, 12, 131], BF16, 'XQ')
                memset('pool', XQ[:], 0.0)
                QC = sb(ph, [128, 12, 128], BF16, 'QC')
                gss = [sb(ph, [128, 4, 128], BF16, 'gs') for _ in range(2)]
                qkv_tok = sb(ph, [128, 12, 128], BF16, 'qkv_tok')
                sqk = sb(ph, [128, 8, 128], BF16, 'sqk')
                ssq = sb(ph, [128, 8], F32, 'ssq')
                rs = sb(ph, [128, 8], F32, 'rs')
                qn_tok = sb(ph, [128, 4, 128], BF16, 'qn_tok')
                kn_tok = sb(ph, [128, 4, 128], BF16, 'kn_tok')
                beta_tok = sb(ph, [128, 4], F32, 'beta_tok')
                g_tok = sb(ph, [128, 4], F32, 'g_tok')
                tmp4 = sb(ph, [128, 4], F32, 'tmp4')
                knT = sb(ph, [128, 4, 128], BF16, 'knT')
                qnT = sb(ph, [128, 4, 128], BF16, 'qnT')
                osb = sb(ph, [128, 512], F32, 'osb')
                sqo = sb(ph, [128, 512], BF16, 'sqo')
                t1 = osb
                omix = sb(ph, [128, 4, 128], BF16, 'omix')
                ctok = [sb(ph, [NS, 512], F32, 'ctokg') for _ in range(2)]

                def pre(c):
                    c0, n = CH[c]
                    hTn = hcols(c0, n)
                    pg_ = psn()
                    for t in range(4):
                        proj(Win, 1536 + t * 128, 128, hTn, pg_[:, t * n:(t + 1) * n])
                    act(gss[c % 2][:, :, 0:n], pg_[:, 0:4 * n].re("p (t n) -> p t n", t=4), AF.Silu)
                    pxs = [psn(), psn(), psn()]
                    for t in range(12):
                        proj(Win, t * 128, 128, hTn, pxs[t // 4][:, (t % 4) * n:(t % 4 + 1) * n])
                    pba = psn()
                    projA(hTn, Win, 2048, 8, pba[0:n, 0:8])
                    act(beta_tok[0:n, :], pba[0:n, 0:4], AF.Sigmoid)
                    softplus_tok(g_tok[0:n, :], pba[0:n, 4:8], dtb, n, 4, tmp4)
                    tt('dve', g_tok[0:n, :], g_tok[0:n, :], arow[0:n, :], ALU.mult)
                    return pxs

                def conv(n, rhs_of):
                    for b3 in range(3):
                        pc = psn()
                        for j in range(4):
                            t = b3 * 4 + j
                            for k in range(4):
                                mm(pc[:, j * n:(j + 1) * n], DW[:, t, k, :], rhs_of(t, k), start=(k == 0), stop=(k == 3))
                        act(QC[:, b3 * 4:(b3 + 1) * 4, 0:n], pc[:, 0:4 * n].re("p (t n) -> p t n", t=4), AF.Silu)

                def conv_state_out(c0, M, oap):
                    for j in range(3):
                        pcs = psn()
                        projA(hcols(c0, M), Win, j * 512, 512, pcs[0:M, :])
                        cp('act', ctok[j % 2][0:M, :], pcs[0:M, :])
                        dma('sp', oap[:, j * 512:(j + 1) * 512], ctok[j % 2][0:M, :], d_out)

                def tokprep(n):
                    pts = [psn(), psn()]
                    ptb0 = pts[0][:].bitcast(BF16)
                    ptb1 = pts[1][:].bitcast(BF16)
                    for t in range(8):
                        tr(ptb0[0:n, t * 128:(t + 1) * 128], QC[:, t, 0:n], identb[:])
                    for t in range(4):
                        tr(ptb1[0:n, t * 128:(t + 1) * 128], QC[:, 8 + t, 0:n], identb[:])
                    cp('dve', qkv_tok[0:n, 0:8, :], ptb0[0:n, :].re("p (t f) -> p t f", t=8))
                    cp('act', qkv_tok[0:n, 8:12, :], ptb1[0:n, 0:512].re("p (t f) -> p t f", t=4))
                    if gstop <= 1.2:
                        return
                    for h in range(8):
                        act(sqk[0:n, h, :], qkv_tok[0:n, h, :], AF.Square, accum=ssq[0:n, h:h + 1])
                    if gstop <= 1.3:
                        return
                    act(rs[0:n, :], ssq[0:n, :], AF.Ln, bias=EPS)
                    act(rs[0:n, :], rs[0:n, :], AF.Exp, scale=-0.5)
                    if gstop <= 1.4:
                        return
                    ts('dve', rs[0:n, 0:4], rs[0:n, 0:4], 128.0 ** -0.5, ALU.mult)
                    tt('dve', qn_tok[0:n], qkv_tok[0:n, 0:4, :], rs[0:n, 0:4].un(2).bc([n, 4, 128]), ALU.mult)
                    tt('dve', kn_tok[0:n], qkv_tok[0:n, 4:8, :], rs[0:n, 4:8].un(2).bc([n, 4, 128]), ALU.mult)
                    if gstop <= 1.6:
                        return
                    pt2a = psn()
                    pt2b = psn()
                    for h in range(4):
                        mm(pt2a[:, h * n:(h + 1) * n], kn_tok[0:n, h, :], identb[0:n, 0:n])
                        mm(pt2b[:, h * n:(h + 1) * n], qn_tok[0:n, h, :], identb[0:n, 0:n])
                    cp('dve', knT[:, :, 0:n], pt2a[:, 0:4 * n].re("p (h n) -> p h n", h=4))
                    cp('act', qnT[:, :, 0:n], pt2b[:, 0:4 * n].re("p (h n) -> p h n", h=4))

                pst = ExitStack()
                cum_tok = sb(pst, [128, 4], F32, 'cum_tokg')
                negcum = sb(pst, [128, 4], F32, 'negcumg')
                wl_tok = sb(pst, [128, 4], F32, 'wl_tokg')
                gam_tok = sb(pst, [128, 4], F32, 'gam_tok')
                bg_tok = sb(pst, [128, 4], F32, 'bg_tok')
                explast = sb(pst, [128, 4], F32, 'explastg')
                gbn = sb(pst, [128, 4, 128], F32, 'gbn')
                Xf = Tl2(sb(pst, [128, 4, 256], F32, 'Xf').t)
                qg_tok = sb(pst, [128, 4, 128], BF16, 'qg_tok')
                kw_tok = sb(pst, [128, 4, 128], BF16, 'kw_tok')
                qgT = sb(pst, [128, 4, 128], BF16, 'qgT')
                decA = sb(pst, [128, 4, 128], BF16, 'decA')
                Pm = Tl2(sb(pst, [128, 4, 128], F32, 'Pm').t)
                PTm = Tl2(sb(pst, [128, 4, 128], F32, 'PTm').t)
                attqT = sb(pst, [128, 4, 128], BF16, 'attqT')
                WkT = sb(pst, [128, 4, 128], BF16, 'WkT')
                u_bf = sb(pst, [128, 4, 128], BF16, 'u_bf')
                Sg = sb(pst, [128, 4, 128], F32, 'Sgd')
                Sbf = sb(pst, [128, 4, 128], BF16, 'Sgdbf')
                memset('pool', Sg[:], 0.0)
                memset('pool', Sbf[:], 0.0)
                def front(c):
                    c0, n = CH[c]
                    pxs = pre(c)
                    for b3 in range(3):
                        cp(('act', 'dve', 'act')[b3], XQ[:, b3 * 4:(b3 + 1) * 4, 3:3 + n], pxs[b3][:, 0:4 * n].re("p (t n) -> p t n", t=4))
                    conv(n, lambda t, k: XQ[:, t, k:k + n])
                    cp('dve', XQ[:, :, 0:3], XQ[:, :, n:n + 3])
                    if c == 15:
                        conv_state_out(T - 3, 3, O['p_gdnc'])

                front(0)
                tokprep(128)
                psrot['set'] = list(range(7))
                po = psb[7]

                def post(c):
                    n = 128
                    pnorm_part(po[:, 0:4 * n], n, ggdn[:, 0:1], gss[c % 2][:, :, 0:n], omix[:, :, 0:n], osb, sqo, t1)
                    out_proj(Wout, omix, n, xch(c))
                    norm_chunk(c, gmlp[:, 1, :], 'dve')

                for c in range(16):
                    c0, n = CH[c]
                    if gstop <= 2:
                        break
                    cum_stuff(g_tok, n, 4, cum_tok, negcum, wl_tok, explast)
                    act(gam_tok[0:n, :], cum_tok[0:n, :], AF.Exp)
                    tt('dve', bg_tok[0:n, :], beta_tok[0:n, :], gam_tok[0:n, :], ALU.mult)
                    tt('dve', Xf[0:n, :, 0:128], qkv_tok[0:n, 8:12, :], beta_tok[0:n, :].un(2).bc([n, 4, 128]), ALU.mult)
                    tt('dve', Xf[0:n, :, 128:256], kn_tok[0:n], bg_tok[0:n, :].un(2).bc([n, 4, 128]), ALU.mult)
                    tt('dve', qg_tok[0:n], qn_tok[0:n], gam_tok[0:n, :].un(2).bc([n, 4, 128]), ALU.mult)
                    tt('dve', kw_tok[0:n], kn_tok[0:n], wl_tok[0:n, :].un(2).bc([n, 4, 128]), ALU.mult)
                    pt3 = psn()
                    for h in range(4):
                        mm(pt3[:, h * n:(h + 1) * n], qg_tok[0:n, h, :], identb[0:n, 0:n])
                    cp('dve', qgT[:, :, 0:n], pt3[:, 0:4 * n].re("p (h n) -> p h n", h=4))
                    if c == 0:
                        ddump(qkv_tok[0:n].re("p t f -> p (t f)"), 0, 1536)
                        ddump(qn_tok[0:n].re("p t f -> p (t f)"), 1536, 512)
                        ddump(kn_tok[0:n].re("p t f -> p (t f)"), 2048, 512)
                        ddump(beta_tok[0:n, :], 2560, 4)
                        ddump(g_tok[0:n, :], 2564, 4)
                        ddump(cum_tok[0:n, :], 2568, 4)
                        ddump(Xf[0:n].re("p t f -> p (t f)"), 6100, 1024)
                    if gstop <= 3:
                        break
                    pkk = psn()
                    pqk = psn()
                    for h in range(4):
                        mm(pkk[0:n, h * n:(h + 1) * n], knT[:, h, 0:n], knT[:, h, 0:n])
                        mm(pqk[0:n, h * n:(h + 1) * n], knT[:, h, 0:n], qnT[:, h, 0:n])
                    cp('dve', gbn[0:n, :, 0:n], g_tok[0:n, :].un(2).bc([n, 4, n]))
                    pr1 = psn()
                    pr2 = psn()
                    for h in range(4):
                        mm(pr1[0:n, h * n:(h + 1) * n], gbn[0:n, h, 0:n], triu[0:n, 0:n], start=True, stop=False)
                        mm(pr1[0:n, h * n:(h + 1) * n], ident[0:n, 0:n], posm_f[0:n, 0:n], start=False, stop=True)
                        mm(pr2[0:n, h * n:(h + 1) * n], gbn[0:n, h, 0:n], triu[0:n, 0:n], start=True, stop=False)
                        mm(pr2[0:n, h * n:(h + 1) * n], ident[0:n, 0:n], negm_f[0:n, 0:n], start=False, stop=True)
                    for h in range(4):
                        act(decA[0:n, h, 0:n], pr1[0:n, h * n:(h + 1) * n], AF.Exp, scale=-1.0, bias=cum_tok[0:n, h:h + 1])
                    tt('dve', decA[0:n, :, 0:n], pkk[0:n, 0:4 * n].re("p (h n) -> p h n", h=4), decA[0:n, :, 0:n], ALU.mult)
                    tt('dve', Pm[0:n, :, 0:n], decA[0:n, :, 0:n], beta_tok[0:n, :].un(2).bc([n, 4, n]), ALU.mult)
                    for h in range(4):
                        act(decA[0:n, h, 0:n], pr2[0:n, h * n:(h + 1) * n], AF.Exp, bias=negcum[0:n, h:h + 1])
                    tt('dve', attqT[0:n, :, 0:n], pqk[0:n, 0:4 * n].re("p (h n) -> p h n", h=4), decA[0:n, :, 0:n], ALU.mult)
                    pt4 = psn()
                    for h in range(4):
                        mm(pt4[0:n, h * n:(h + 1) * n], Pm[0:n, h, 0:n], ident[0:n, 0:n])
                    cp('act', PTm[0:n, :, 0:n], pt4[0:n, 0:4 * n].re("p (h n) -> p h n", h=4))
                    if c == 0:
                        ddump(Pm[0:n].re("p t f -> p (t f)"), 2600, 512)
                        ddump(attqT[0:n].re("p t f -> p (t f)"), 4300, 512)
                    if gstop <= 4:
                        break
                    if c > 0:
                        post(c - 1)
                    if c + 1 < 16:
                        front(c + 1)
                    nlev = 7
                    Xh = [Xf.half(hp, 2) for hp in range(2)]
                    Ph = [Pm.half(hp, 2) for hp in range(2)]
                    PTh = [PTm.half(hp, 2) for hp in range(2)]
                    for l in range(nlev):
                        pXs, pPs = [], []
                        for hp in range(2):
                            pX = psn()
                            for j in range(2):
                                mm(pX[0:n, j * 256:(j + 1) * 256], PTh[hp][0:n, j, 0:n], Xh[hp][0:n, j, :])
                            pXs.append(pX)
                            if l + 1 < nlev:
                                pP = psn()
                                for j in range(2):
                                    mm(pP[0:n, j * n:(j + 1) * n], PTh[hp][0:n, j, 0:n], Ph[hp][0:n, j, 0:n])
                                for j in range(2):
                                    mm(pP[0:n, 256 + j * n:256 + (j + 1) * n], Ph[hp][0:n, j, 0:n], PTh[hp][0:n, j, 0:n])
                                pPs.append(pP)
                        for hp in range(2):
                            tt('dve', Xh[hp][0:n], Xh[hp][0:n], pXs[hp][0:n, :].re("p (h f) -> p h f", h=2), ALU.subtract if l == 0 else ALU.add)
                            if l + 1 < nlev:
                                cp('act', Ph[hp][0:n, :, 0:n], pPs[hp][0:n, 0:2 * n].re("p (h n) -> p h n", h=2))
                                cp('act', PTh[hp][0:n, :, 0:n], pPs[hp][0:n, 256:256 + 2 * n].re("p (h n) -> p h n", h=2))
                    if c == 0:
                        ddump(Xf[0:n].re("p t f -> p (t f)"), 3200, 1024)
                    if gstop <= 5:
                        break
                    pt5 = psn()
                    for h in range(4):
                        mm(pt5[:, h * n:(h + 1) * n], Xf[0:n, h, 128:256], ident[0:n, 0:n])
                    cp('dve', WkT[:, :, 0:n], pt5[:, 0:4 * n].re("p (h n) -> p h n", h=4))
                    pws = psn()
                    for h in range(4):
                        mm(pws[0:n, h * 128:(h + 1) * 128], WkT[:, h, 0:n], Sbf[:, h, :])
                    tt('dve', u_bf[0:n], Xf[0:n, :, 0:128], pws[0:n, :].re("p (h v) -> p h v", h=4), ALU.subtract)
                    for h in range(4):
                        mm(po[:, h * n:(h + 1) * n], u_bf[0:n, h, :], attqT[0:n, h, 0:n], start=True, stop=False)
                        mm(po[:, h * n:(h + 1) * n], Sbf[:, h, :], qgT[:, h, 0:n], start=False, stop=True)
                    pS = psn()
                    for h in range(4):
                        mm(pS[:, h * 128:(h + 1) * 128], kw_tok[0:n, h, :], u_bf[0:n, h, :])
                    tt('dve', Sg[:], Sg[:], explast[:].un(2).bc([128, 4, 128]), ALU.mult)
                    tt('dve', Sg[:], pS[:, :].re("p (h v) -> p h v", h=4), Sg[:], ALU.add)
                    cp('act', Sbf[:], Sg[:])
                    if c == 15:
                        dma('sp', O['p_gdn'].rearrange("h k v -> k h v"), Sg[:], d_out)
                    if c + 1 < 16:
                        tokprep(128)
                    if c == 0:
                        ddump(u_bf[0:n].re("p t f -> p (t f)"), 4900, 512)
                        cp('act', osb[:, 0:4 * n], po[:, 0:4 * n])
                        ddump(osb[:, 0:4 * n], 5500, 512)
                post(15)
                psrot['set'] = list(range(8))
                S.barrier()
                pst.close()
                if gdn == 'prompt':
                    return

                with ExitStack() as sst:
                    c = 16
                    c0, n = CH[c]
                    HX = sb(sst, [128, 12, 4, NS], BF16, 'HXg')
                    scv = [sb(sst, [3 * NS, 512], F32, 'scvg') for _ in range(2)]
                    dma('sp', O['s_gdnc'][:, 0:2, :], I['sgdnc'][:, 1:3, :], d_out)
                    pxs = pre(c)
                    for b3 in range(3):
                        cp(('act', 'dve', 'act')[b3], HX[:, b3 * 4:(b3 + 1) * 4, 3, :], pxs[b3][:, 0:4 * n].re("p (t n) -> p t n", t=4))
                    for b3 in range(3):
                        ph_ = psn()
                        dma('sp', scv[b3 % 2][:], I['sgdnc'].rearrange("s k f -> (s k) f")[:, b3 * 512:(b3 + 1) * 512], d_st[6 + b3 % 2])
                        for j in range(4):
                            tr(ph_[:, j * 48:(j + 1) * 48], scv[b3 % 2][:, j * 128:(j + 1) * 128], ident[0:48, 0:48])
                        cp('dve', HX[:, b3 * 4:(b3 + 1) * 4, 0:3, :], ph_[:, 0:4 * 48].re("p (t s k) -> p t k s", t=4, k=3))
                    conv(n, lambda t, k: HX[:, t, k, :])
                    conv_state_out(T, NS, O['s_gdnc'][:, 2, :])
                    tokprep(n)
                    gam_s = sb(sst, [NS, 4], F32, 'gam_s')
                    act(gam_s[:], g_tok[0:NS, :], AF.Exp)
                    Dg = sb(sst, [NS, 2, 4, NS], F32, 'Dg')
                    idf16 = ident[0:NS, 0:NS].un(1).bc([NS, 4, NS])
                    tt('dve', Dg[:, 0], idf16, beta_tok[0:NS, :].un(2).bc([NS, 4, NS]), ALU.mult)
                    tt('dve', Dg[:, 1], idf16, gam_s[:].un(2).bc([NS, 4, NS]), ALU.mult)
                    pbc = psn()
                    mm(pbc[:, 0:128], ones_f[0:NS, :], Dg[:].re("p a h s -> p (a h s)"))
                    bgb = sb(sst, [128, 2, 4, NS], F32, 'bgb')
                    cp('dve', bgb[:], pbc[:, 0:128].re("p (a h s) -> p a h s", a=2, h=4))
                    knTf = sb(sst, [128, 4, NS], F32, 'knTf')
                    qnTf = sb(sst, [128, 4, NS], F32, 'qnTf')
                    cp('dve', knTf[:], knT[:, :, 0:NS])
                    cp('dve', qnTf[:], qnT[:, :, 0:NS])
                    GS = 2
                    KdG = sb(sst, [NS, GS, 512], BF16, 'KdG')
                    Ssl = [sb(sst, [128, GS, 4, 128], F32, 'Sslg') for _ in range(2)]
                    uT_s = sb(sst, [128, 4, NS], F32, 'uT_s')
                    t2_s = sb(sst, [128, 4, NS], F32, 't2_s')
                    u_tok = sb(sst, [NS, 4, 128], BF16, 'u_tok')
                    uTb = sb(sst, [128, 4, NS], BF16, 'uTb')
                    po = psb[7]
                    pks = psb[6]
                    psrot['set'] = list(range(6))
                    for gi in range(NS // GS):
                        b = gi % 2
                        s0 = gi * GS
                        for si in range(GS):
                            dma('sp', Ssl[b][:, si], I['sgdn'][s0 + si].rearrange("h k v -> k h v"), d_st[b])
                        for h in range(4):
                            for si in range(GS):
                                col = h * NS + s0 + si
                                mm(pks[:, col:col + 1], Ssl[b][:, si, h, :], knTf[:, h, s0 + si:s0 + si + 1])
                    tt('dve', t2_s[:], pks[:, 0:4 * NS].re("p (h s) -> p h s", h=4), bgb[:, 1], ALU.mult)
                    tt('dve', t2_s[:], QC[:, 8:12, 0:NS], t2_s[:], ALU.subtract)
                    tt('dve', uT_s[:], t2_s[:], bgb[:, 0], ALU.mult)
                    cp('dve', uTb[:], uT_s[:])
                    ptu = psn()
                    ptub = ptu[:].bitcast(BF16)
                    for h in range(4):
                        tr(ptub[0:NS, h * 128:(h + 1) * 128], uTb[:, h, :], identb[:])
                    cp('dve', u_tok[:], ptub[0:NS, 0:512].re("p (h v) -> p h v", h=4))
                    for gi in range(NS // GS):
                        b = gi % 2
                        s0 = gi * GS
                        for si in range(GS):
                            dma('sp', Ssl[b][:, si], I['sgdn'][s0 + si].rearrange("h k v -> k h v"), d_st[b])
                        tt('pool', KdG[:], kn_tok[0:NS].re("p h k -> p (h k)").un(1).bc([NS, GS, 512]), identb[0:NS, s0:s0 + GS].un(2).bc([NS, GS, 512]), ALU.mult)
                        for si in range(GS):
                            pso = psn()
                            for h in range(4):
                                mm(pso[:, h * 128:(h + 1) * 128], KdG[:, si, h * 128:(h + 1) * 128], u_tok[:, h, :])
                            tt('pool', Ssl[b][:, si], Ssl[b][:, si], bgb[:, 1, :, s0 + si:s0 + si + 1].bc([128, 4, 128]), ALU.mult)
                            tt('dve', Ssl[b][:, si], pso[:, :].re("p (h v) -> p h v", h=4), Ssl[b][:, si], ALU.add)
                            for h in range(4):
                                col = h * NS + s0 + si
                                mm(po[:, col:col + 1], Ssl[b][:, si, h, :], qnTf[:, h, s0 + si:s0 + si + 1])
                            dma('sp', O['s_gdn'][s0 + si].rearrange("h k v -> k h v"), Ssl[b][:, si], d_st[2 + b])
                    pnorm_part(po[:, 0:4 * n], n, ggdn[:, 0:1], gss[0][:, :, 0:n], omix[:, :, 0:n], osb, sqo, t1)
                    out_proj(Wout, omix, n, xch(c))
                    norm_chunk(c, gmlp[:, 1, :], 'dve')
                    psrot['set'] = list(range(8))
                    S.barrier()
                S.barrier()

        FT = {}

        def final_alloc(ph):
            FT['yT'] = sb(ph, [128, 8, 128], F32, 'yT')
            FT['ytok'] = [sb(ph, [128, D], F32, 'ytok') for _ in range(2)]
            FT['rs2'] = sb(ph, [128, 128], F32, 'rs2')

        def final_chunk(c):
            yT, ytok, rs2 = FT['yT'], FT['ytok'], FT['rs2']
            c0, n = CH[c]
            xc = xch(c)
            sq = sq_s[:, :, 0:n]
            tt('dve', sq, xc, xc, ALU.mult)
            pb = psn()
            for k in range(8):
                mm(pb[:, 0:n], ones_bf[:], sq_s[:, k, 0:n], start=(k == 0), stop=(k == 7))
            act(v_s[:, 0:n], pb[:, 0:n], AF.Ln, scale=1.0 / D, bias=EPS)
            act(rs2[:, 0:n], v_s[:, 0:n], AF.Exp, scale=-0.5)
            tt('dve', ntmp[:, :, 0:n], xc, rs2[:, 0:n].un(1).bc([128, 8, n]), ALU.mult)
            tt('dve', yT[:, :, 0:n], ntmp[:, :, 0:n], gfin[:].un(2).bc([128, 8, n]), ALU.mult)
            b = c % 2
            for half in range(2):
                pz = psn()
                for j in range(4):
                    k = half * 4 + j
                    tr(pz[0:n, j * 128:(j + 1) * 128], yT[:, k, 0:n], ident[:])
                cp('act' if half == 0 else 'dve', ytok[b][0:n, half * 512:(half + 1) * 512], pz[0:n, :])
            if c < 16:
                dma('sp', O['yp'][c0:c0 + n, :], ytok[b][0:n, :], d_out)
            else:
                dma('sp', O['ys'], ytok[b][0:n, :], d_out)

        dpre = [S.dsem('pre%d' % i) for i in range(3)]
        with ExitStack() as ws5:
            Win5 = load_wc(ws5, 'win_s5', I['w_in_ab'], 1552, 512, dpre[1])
            Wout5 = load_wr(ws5, 'wout_s5', I['w_out_ab'], 512, dpre[1])
            Wglu5 = sb(ws5, [128, 4, 512], BF16, 'wglu')
            with ExitStack() as wgla:
                Wing = load_wc(wgla, 'win_gla', I['w_in_ab'], 0, 1552, dpre[0])
                Woutg = load_wr(wgla, 'wout_gla', I['w_out_ab'], 0, dpre[0])
                Wgateg = sb(wgla, [16, 256], BF16, 'wgate')
                dma('pool', Wgateg[:], I['w_gla_gate'], dpre[0])
                dma('pool', Wglu5[:], I['w_s5_glu'].rearrange("(k p) n -> p k n", p=128), dpre[1])
                phase0()
                pass_gla((Wing, Woutg, Wgateg))
            pass_s5((Win5, Wout5, Wglu5))
        dump(0)
        if nlayers > 1:
            with ExitStack() as wssd:
                Winssd = load_wc(wssd, 'win_ssd', I['w_in_cd'], 0, 1544, dpre[2])
                mlp_phase(0, do_norm=False, cg_hook=lambda c: norm_chunk(c, gmix[:, 1, :], 'dve'))
                dump(1)
                pass_ssd(Winssd)
            dump(2)
            pass_gdn()
            dump(3)
            mlp_phase(1, do_norm=False, cg_hook=final_chunk, st_hook=final_alloc)
            dump(4)
        else:
            mlp_phase(0, do_norm=False, cg_hook=final_chunk, st_hook=final_alloc)
            dump(1)
        S.barrier()
        print("program: ops=%d waits=%d counts=%s" % (S.nops, S.nwaits, S.cnt))
    return nc


_NC_CACHE = {}


def _prep_inputs(inputs):
    f = lambda a: np.ascontiguousarray(np.asarray(a, dtype=np.float32))
    shared = {
        'norm_mix': f(inputs['norm_mix']), 'norm_mlp': f(inputs['norm_mlp']), 'norm_final': f(inputs['norm_final']),
        'w_up': f(inputs['w_up']), 'w_down': f(inputs['w_down']),
        'w_in_ab': f(inputs['w_in_ab'][0]), 'w_out_ab': f(inputs['w_out_ab'][0]), 'w_gla_gate': f(inputs['w_gla_gate'][0]),
        'b_gla_gate': f(inputs['b_gla_gate'][0]), 'g_gla_norm': f(inputs['g_gla_norm'][0]),
        's5_lam_re': f(inputs['s5_lam_re'][0]).reshape(2048), 's5_lam_im': f(inputs['s5_lam_im'][0]).reshape(2048),
        's5_b_re': f(inputs['s5_b_re'][0]).reshape(2048, 16), 's5_b_im': f(inputs['s5_b_im'][0]).reshape(2048, 16),
        's5_c_re': f(inputs['s5_c_re'][0]).reshape(512, 64), 's5_c_im': f(inputs['s5_c_im'][0]).reshape(512, 64),
        's5_d': f(inputs['s5_d'][0]).reshape(512), 's5_log_dt': f(inputs['s5_log_dt'][0]),
        'w_s5_glu': f(inputs['w_s5_glu'][0]), 'b_s5_glu': f(inputs['b_s5_glu'][0]),
        'w_in_cd': f(inputs['w_in_cd'][0]), 'w_out_cd': f(inputs['w_out_cd'][0]),
        'ssd_conv_w': f(inputs['ssd_conv_w'][0]), 'ssd_conv_b': f(inputs['ssd_conv_b'][0]), 'ssd_dt_bias': f(inputs['ssd_dt_bias'][0]),
        'ssd_a_log': f(inputs['ssd_a_log'][0]), 'ssd_d': f(inputs['ssd_d'][0]), 'ssd_norm': f(inputs['ssd_norm'][0]),
        'gdn_conv_w': f(inputs['gdn_conv_w'][0]), 'gdn_a_log': f(inputs['gdn_a_log'][0]), 'gdn_dt_bias': f(inputs['gdn_dt_bias'][0]),
        'gdn_norm': f(inputs['gdn_norm'][0]),
    }
    maps = []
    for c in range(8):
        s = slice(c * NS, (c + 1) * NS)
        m = dict(shared)
        m['xp'] = f(inputs['x_prompt'][c])
        m['xs'] = f(inputs['x_sample'][s, 0])
        m['sgla'] = f(inputs['state_gla'][0, s])
        m['ss5re'] = f(inputs['state_s5_re'][0, s]).reshape(NS, 2048)
        m['ss5im'] = f(inputs['state_s5_im'][0, s]).reshape(NS, 2048)
        m['sssd'] = f(inputs['state_ssd'][0, s])
        m['sssdc'] = f(inputs['state_ssd_conv'][0, s])
        m['sgdn'] = f(inputs['state_gdn'][0, s])
        m['sgdnc'] = f(inputs['state_gdn_conv'][0, s])
        maps.append(m)
    return maps


def _gather(res):
    R = res.results
    cat = lambda k: np.concatenate([np.asarray(r[k]) for r in R], axis=0)
    stk = lambda k: np.stack([np.asarray(r[k]) for r in R], axis=0)
    y_prompt = stk('yp')
    y_sample = cat('ys').reshape(128, 1, D)
    p_gla = stk('p_gla')[None]
    p_s5re = stk('p_s5re').reshape(8, 32, 64)[None]
    p_s5im = stk('p_s5im').reshape(8, 32, 64)[None]
    p_ssd = stk('p_ssd')[None]
    p_ssdc = stk('p_ssdc')[None]
    p_gdn = stk('p_gdn')[None]
    p_gdnc = stk('p_gdnc')[None]
    s_gla = cat('s_gla')[None]
    s_s5re = cat('s_s5re').reshape(128, 32, 64)[None]
    s_s5im = cat('s_s5im').reshape(128, 32, 64)[None]
    s_ssd = cat('s_ssd')[None]
    s_ssdc = cat('s_ssdc')[None]
    s_gdn = cat('s_gdn')[None]
    s_gdnc = cat('s_gdnc')[None]
    outs = (y_prompt, y_sample, p_gla, p_s5re, p_s5im, p_ssd, p_ssdc, p_gdn, p_gdnc,
            s_gla, s_s5re, s_s5im, s_ssd, s_ssdc, s_gdn, s_gdnc)
    return tuple(np.ascontiguousarray(o, dtype=np.float32) for o in outs)


def kernel(**inputs):
    if 'nc' not in _NC_CACHE:
        _NC_CACHE['nc'] = build_program()
    nc = _NC_CACHE['nc']
    maps = _prep_inputs(inputs)
    res = run_bass_kernel_spmd(nc, maps, core_ids=list(range(8)))
    return _gather(res)
```
